# Optimizing a Trainium2 kernel written in Bass

```python
import math
import jax, jax.numpy as jnp
from jax import lax
import numpy as np

D_MODEL = 2048
BATCH = 4
SEQ = 8192
DEPTH = 4

N_MIXERS = 2
N_MEM = 256
MIX_WIDTH = 2 * D_MODEL
MEM_WIDTH = MIX_WIDTH // 4
TOK_WIDTH = MIX_WIDTH - MEM_WIDTH
MEM_HEADS = 4
MEM_HEAD_DIM = MEM_WIDTH // MEM_HEADS

SSD_HEAD_DIM = 64
SSD_HEADS = TOK_WIDTH // SSD_HEAD_DIM
SSD_GROUPS = 8
SSD_HEADS_PER_GROUP = SSD_HEADS // SSD_GROUPS
SSD_STATE = 128
SSD_CONV = 4
SSD_CHUNK = 128
SSD_CONV_DIM = TOK_WIDTH + 2 * SSD_GROUPS * SSD_STATE
SSD_IN_COLS = SSD_CONV_DIM + SSD_HEADS + MEM_WIDTH + MIX_WIDTH

ATTN_HEAD_DIM = 128
ATTN_HEADS_PER_GROUP = TOK_WIDTH // ATTN_HEAD_DIM
DILATED_GROUPS = ((128, 1), (512, 4), (2048, 16))
N_DIL = len(DILATED_GROUPS)
N_ALIBI_HEADS = N_DIL * ATTN_HEADS_PER_GROUP
ALIBI_MAX_EXP = 8.0
ATTN_GROUP_COLS = 3 * TOK_WIDTH
ATTN_IN_COLS = N_DIL * ATTN_GROUP_COLS + MEM_WIDTH + MIX_WIDTH
ATTN_BLOCK = 128
EPS = 1e-6

kernel_name = 'hybrid_ssd_dilated_memory_trunk'


def _rmsnorm(x, g):
    xf = x.astype(jnp.float32)
    xf = xf * lax.rsqrt(jnp.mean(xf * xf, axis=-1, keepdims=True) + EPS)
    return xf.astype(x.dtype) * g


def _grouped_rmsnorm(x, g, groups):
    shp = x.shape
    xg = x.reshape(shp[:-1] + (groups, shp[-1] // groups)).astype(jnp.float32)
    xg = xg * lax.rsqrt(jnp.mean(xg * xg, axis=-1, keepdims=True) + EPS)
    return xg.reshape(shp).astype(x.dtype) * g


def _causal_depthwise_conv(u, w, b):
    y = lax.conv_general_dilated(u, w[:, None, :], window_strides=(1,), padding=[(SSD_CONV - 1, 0)],
                                 dimension_numbers=('NWC', 'WIO', 'NWC'), feature_group_count=u.shape[-1])
    return y + b


def _memory_cross_attention(q_mem, mem_n, w_mem_kv):
    b_, t_, _ = q_mem.shape
    mk, mv = jnp.split(mem_n @ w_mem_kv, 2, axis=-1)
    q = q_mem.reshape(b_, t_, MEM_HEADS, MEM_HEAD_DIM)
    mk = mk.reshape(b_, -1, MEM_HEADS, MEM_HEAD_DIM)
    mv = mv.reshape(b_, -1, MEM_HEADS, MEM_HEAD_DIM)
    s = jnp.einsum('bthe,bmhe->bhtm', q, mk).astype(jnp.float32) * (MEM_HEAD_DIM ** -0.5)
    p = jax.nn.softmax(s, axis=-1).astype(mv.dtype)
    return jnp.einsum('bhtm,bmhe->bthe', p, mv).reshape(b_, t_, MEM_WIDTH)


def _ssd_chunked(xh, dt, a, bm, cm):
    b_, t_ = xh.shape[:2]
    nc = t_ // SSD_CHUNK
    def chunk(v):
        return v.reshape((b_, nc, SSD_CHUNK) + v.shape[2:])
    xc, dtc, bc, cc = chunk(xh), chunk(dt), chunk(bm), chunk(cm)
    a_cs = jnp.cumsum(dtc * a, axis=2)
    xdt = xc * dtc[..., None]
    pos = jnp.arange(SSD_CHUNK)
    causal = (pos[:, None] >= pos[None, :])[None, None, :, :, None, None]
    seg = a_cs[:, :, :, None] - a_cs[:, :, None, :]
    decay_ls = jnp.exp(jnp.where(causal, seg, -jnp.inf))
    cb = jnp.einsum('bclgn,bcsgn->bclsg', cc, bc)
    y_diag = jnp.einsum('bclsg,bclsgh,bcsghp->bclghp', cb, decay_ls, xdt)
    decay_to_end = jnp.exp(a_cs[:, :, -1:] - a_cs)
    chunk_states = jnp.einsum('bcsgn,bcsgh,bcsghp->bcghpn', bc, decay_to_end, xdt)
    chunk_decay = jnp.exp(a_cs[:, :, -1])

    def step(h, inp):
        st, dec = inp
        return dec[..., None, None] * h + st, h

    h0 = jnp.zeros((b_,) + chunk_states.shape[2:], chunk_states.dtype)
    _, h_prev = lax.scan(step, h0, (jnp.moveaxis(chunk_states, 1, 0), jnp.moveaxis(chunk_decay, 1, 0)))
    h_prev = jnp.moveaxis(h_prev, 0, 1)
    y_off = jnp.einsum('bclgn,bcghpn,bclgh->bclghp', cc, h_prev, jnp.exp(a_cs))
    return (y_diag + y_off).reshape(xh.shape)


def _dilated_group_attention(q, k, v, window, dilation, slopes):
    b_, t_, nh, e_ = q.shape
    n_sub = t_ // dilation
    w = window // dilation
    c = min(ATTN_BLOCK, n_sub)
    nb = -(-n_sub // c)
    lp = nb * c
    tail = lp - n_sub
    def strided(arr):
        return arr.reshape(b_, n_sub, dilation, nh, e_)
    qs = jnp.pad(strided(q), ((0, 0), (0, tail), (0, 0), (0, 0), (0, 0)))
    ks = jnp.pad(strided(k), ((0, 0), (w, tail), (0, 0), (0, 0), (0, 0)))
    vs = jnp.pad(strided(v), ((0, 0), (w, tail), (0, 0), (0, 0), (0, 0)))
    rel = jnp.arange(c)[:, None] + w - jnp.arange(c + w)[None, :]
    band = (rel >= 0) & (rel <= w)
    bias = -slopes[:, None, None] * (dilation * rel).astype(jnp.float32)[None]
    scale = e_ ** -0.5

    def block(n):
        start = n * c
        qb = lax.dynamic_slice_in_dim(qs, start, c, axis=1)
        kb = lax.dynamic_slice_in_dim(ks, start, c + w, axis=1)
        vb = lax.dynamic_slice_in_dim(vs, start, c + w, axis=1)
        key_pos = start - w + jnp.arange(c + w)
        valid = band & (key_pos >= 0)[None, :]
        s = jnp.einsum('bqrhe,bkrhe->brhqk', qb, kb).astype(jnp.float32) * scale + bias
        s = jnp.where(valid, s, -jnp.inf)
        m = jnp.max(s, axis=-1, keepdims=True)
        p = jnp.exp(s - m)
        den = jnp.transpose(jnp.sum(p, axis=-1), (0, 3, 1, 2))
        o = jnp.einsum('brhqk,bkrhe->bqrhe', p.astype(vb.dtype), vb).astype(jnp.float32)
        lse = jnp.transpose(m[..., 0], (0, 3, 1, 2)) + jnp.log(den)
        return o / den[..., None], lse

    o, lse = lax.map(block, jnp.arange(nb))
    o = jnp.moveaxis(o, 0, 1).reshape(b_, lp, dilation, nh, e_)[:, :n_sub].reshape(b_, t_, nh, e_)
    lse = jnp.moveaxis(lse, 0, 1).reshape(b_, lp, dilation, nh)[:, :n_sub].reshape(b_, t_, nh)
    return o, lse


def _ssd_layer(x, mem_n, norm_g, w_in, conv_w, conv_b, dt_bias, a_log, d_skip, ssd_norm_g, w_mem_kv, w_out):
    b_, t_, _ = x.shape
    h = _rmsnorm(x, norm_g)
    proj = h @ w_in
    xbc, dt_raw, q_mem, z = jnp.split(
        proj, [SSD_CONV_DIM, SSD_CONV_DIM + SSD_HEADS, SSD_CONV_DIM + SSD_HEADS + MEM_WIDTH], axis=-1)
    xbc = jax.nn.silu(_causal_depthwise_conv(xbc, conv_w, conv_b))
    xs, bm, cm = jnp.split(xbc, [TOK_WIDTH, TOK_WIDTH + SSD_GROUPS * SSD_STATE], axis=-1)
    xs = xs.reshape(b_, t_, SSD_GROUPS, SSD_HEADS_PER_GROUP, SSD_HEAD_DIM)
    bm = bm.reshape(b_, t_, SSD_GROUPS, SSD_STATE)
    cm = cm.reshape(b_, t_, SSD_GROUPS, SSD_STATE)
    dt = jax.nn.softplus(dt_raw.astype(jnp.float32) + dt_bias.astype(jnp.float32))
    dt = dt.reshape(b_, t_, SSD_GROUPS, SSD_HEADS_PER_GROUP)
    a = -jnp.exp(a_log.astype(jnp.float32)).reshape(SSD_GROUPS, SSD_HEADS_PER_GROUP)
    y = _ssd_chunked(xs, dt, a, bm, cm) + d_skip.reshape(SSD_GROUPS, SSD_HEADS_PER_GROUP, 1) * xs
    y_tok = y.reshape(b_, t_, TOK_WIDTH).astype(x.dtype)
    y_mem = _memory_cross_attention(q_mem, mem_n, w_mem_kv)
    gated = jnp.concatenate([y_tok, y_mem], axis=-1) * jax.nn.silu(z)
    gated = jnp.concatenate([_grouped_rmsnorm(gated[..., :TOK_WIDTH], ssd_norm_g, SSD_GROUPS),
                             gated[..., TOK_WIDTH:]], axis=-1)
    return x + gated @ w_out


def _dilated_attention_layer(x, mem_n, norm_g, w_in, w_mem_kv, w_out):
    b_, t_, _ = x.shape
    h = _rmsnorm(x, norm_g)
    slopes = jnp.exp2(-ALIBI_MAX_EXP * jnp.arange(1, N_ALIBI_HEADS + 1, dtype=jnp.float32) / N_ALIBI_HEADS)
    slopes = slopes.reshape(N_DIL, ATTN_HEADS_PER_GROUP)
    outs, lses = [], []
    for g, (window, dilation) in enumerate(DILATED_GROUPS):
        qkv = (h @ w_in[:, g * ATTN_GROUP_COLS:(g + 1) * ATTN_GROUP_COLS])
        qkv = qkv.reshape(b_, t_, 3, ATTN_HEADS_PER_GROUP, ATTN_HEAD_DIM)
        o, lse = _dilated_group_attention(qkv[:, :, 0], qkv[:, :, 1], qkv[:, :, 2], window, dilation, slopes[g])
        outs.append(o)
        lses.append(lse)
    wts = jax.nn.softmax(jnp.stack(lses), axis=0)
    y_tok = jnp.einsum('gbth,gbthe->bthe', wts, jnp.stack(outs)).reshape(b_, t_, TOK_WIDTH).astype(x.dtype)
    q_mem, z = jnp.split(h @ w_in[:, N_DIL * ATTN_GROUP_COLS:], [MEM_WIDTH], axis=-1)
    y_mem = _memory_cross_attention(q_mem, mem_n, w_mem_kv)
    gated = jnp.concatenate([y_tok, y_mem], axis=-1) * jax.nn.silu(z)
    return x + gated @ w_out


def setup_inputs(seed: int = 0) -> dict:
    key = jax.random.key(seed)
    keys = iter(jax.random.split(key, 64))
    f32 = jnp.float32

    def nrm(shape, scale):
        return scale * jax.random.normal(next(keys), shape, f32)

    def gain(n):
        return 1.0 + nrm((n,), 0.02)

    inp = {
        'x': nrm((BATCH, SEQ, D_MODEL), 1.0),
        'mem': nrm((BATCH, N_MEM, D_MODEL), 1.0),
        'mem_norm_g': gain(D_MODEL),
        'final_norm_g': gain(D_MODEL),
    }
    for i in range(DEPTH):
        inp[f'norm_g_{i}'] = gain(D_MODEL)
        if i % N_MIXERS == 0:
            inp[f'w_in_{i}'] = nrm((D_MODEL, SSD_IN_COLS), D_MODEL ** -0.5)
            inp[f'conv_w_{i}'] = nrm((SSD_CONV, SSD_CONV_DIM), SSD_CONV ** -0.5)
            inp[f'conv_b_{i}'] = nrm((SSD_CONV_DIM,), 0.01)
            dt0 = jnp.exp(jax.random.uniform(next(keys), (SSD_HEADS,), f32, math.log(1e-3), math.log(1e-1)))
            inp[f'dt_bias_{i}'] = dt0 + jnp.log(-jnp.expm1(-dt0))
            inp[f'a_log_{i}'] = jnp.log(jax.random.uniform(next(keys), (SSD_HEADS,), f32, 1.0, 16.0))
            inp[f'd_skip_{i}'] = gain(SSD_HEADS)
            inp[f'ssd_norm_g_{i}'] = gain(TOK_WIDTH)
        else:
            inp[f'w_in_{i}'] = nrm((D_MODEL, ATTN_IN_COLS), D_MODEL ** -0.5)
        inp[f'w_mem_kv_{i}'] = nrm((D_MODEL, 2 * MEM_WIDTH), D_MODEL ** -0.5)
        inp[f'w_out_{i}'] = nrm((MIX_WIDTH, D_MODEL), MIX_WIDTH ** -0.5)
    return inp


def reference(x, mem, mem_norm_g, final_norm_g,
              norm_g_0, w_in_0, conv_w_0, conv_b_0, dt_bias_0, a_log_0, d_skip_0, ssd_norm_g_0, w_mem_kv_0, w_out_0,
              norm_g_1, w_in_1, w_mem_kv_1, w_out_1,
              norm_g_2, w_in_2, conv_w_2, conv_b_2, dt_bias_2, a_log_2, d_skip_2, ssd_norm_g_2, w_mem_kv_2, w_out_2,
              norm_g_3, w_in_3, w_mem_kv_3, w_out_3):
    mem_n = _rmsnorm(mem, mem_norm_g)
    params = [
        (norm_g_0, w_in_0, conv_w_0, conv_b_0, dt_bias_0, a_log_0, d_skip_0, ssd_norm_g_0, w_mem_kv_0, w_out_0),
        (norm_g_1, w_in_1, w_mem_kv_1, w_out_1),
        (norm_g_2, w_in_2, conv_w_2, conv_b_2, dt_bias_2, a_log_2, d_skip_2, ssd_norm_g_2, w_mem_kv_2, w_out_2),
        (norm_g_3, w_in_3, w_mem_kv_3, w_out_3),
    ]
    for i in range(DEPTH):
        layer_fn = _ssd_layer if i % N_MIXERS == 0 else _dilated_attention_layer
        x = layer_fn(x, mem_n, *params[i])
    return _rmsnorm(x, final_norm_g)
```

```python
import contextlib
import math
import numpy as np
import ml_dtypes
import concourse.bass as bass
import concourse.mybir as mybir
from concourse.bass_utils import run_bass_kernel_spmd

F32 = mybir.dt.float32
BF16 = mybir.dt.bfloat16
ACT = mybir.ActivationFunctionType
ALU = mybir.AluOpType

D = 2048
KT = D // 128
N_MEM = 256
MIXW = 4096
TOKW = 3072
MEMW = 1024
SSD_IN = 10288
ATT_IN = 32768
EPS = 1e-6
NEG = -30000.0
DIL = ((128, 1), (512, 4), (2048, 16))


class Sem:
    def __init__(self, h, idx):
        self.h = h
        self.idx = idx
        self.count = 0


class Eng:
    def __init__(self, name, eng, sem):
        self.name = name
        self.eng = eng
        self.sem = sem
        self.known = {}


class Buf:
    def __init__(self, t, name, dsem=None):
        self.t = t
        self.name = name
        self.w = {}
        self.r = {}
        self.dsem = dsem

    def __getitem__(self, k):
        return self.t[k]


class KB:
    def __init__(self, nc, n_dma_sems=90):
        self.nc = nc
        self.es = contextlib.ExitStack()
        self.sems = []
        self.engs = {}
        for name, eng in (("pe", nc.tensor), ("act", nc.scalar), ("dve", nc.vector),
                          ("pool", nc.gpsimd), ("sp", nc.sync)):
            s = self._new_sem("e_" + name)
            self.engs[name] = Eng(name, eng, s)
        self.dma_pool = [self._new_sem(f"d{i}") for i in range(n_dma_sems)]
        self.dma_free = list(self.dma_pool)
        self.regions = {}
        self.out_tokens = []
        self.uid = 0

    def _new_sem(self, name):
        h = self.es.enter_context(self.nc.semaphore(name))
        s = Sem(h, len(self.sems))
        self.sems.append(s)
        return s

    def phase(self):
        return Phase(self)

    def region(self, name):
        if name not in self.regions:
            self.regions[name] = Buf(None, name)
        return self.regions[name]

    def _need(self, E, idx, val):
        if val <= 0:
            return
        if idx == E.sem.idx and E.name in ("pe", "sp"):
            return
        if E.known.get(idx, 0) >= val:
            return
        E.eng.wait_ge(self.sems[idx].h, val)
        E.known[idx] = val

    def _deps(self, E, reads, writes):
        for b in reads:
            for i, v in b.w.items():
                self._need(E, i, v)
        for b in writes:
            for i, v in b.w.items():
                self._need(E, i, v)
            for i, v in b.r.items():
                self._need(E, i, v)

    def _mark(self, idx, val, reads, writes):
        for b in reads:
            if b.r.get(idx, 0) < val:
                b.r[idx] = val
        for b in writes:
            b.w = {idx: val}
            b.r = {}

    def _pdeps(self, E, pwrites):
        for b in pwrites:
            for i, v in b.r.items():
                self._need(E, i, v)

    def _pmark(self, idx, val, pwrites):
        for b in pwrites:
            if b.w.get(idx, 0) < val:
                b.w[idx] = val

    def op(self, e, fn, reads=(), writes=(), pwrites=()):
        E = self.engs[e]
        self._deps(E, reads, writes)
        self._pdeps(E, pwrites)
        ins = fn(E.eng)
        E.sem.count += 1
        ins.then_inc(E.sem.h, 1)
        self._mark(E.sem.idx, E.sem.count, reads, writes)
        self._pmark(E.sem.idx, E.sem.count, pwrites)

    def pe(self, fns, reads=(), writes=(), pwrites=()):
        E = self.engs["pe"]
        self._deps(E, reads, writes)
        self._pdeps(E, pwrites)
        ins = None
        for f in fns:
            ins = f(E.eng)
        E.sem.count += 1
        ins.then_inc(E.sem.h, 1)
        self._mark(E.sem.idx, E.sem.count, reads, writes)
        self._pmark(E.sem.idx, E.sem.count, pwrites)

    def dma(self, q, out, in_, sb, load, regs=(), final=False):
        E = self.engs[q]
        if load:
            for b in regs:
                for i, v in b.w.items():
                    self._need(E, i, v)
            for i, v in sb.w.items():
                if i != sb.dsem.idx:
                    self._need(E, i, v)
            for i, v in sb.r.items():
                self._need(E, i, v)
        else:
            self._deps(E, [sb], ())
            for b in regs:
                for i, v in b.r.items():
                    self._need(E, i, v)
        s = sb.dsem
        E.eng.dma_start(out=out, in_=in_).then_inc(s.h, 16)
        s.count += 16
        if load:
            for b in regs:
                if b.r.get(s.idx, 0) < s.count:
                    b.r[s.idx] = s.count
            sb.w[s.idx] = s.count
        else:
            if sb.r.get(s.idx, 0) < s.count:
                sb.r[s.idx] = s.count
            for rg in regs:
                rg.w[s.idx] = s.count
            if final:
                self.out_tokens.append((s.idx, s.count))

    def barrier(self):
        names = ["pe", "act", "dve", "pool", "sp"]
        for e in names:
            E = self.engs[e]
            for x in names:
                if x != e:
                    X = self.engs[x]
                    self._need(E, X.sem.idx, X.sem.count)
            for s in self.dma_pool:
                self._need(E, s.idx, s.count)

    def finish(self):
        E = self.engs["sp"]
        for i, v in self.out_tokens:
            self._need(E, i, v)
        self.barrier()


class Phase:
    def __init__(self, kb):
        self.kb = kb
        self.es = contextlib.ExitStack()
        self.taken = []

    def __enter__(self):
        self.es.__enter__()
        return self

    def __exit__(self, *a):
        self.kb.barrier()
        self.kb.dma_free = self.taken + self.kb.dma_free
        return self.es.__exit__(*a)

    def sb(self, name, shape, dt, dma=False):
        self.kb.uid += 1
        t = self.es.enter_context(self.kb.nc.sbuf_tensor(f"{name}_{self.kb.uid}", list(shape), dt))
        ds = None
        if dma:
            ds = self.kb.dma_free.pop(0)
            self.taken.append(ds)
        return Buf(t, name, ds)

    def ps(self, name, shape, dt=F32):
        self.kb.uid += 1
        t = self.es.enter_context(self.kb.nc.psum_tensor(f"{name}_{self.kb.uid}", list(shape), dt))
        return Buf(t, name)


class Ring:
    def __init__(self, bufs):
        self.bufs = bufs
        self.i = 0

    def next(self):
        b = self.bufs[self.i % len(self.bufs)]
        self.i += 1
        return b


def bcast_rows(ap1d, n, parts=128):
    return bass.AP(ap1d.tensor, ap1d.offset, [[0, parts], [1, n]])


C_ID, C_TRIU, C_NEGM, C_ONES, C_M01, C_REL = 0, 128, 256, 384, 512, 640
NCONST = 896


def make_consts():
    c = np.zeros((128, NCONST), np.float32)
    s = np.arange(128)[:, None]
    l = np.arange(128)[None, :]
    c[:, C_ID:C_ID + 128] = (s == l)
    c[:, C_TRIU:C_TRIU + 128] = (s <= l)
    c[:, C_NEGM:C_NEGM + 128] = np.where(l >= s, 0.0, NEG)
    c[:, C_ONES:C_ONES + 128] = 1.0
    c[:, C_M01:C_M01 + 128] = (l >= s)
    BIG = 1.0e4
    rel_prev = 128 + l - s
    rel_diag = l - s
    c[:, C_REL:C_REL + 128] = np.where(rel_prev <= 128, rel_prev, BIG)
    c[:, C_REL + 128:C_REL + 256] = np.where(rel_diag >= 0, rel_diag, BIG)
    return c


def alibi_slopes():
    j = np.arange(1, 73, dtype=np.float64)
    return np.exp2(-8.0 * j / 72.0).reshape(3, 24)


class Ctx:
    pass


def evac_engine(i):
    return "act" if i % 2 == 0 else "dve"


def copy_op(kb, e, out, in_, reads, writes=(), pwrites=(), scale=None):
    if e == "act":
        if scale is None:
            kb.op("act", lambda g: g.activation(out=out, in_=in_, func=ACT.Copy), reads, writes, pwrites)
        else:
            kb.op("act", lambda g: g.activation(out=out, in_=in_, func=ACT.Copy, scale=float(scale)),
                  reads, writes, pwrites)
    else:
        if scale is None:
            kb.op(e, lambda g: g.tensor_copy(out=out, in_=in_), reads, writes, pwrites)
        else:
            kb.op(e, lambda g: g.tensor_scalar(out=out, in0=in_, scalar1=float(scale), scalar2=None,
                                                op0=ALU.mult), reads, writes, pwrites)


def norm_tiles(kb, cx, ph, nb, x_rows_fn, xreg, gbc, hT, ntiles, t_off=0):
    for i in range(ntiles):
        xt = nb.xring.next()
        kb.dma("sp", xt[:, :], x_rows_fn(i), xt, True, regs=[xreg])
        ss = nb.ssring.next()
        rs = nb.rsring.next()
        junk = nb.junk
        kb.op("act", lambda g: g.activation(out=junk[:, :], in_=xt[:, :], func=ACT.Square, accum_out=ss[:, :]),
              reads=[xt], writes=[junk, ss])
        kb.op("dve", lambda g: g.tensor_scalar(out=ss[:, :], in0=ss[:, :], scalar1=1.0 / D, scalar2=EPS,
                                               op0=ALU.mult, op1=ALU.add), reads=[ss], writes=[ss])
        kb.op("pool", lambda g: g.tensor_tensor(out=rs[:, :], in0=ss[:, :], in1=cx.neghalf[:, :], op=ALU.pow),
              reads=[ss, cx.neghalf], writes=[rs])
        hb = nb.hbring.next()
        kb.op("dve", lambda g: g.scalar_tensor_tensor(out=hb[:, :], in0=xt[:, :], scalar=rs[:, :], in1=gbc[:, :],
                                                      op0=ALU.mult, op1=ALU.mult), reads=[xt, rs, gbc], writes=[hb])
        for q in range(4):
            pt = nb.tring.next()
            kb.pe([(lambda g, k=k: g.transpose(out=pt[:, k % 4, :], in_=hb[:, k * 128:(k + 1) * 128],
                                               identity=cx.identb[:, :])) for k in range(4 * q, 4 * q + 4)],
                  reads=[hb, cx.identb], writes=[pt])
            copy_op(kb, evac_engine(i), hT[:, 4 * q:4 * q + 4, t_off + i * 128:t_off + (i + 1) * 128], pt[:, :, :],
                    reads=[pt], pwrites=[hT])


class NormBufs:
    def __init__(self, ph):
        self.xring = Ring([ph.sb(f"xt{i}", [128, D], F32, dma=True) for i in range(2)])
        self.ssring = Ring([ph.sb(f"ss{i}", [128, 1], F32) for i in range(2)])
        self.rsring = Ring([ph.sb(f"rs{i}", [128, 1], F32) for i in range(2)])
        self.hbring = Ring([ph.sb(f"hb{i}", [128, D], BF16) for i in range(2)])
        self.junk = ph.sb("junk", [128, D], BF16)
        self.tring = Ring([ph.ps(f"tp{i}", [128, 4, 128], BF16) for i in range(2)])


def load_gain(kb, ph, g_ap, n, name):
    gb = ph.sb(name, [128, n], F32, dma=True)
    kb.dma("sp", gb[:, :], bcast_rows(g_ap, n), gb, True)
    return gb


def load_w(kb, wring, w_ap, c0, cw, kt=KT):
    wt = wring.next()
    src = w_ap.rearrange("(k p) c -> p k c", p=128)[:, :, c0:c0 + cw]
    kb.dma("pool", wt[:, :kt, :cw], src, wt, True)
    return wt


def mm_fm(kb, psb, wt, m, hT, n0, nw, kt=KT):
    kb.pe([(lambda g, k=k: g.matmul(psb[:, :nw], lhsT=wt[:, k, m * 128:(m + 1) * 128], rhs=hT[:, k, n0:n0 + nw],
                                    start=(k == 0), stop=(k == kt - 1))) for k in range(kt)],
          reads=[wt, hT], writes=[psb])


def mm_tm(kb, psb, wt, cw, hT, t0, kt=KT):
    kb.pe([(lambda g, k=k: g.matmul(psb[:, :cw], lhsT=hT[:, k, t0:t0 + 128], rhs=wt[:, k, :cw],
                                    start=(k == 0), stop=(k == kt - 1))) for k in range(kt)],
          reads=[wt, hT], writes=[psb])


def mem_prep(kb, cx, ph, mem_ap, g_ap):
    with kb.phase() as p2:
        nb = NormBufs(p2)
        gbc = load_gain(kb, p2, g_ap, D, "memg")
        norm_tiles(kb, cx, p2, nb, lambda i: mem_ap[i * 128:(i + 1) * 128, :], kb.region("mem"), gbc, cx.memnT, 2)


def mem_kv(kb, cx, ph, wkv_ap, mkT, mv):
    with kb.phase() as p2:
        wring = Ring([p2.sb(f"wkv{i}", [128, KT, 512], BF16, dma=True) for i in range(2)])
        pring = Ring([p2.ps(f"pkv{i}", [128, 512]) for i in range(2)])
        for j in range(2):
            wt = load_w(kb, wring, wkv_ap, j * 512, 512)
            for m in range(4):
                psb = pring.next()
                mm_fm(kb, psb, wt, m, cx.memnT, 0, 256)
                copy_op(kb, evac_engine(m), mkT[:, j * 4 + m, :], psb[:, :256], reads=[psb], pwrites=[mkT])
        for j in range(2):
            wt = load_w(kb, wring, wkv_ap, 1024 + j * 512, 512)
            for i in range(2):
                psb = pring.next()
                mm_tm(kb, psb, wt, 512, cx.memnT, i * 128)
                copy_op(kb, evac_engine(i), mv[:, i, j * 512:(j + 1) * 512], psb[:, :], reads=[psb], pwrites=[mv])


def mem_attention(kb, cx, T, QM, ZM, G, mkT, mv):
    rQ, rZ, rG = kb.region("QM"), kb.region("ZS"), kb.region("G")
    TBm = 512
    with kb.phase() as ph:
        qring = Ring([ph.sb(f"qm{i}", [128, 8, TBm], BF16, dma=True) for i in range(2)])
        zring = Ring([ph.sb(f"zm{i}", [128, 8, TBm], BF16, dma=True) for i in range(2)])
        gring = Ring([ph.sb(f"gm{i}", [128, 8, TBm], BF16, dma=True) for i in range(2)])
        ptring = Ring([ph.sb(f"pt{i}", [128, 2, TBm], BF16) for i in range(2)])
        rdring = Ring([ph.sb(f"rd{i}", [128, TBm], F32) for i in range(2)])
        t1ring = Ring([ph.sb(f"t1{i}", [128, TBm], F32) for i in range(2)])
        sring = Ring([ph.ps(f"sps{i}", [128, TBm]) for i in range(3)])
        dring = Ring([ph.ps(f"dps{i}", [128, TBm]) for i in range(2)])
        oring = Ring([ph.ps(f"ops{i}", [128, TBm]) for i in range(3)])
        for tb in range(T // TBm):
            t0 = tb * TBm
            qm, zm, gm = qring.next(), zring.next(), gring.next()
            kb.dma("sp", qm[:, :, :], QM.rearrange("(e p) t -> p e t", p=128)[:, :, t0:t0 + TBm], qm, True, regs=[rQ])
            kb.dma("sp", zm[:, :, :], ZM.rearrange("(e p) t -> p e t", p=128)[:, :, t0:t0 + TBm], zm, True, regs=[rZ])
            for hm in range(4):
                pt = ptring.next()
                for mb in range(2):
                    sp = sring.next()
                    kb.pe([(lambda g, et=et: g.matmul(sp[:, :], lhsT=mkT[:, hm * 2 + et, mb * 128:(mb + 1) * 128],
                                                      rhs=qm[:, hm * 2 + et, :], start=(et == 0), stop=(et == 1)))
                           for et in range(2)], reads=[mkT, qm], writes=[sp])
                    kb.op("act", lambda g: g.activation(out=pt[:, mb, :], in_=sp[:, :], func=ACT.Exp),
                          reads=[sp], pwrites=[pt])
                dp = dring.next()
                kb.pe([(lambda g, mb=mb: g.matmul(dp[:, :], lhsT=cx.onesb[:, :], rhs=pt[:, mb, :],
                                                  start=(mb == 0), stop=(mb == 1))) for mb in range(2)],
                      reads=[cx.onesb, pt], writes=[dp])
                rd = rdring.next()
                kb.op("dve", lambda g: g.reciprocal(out=rd[:, :], in_=dp[:, :]), reads=[dp], writes=[rd])
                for e2 in range(2):
                    op_ = oring.next()
                    kb.pe([(lambda g, mb=mb: g.matmul(op_[:, :], lhsT=mv[:, mb, hm * 256 + e2 * 128:hm * 256 + (e2 + 1) * 128],
                                                      rhs=pt[:, mb, :], start=(mb == 0), stop=(mb == 1)))
                           for mb in range(2)], reads=[mv, pt], writes=[op_])
                    t1 = t1ring.next()
                    kb.op("dve", lambda g: g.tensor_tensor(out=t1[:, :], in0=op_[:, :], in1=rd[:, :], op=ALU.mult),
                          reads=[op_, rd], writes=[t1])
                    kb.op("pool", lambda g: g.tensor_tensor(out=gm[:, hm * 2 + e2, :], in0=t1[:, :],
                                                            in1=zm[:, hm * 2 + e2, :], op=ALU.mult),
                          reads=[t1, zm], pwrites=[gm])
            kb.dma("sp", G[TOKW:MIXW, :].rearrange("(e p) t -> p e t", p=128)[:, :, t0:t0 + TBm], gm[:, :, :], gm, False,
                   regs=[rG])


def out_proj(kb, cx, T, G, wout_ap, x_in, x_out, rin, rout):
    rG = kb.region("G")
    TBo = 512
    KO = MIXW // 128
    with kb.phase() as ph:
        gring = Ring([ph.sb(f"gt{i}", [128, KO, TBo], BF16, dma=True) for i in range(2)])
        wring = Ring([ph.sb(f"wo{i}", [128, KO, 512], BF16, dma=True) for i in range(2)])
        xo = [ph.sb(f"xo{i}", [128, D], F32, dma=True) for i in range(4)]
        pring = Ring([ph.ps(f"pso{i}", [128, 512]) for i in range(4)])
        for tb in range(T // TBo):
            t0 = tb * TBo
            gt = gring.next()
            kb.dma("sp", gt[:, :, :], G.rearrange("(k p) t -> p k t", p=128)[:, :, t0:t0 + TBo], gt, True, regs=[rG])
            for i in range(4):
                kb.dma("sp", xo[i][:, :], x_in[t0 + i * 128:t0 + (i + 1) * 128, :], xo[i], True, regs=[rin])
            for n in range(4):
                wt = load_w(kb, wring, wout_ap, n * 512, 512, kt=KO)
                for i in range(4):
                    psb = pring.next()
                    kb.pe([(lambda g, k=k: g.matmul(psb[:, :], lhsT=gt[:, k, i * 128:(i + 1) * 128], rhs=wt[:, k, :],
                                                    start=(k == 0), stop=(k == KO - 1))) for k in range(KO)],
                          reads=[gt, wt], writes=[psb])
                    kb.op("dve", lambda g: g.tensor_tensor(out=xo[i][:, n * 512:(n + 1) * 512], in0=psb[:, :],
                                                           in1=xo[i][:, n * 512:(n + 1) * 512], op=ALU.add),
                          reads=[psb, xo[i]], pwrites=[xo[i]])
            for i in range(4):
                kb.dma("sp", x_out[t0 + i * 128:t0 + (i + 1) * 128, :], xo[i][:, :], xo[i], False, regs=[rout])


def final_norm(kb, cx, T, x_in, rin, g_ap, y_out):
    with kb.phase() as ph:
        gbc = load_gain(kb, ph, g_ap, D, "fg")
        xring = Ring([ph.sb(f"fx{i}", [128, D], F32, dma=True) for i in range(3)])
        ssring = Ring([ph.sb(f"fs{i}", [128, 1], F32) for i in range(2)])
        rsring = Ring([ph.sb(f"fr{i}", [128, 1], F32) for i in range(2)])
        junk = ph.sb("fjunk", [128, D], BF16)
        ry = kb.region("Y")
        for i in range(T // 128):
            xt, ss, rs = xring.next(), ssring.next(), rsring.next()
            kb.dma("sp", xt[:, :], x_in[i * 128:(i + 1) * 128, :], xt, True, regs=[rin])
            kb.op("act", lambda g: g.activation(out=junk[:, :], in_=xt[:, :], func=ACT.Square, accum_out=ss[:, :]),
                  reads=[xt], writes=[junk, ss])
            kb.op("dve", lambda g: g.tensor_scalar(out=ss[:, :], in0=ss[:, :], scalar1=1.0 / D, scalar2=EPS,
                                                   op0=ALU.mult, op1=ALU.add), reads=[ss], writes=[ss])
            kb.op("pool", lambda g: g.tensor_tensor(out=rs[:, :], in0=ss[:, :], in1=cx.neghalf[:, :], op=ALU.pow),
                  reads=[ss, cx.neghalf], writes=[rs])
            kb.op("dve", lambda g: g.scalar_tensor_tensor(out=xt[:, :], in0=xt[:, :], scalar=rs[:, :], in1=gbc[:, :],
                                                          op0=ALU.mult, op1=ALU.mult), reads=[xt, rs, gbc], writes=[xt])
            kb.dma("sp", y_out[i * 128:(i + 1) * 128, :], xt[:, :], xt, False, regs=[ry], final=True)


def ssd_proj(kb, cx, T, x_in, rin, P, S):
    TB = min(1024, T)
    NT = TB // 128
    NS = TB // 512
    w_in = P["w_in"]
    rS = kb.region("SSD")
    rQ, rZ = kb.region("QM"), kb.region("ZS")
    with kb.phase() as ph:
        nb = NormBufs(ph)
        gbc = load_gain(kb, ph, P["norm_g"], D, "ng")
        hT = ph.sb("hT", [128, KT, TB], BF16)
        wring = Ring([ph.sb(f"w{i}", [128, KT, 512], BF16, dma=True) for i in range(3)])
        pring = Ring([ph.ps(f"pp{i}", [128, 512]) for i in range(4)])
        tring = Ring([ph.ps(f"tq{i}", [128, 4, 128], BF16) for i in range(2)])
        cwsb = ph.sb("cw", [128, 40, 4], F32, dma=True)
        cbsb = ph.sb("cb", [128, 40], F32, dma=True)
        kb.dma("sp", cwsb[:, :, :], P["conv_w"], cwsb, True)
        kb.dma("sp", cbsb[:, :], P["conv_b"], cbsb, True)
        dtb = load_gain(kb, ph, P["dt_bias"], 48, "dtb")
        abc = load_gain(kb, ph, P["a_log"], 48, "abc")
        kb.op("act", lambda g: g.activation(out=abc[:, :], in_=abc[:, :], func=ACT.Exp), reads=[abc], writes=[abc])
        kb.op("dve", lambda g: g.tensor_scalar(out=abc[:, :], in0=abc[:, :], scalar1=-1.0, scalar2=None, op0=ALU.mult),
              reads=[abc], writes=[abc])
        hist = ph.sb("hist", [128, 40, 3], F32)
        uring = Ring([ph.sb(f"U{i}", [128, 515], F32) for i in range(2)])
        aring = Ring([ph.sb(f"acc{i}", [128, 512], F32) for i in range(2)])
        osring = Ring([ph.sb(f"os{i}", [128, 512], BF16) for i in range(2)])
        ofring = Ring([ph.sb(f"ofm{i}", [128, 4, TB], BF16, dma=True) for i in range(2)])
        otring = Ring([ph.sb(f"otm{i}", [128, NT, 512], BF16, dma=True) for i in range(2)])
        dt_all = ph.sb("dtall", [128, NT, 48], F32, dma=True)
        acs_all = ph.sb("acsall", [128, NT, 48], F32, dma=True)
        acsT_all = ph.sb("acsTall", [48, TB], F32, dma=True)
        dtx = ph.sb("dtx", [128, 48], F32)
        dta = ph.sb("dta", [128, 48], F32)
        for tb in range(T // TB):
            tok0 = tb * TB
            norm_tiles(kb, cx, ph, nb, lambda i: x_in[tok0 + i * 128:tok0 + (i + 1) * 128, :], rin, gbc, hT, NT)
            for j in range(10):
                wt = load_w(kb, wring, w_in, j * 512, 512)
                ofm = ofring.next()
                otm = otring.next() if j < 8 else None
                for m in range(4):
                    ct = j * 4 + m
                    for n in range(NS):
                        psb = pring.next()
                        mm_fm(kb, psb, wt, m, hT, n * 512, 512)
                        U = uring.next()
                        if tb == 0 and n == 0:
                            kb.op("pool", lambda g: g.memset(U[:, 0:3], 0.0), pwrites=[U])
                        else:
                            kb.op("act", lambda g: g.activation(out=U[:, 0:3], in_=hist[:, ct, :], func=ACT.Copy),
                                  reads=[hist], pwrites=[U])
                        kb.op("act", lambda g: g.activation(out=U[:, 3:515], in_=psb[:, :], func=ACT.Copy),
                              reads=[psb], pwrites=[U])
                        kb.op("act", lambda g: g.activation(out=hist[:, ct, :], in_=U[:, 512:515], func=ACT.Copy),
                              reads=[U], pwrites=[hist])
                        acc = aring.next()
                        kb.op("dve", lambda g: g.tensor_scalar(out=acc[:, :], in0=U[:, 0:512], scalar1=cwsb[:, ct, 0:1],
                                                               scalar2=cbsb[:, ct:ct + 1], op0=ALU.mult, op1=ALU.add),
                              reads=[U, cwsb, cbsb], writes=[acc])
                        for k in range(1, 4):
                            kb.op("dve", lambda g, k=k: g.scalar_tensor_tensor(
                                out=acc[:, :], in0=U[:, k:k + 512], scalar=cwsb[:, ct, k:k + 1], in1=acc[:, :],
                                op0=ALU.mult, op1=ALU.add), reads=[U, cwsb, acc], writes=[acc])
                        osb = osring.next()
                        kb.op("act", lambda g: g.activation(out=osb[:, :], in_=acc[:, :], func=ACT.Silu),
                              reads=[acc], writes=[osb])
                        if j >= 6:
                            kb.op("pool", lambda g: g.tensor_copy(out=ofm[:, m, n * 512:(n + 1) * 512], in_=osb[:, :]),
                                  reads=[osb], pwrites=[ofm])
                        if j < 8:
                            pt = tring.next()
                            kb.pe([(lambda g, q=q: g.transpose(out=pt[:, q, :], in_=osb[:, q * 128:(q + 1) * 128],
                                                               identity=cx.identb[:, :])) for q in range(4)],
                                  reads=[osb, cx.identb], writes=[pt])
                            kb.op("dve", lambda g: g.tensor_copy(out=otm[:, n * 4:(n + 1) * 4, m * 128:(m + 1) * 128],
                                                                 in_=pt[:, :, :]), reads=[pt], pwrites=[otm])
                if j < 6:
                    dst = S["XS"][tok0:tok0 + TB, j * 512:(j + 1) * 512]
                elif j < 8:
                    dst = S["BTM"][tok0:tok0 + TB, (j - 6) * 512:(j - 5) * 512]
                if j < 8:
                    kb.dma("sp", dst.rearrange("(s p) c -> p s c", p=128), otm[:, :, :], otm, False, regs=[rS])
                if j >= 6:
                    fm = S["BT"] if j < 8 else S["CT"]
                    r0 = (j - 6) * 512 if j < 8 else (j - 8) * 512
                    kb.dma("sp", fm[r0:r0 + 512, tok0:tok0 + TB].rearrange("(m p) t -> p m t", p=128), ofm[:, :, :], ofm,
                           False, regs=[rS])
            wt = load_w(kb, wring, w_in, 5120, 48)
            for i in range(NT):
                psb = pring.next()
                mm_tm(kb, psb, wt, 48, hT, i * 128)
                kb.op("dve", lambda g: g.tensor_tensor(out=dtx[:, :], in0=psb[:, :48], in1=dtb[:, :], op=ALU.add),
                      reads=[psb, dtb], writes=[dtx])
                kb.op("act", lambda g: g.activation(out=dtx[:, :], in_=dtx[:, :], func=ACT.Exp), reads=[dtx], writes=[dtx])
                kb.op("act", lambda g: g.activation(out=dt_all[:, i, :], in_=dtx[:, :], func=ACT.Ln, bias=1.0),
                      reads=[dtx], pwrites=[dt_all])
                kb.op("dve", lambda g: g.tensor_tensor(out=dta[:, :], in0=dt_all[:, i, :], in1=abc[:, :], op=ALU.mult),
                      reads=[dt_all, abc], writes=[dta])
                p1 = pring.next()
                kb.pe([lambda g: g.matmul(p1[:, :48], lhsT=cx.cst[:, C_TRIU:C_TRIU + 128], rhs=dta[:, :],
                                          start=True, stop=True)], reads=[cx.cst, dta], writes=[p1])
                kb.op("dve", lambda g: g.tensor_copy(out=acs_all[:, i, :], in_=p1[:, :48]), reads=[p1], pwrites=[acs_all])
                p2 = pring.next()
                kb.pe([lambda g: g.matmul(p2[:48, :128], lhsT=dta[:, :], rhs=cx.cst[:, C_TRIU:C_TRIU + 128],
                                          start=True, stop=True)], reads=[cx.cst, dta], writes=[p2])
                kb.op("act", lambda g: g.activation(out=acsT_all[:, i * 128:(i + 1) * 128], in_=p2[:48, :128], func=ACT.Copy),
                      reads=[p2], pwrites=[acsT_all])
            kb.dma("sp", S["DT"][tok0:tok0 + TB, :].rearrange("(i p) h -> p i h", p=128), dt_all[:, :, :], dt_all, False, regs=[rS])
            kb.dma("sp", S["ACS"][tok0:tok0 + TB, :].rearrange("(i p) h -> p i h", p=128), acs_all[:, :, :], acs_all, False, regs=[rS])
            kb.dma("sp", S["ACSH"][:, tok0:tok0 + TB], acsT_all[:, :], acsT_all, False, regs=[rS])
            for j in range(2):
                wt = load_w(kb, wring, w_in, 5168 + j * 512, 512)
                ofm = ofring.next()
                for m in range(4):
                    for n in range(NS):
                        psb = pring.next()
                        mm_fm(kb, psb, wt, m, hT, n * 512, 512)
                        copy_op(kb, evac_engine(n), ofm[:, m, n * 512:(n + 1) * 512], psb[:, :], reads=[psb], pwrites=[ofm],
                                scale=1.0 / 16.0)
                kb.dma("sp", S["QM"][j * 512:(j + 1) * 512, tok0:tok0 + TB].rearrange("(m p) t -> p m t", p=128),
                       ofm[:, :, :], ofm, False, regs=[rQ])
            for j in range(6):
                wt = load_w(kb, wring, w_in, 6192 + j * 512, 512)
                otm = otring.next()
                for i in range(NT):
                    psb = pring.next()
                    mm_tm(kb, psb, wt, 512, hT, i * 128)
                    kb.op("act", lambda g: g.activation(out=otm[:, i, :], in_=psb[:, :], func=ACT.Silu),
                          reads=[psb], pwrites=[otm])
                kb.dma("sp", S["ZT"][tok0:tok0 + TB, j * 512:(j + 1) * 512].rearrange("(s p) c -> p s c", p=128),
                       otm[:, :, :], otm, False, regs=[rS])
            for j in range(2):
                wt = load_w(kb, wring, w_in, 9264 + j * 512, 512)
                ofm = ofring.next()
                for m in range(4):
                    for n in range(NS):
                        psb = pring.next()
                        mm_fm(kb, psb, wt, m, hT, n * 512, 512)
                        kb.op("act", lambda g: g.activation(out=ofm[:, m, n * 512:(n + 1) * 512], in_=psb[:, :], func=ACT.Silu),
                              reads=[psb], pwrites=[ofm])
                kb.dma("sp", S["ZS"][TOKW + j * 512:TOKW + (j + 1) * 512, tok0:tok0 + TB].rearrange("(m p) t -> p m t", p=128),
                       ofm[:, :, :], ofm, False, regs=[rZ])


def bc_last(ap2d, n):
    return ap2d.unsqueeze(2).broadcast_to([ap2d.shape[0], ap2d.shape[1], n])


def ssd_scan(kb, cx, T, P, S, G):
    rS, rG = kb.region("SSD"), kb.region("G")
    NCH = T // 128
    with kb.phase() as ph:
        dsk = load_gain(kb, ph, P["d_skip"], 48, "dsk")
        g2 = load_gain(kb, ph, P["ssd_norm_g"], TOKW, "g2")
        xsr = Ring([ph.sb(f"xs{i}", [128, 48, 64], BF16, dma=True) for i in range(2)])
        btr = Ring([ph.sb(f"btm{i}", [128, 1024], BF16, dma=True) for i in range(2)])
        bcr = Ring([ph.sb(f"bct{i}", [128, 16, 128], BF16, dma=True) for i in range(2)])
        dtr = Ring([ph.sb(f"dtc{i}", [128, 48], F32, dma=True) for i in range(2)])
        acr = Ring([ph.sb(f"acs{i}", [128, 48], F32, dma=True) for i in range(2)])
        ztr = Ring([ph.sb(f"zt{i}", [128, TOKW], BF16, dma=True) for i in range(2)])
        rall = ph.sb("rall", [128, 48, 128], F32, dma=True)
        dec = ph.sb("dec", [128, 48, 128], BF16)
        MT = ph.sb("MT", [128, 48, 128], BF16)
        xdt = ph.sb("xdt", [128, 48, 64], BF16)
        xdtw = ph.sb("xdtw", [128, 48, 64], BF16)
        xsD = ph.sb("xsD", [128, 48, 64], BF16)
        hst = [ph.sb(f"hst{g}", [128, 6, 64], F32) for g in range(8)]
        hbf = [ph.sb(f"hbf{g}", [128, 384], BF16) for g in range(8)]
        ein = ph.sb("ein", [128, 48], F32)
        cd = ph.sb("cd", [128, 48], F32)
        dtw = ph.sb("dtw", [128, 48], F32)
        t48 = ph.sb("t48", [128, 48], F32)
        t1r = Ring([ph.sb(f"t1{i}", [128, 6, 64], F32) for i in range(2)])
        t2r = Ring([ph.sb(f"t2{i}", [128, 384], F32) for i in range(2)])
        gtr = Ring([ph.sb(f"gt{i}", [128, 384], F32) for i in range(2)])
        ssr = Ring([ph.sb(f"sq{i}", [128, 1], F32) for i in range(2)])
        rsr = Ring([ph.sb(f"rq{i}", [128, 1], F32) for i in range(2)])
        junk = ph.sb("sjunk", [128, 384], BF16)
        gated = ph.sb("gated", [128, TOKW], BF16)
        gfr = Ring([ph.sb(f"gfm{i}", [128, 24, 128], BF16, dma=True) for i in range(2)])
        cbp = [ph.ps(f"cbp{i}", [128, 4, 128]) for i in range(2)]
        ydr = Ring([ph.ps(f"yd{i}", [128, 6, 64]) for i in range(2)])
        ysr = Ring([ph.ps(f"ys{i}", [128, 6, 64]) for i in range(2)])
        tpr = Ring([ph.ps(f"tg{i}", [128, 4, 128], BF16) for i in range(2)])
        negm = cx.cst[:, C_NEGM:C_NEGM + 128]
        for c in range(NCH):
            t0 = c * 128
            xs, btm, bct, dtc, acs, zt = xsr.next(), btr.next(), bcr.next(), dtr.next(), acr.next(), ztr.next()
            kb.dma("sp", xs[:, :, :], S["XS"][t0:t0 + 128, :].rearrange("p (h e) -> p h e", e=64), xs, True, regs=[rS])
            kb.dma("sp", btm[:, :], S["BTM"][t0:t0 + 128, :], btm, True, regs=[rS])
            kb.dma("sp", bct[:, 0:8, :], S["BT"][:, t0:t0 + 128].rearrange("(g p) t -> p g t", p=128), bct, True, regs=[rS])
            kb.dma("sp", bct[:, 8:16, :], S["CT"][:, t0:t0 + 128].rearrange("(g p) t -> p g t", p=128), bct, True, regs=[rS])
            kb.dma("sp", dtc[:, :], S["DT"][t0:t0 + 128, :], dtc, True, regs=[rS])
            kb.dma("sp", acs[:, :], S["ACS"][t0:t0 + 128, :], acs, True, regs=[rS])
            src = S["ACSH"]
            kb.dma("sp", rall[:, :, :], bass.AP(src.tensor, src.offset + t0, [[0, 128], [T, 48], [1, 128]]), rall, True,
                   regs=[rS])
            kb.dma("sp", zt[:, :], S["ZT"][t0:t0 + 128, :], zt, True, regs=[rS])
            kb.op("dve", lambda g: g.tensor_tensor(out=rall[:, :, :], in0=rall[:, :, :], in1=bc_last(acs[:, :], 128),
                                                   op=ALU.subtract), reads=[rall, acs], writes=[rall])
            kb.op("pool", lambda g: g.tensor_tensor(out=rall[:, :, :], in0=rall[:, :, :],
                                                    in1=negm.unsqueeze(1).broadcast_to([128, 48, 128]), op=ALU.add),
                  reads=[rall, cx.cst], writes=[rall])
            kb.op("act", lambda g: g.activation(out=dec[:, :, :], in_=rall[:, :, :], func=ACT.Exp), reads=[rall], writes=[dec])
            kb.op("act", lambda g: g.activation(out=ein[:, :], in_=acs[:, :], func=ACT.Exp), reads=[acs], writes=[ein])
            kb.op("dve", lambda g: g.tensor_tensor(out=t48[:, :], in0=acs[:, :], in1=rall[:, :, 127], op=ALU.add),
                  reads=[acs, rall], writes=[t48])
            kb.op("act", lambda g: g.activation(out=cd[:, :], in_=t48[:, :], func=ACT.Exp), reads=[t48], writes=[cd])
            kb.op("dve", lambda g: g.tensor_tensor(out=dtw[:, :], in0=dtc[:, :], in1=dec[:, :, 127], op=ALU.mult),
                  reads=[dtc, dec], writes=[dtw])
            kb.op("dve", lambda g: g.tensor_tensor(out=xdt[:, :, :], in0=xs[:, :, :], in1=bc_last(dtc[:, :], 64),
                                                   op=ALU.mult), reads=[xs, dtc], writes=[xdt])
            kb.op("pool", lambda g: g.tensor_tensor(out=xdtw[:, :, :], in0=xs[:, :, :], in1=bc_last(dtw[:, :], 64),
                                                    op=ALU.mult), reads=[xs, dtw], writes=[xdtw])
            kb.op("pool", lambda g: g.tensor_tensor(out=xsD[:, :, :], in0=xs[:, :, :], in1=bc_last(dsk[:, :], 64),
                                                    op=ALU.mult), reads=[xs, dsk], writes=[xsD])
            for hf in range(2):
                kb.pe([(lambda g, q=q: g.matmul(cbp[hf][:, q, :], lhsT=bct[:, hf * 4 + q, :], rhs=bct[:, 8 + hf * 4 + q, :],
                                                start=True, stop=True)) for q in range(4)], reads=[bct], writes=[cbp[hf]])
                kb.op("dve", lambda g: g.tensor_tensor(
                    out=MT[:, hf * 24:(hf + 1) * 24, :].rearrange("p (q j) l -> p q j l", j=6),
                    in0=dec[:, hf * 24:(hf + 1) * 24, :].rearrange("p (q j) l -> p q j l", j=6),
                    in1=cbp[hf][:, :, :].unsqueeze(2).broadcast_to([128, 4, 6, 128]), op=ALU.mult),
                    reads=[dec, cbp[hf]], pwrites=[MT])
            for gi in range(8):
                yd = ydr.next()
                fns = [lambda g: g.matmul(yd[:, :, :], lhsT=cx.identb[:, :], rhs=xsD[:, gi * 6:(gi + 1) * 6, :],
                                          start=True, stop=False)]
                for hh in range(6):
                    h = gi * 6 + hh
                    fns.append(lambda g, h=h, hh=hh: g.matmul(yd[:, hh, :], lhsT=MT[:, h, :], rhs=xdt[:, h, :],
                                                              start=False, stop=True))
                kb.pe(fns, reads=[cx.identb, xsD, MT, xdt], writes=[yd])
                gt = gtr.next()
                ydf = yd[:, :, :].rearrange("p h e -> p (h e)")
                if c > 0:
                    yo = ysr.next()
                    kb.pe([lambda g: g.matmul(yo[:, :, :], lhsT=bct[:, 8 + gi, :], rhs=hbf[gi][:, :], start=True, stop=True)],
                          reads=[bct, hbf[gi]], writes=[yo])
                    t1 = t1r.next()
                    kb.op("dve", lambda g: g.tensor_tensor(out=t1[:, :, :], in0=yo[:, :, :],
                                                           in1=bc_last(ein[:, gi * 6:(gi + 1) * 6], 64), op=ALU.mult),
                          reads=[yo, ein], writes=[t1])
                    t2 = t2r.next()
                    kb.op("dve", lambda g: g.tensor_tensor(out=t2[:, :], in0=ydf, in1=t1[:, :, :].rearrange("p h e -> p (h e)"),
                                                           op=ALU.add), reads=[yd, t1], writes=[t2])
                    kb.op("dve", lambda g: g.tensor_tensor(out=gt[:, :], in0=t2[:, :], in1=zt[:, gi * 384:(gi + 1) * 384],
                                                           op=ALU.mult), reads=[t2, zt], writes=[gt])
                else:
                    kb.op("dve", lambda g: g.tensor_tensor(out=gt[:, :], in0=ydf, in1=zt[:, gi * 384:(gi + 1) * 384],
                                                           op=ALU.mult), reads=[yd, zt], writes=[gt])
                ss, rs = ssr.next(), rsr.next()
                kb.op("act", lambda g: g.activation(out=junk[:, :], in_=gt[:, :], func=ACT.Square, accum_out=ss[:, :]),
                      reads=[gt], writes=[junk, ss])
                kb.op("dve", lambda g: g.tensor_scalar(out=ss[:, :], in0=ss[:, :], scalar1=1.0 / 384.0, scalar2=EPS,
                                                       op0=ALU.mult, op1=ALU.add), reads=[ss], writes=[ss])
                kb.op("pool", lambda g: g.tensor_tensor(out=rs[:, :], in0=ss[:, :], in1=cx.neghalf[:, :], op=ALU.pow),
                      reads=[ss, cx.neghalf], writes=[rs])
                kb.op("dve", lambda g: g.scalar_tensor_tensor(out=gated[:, gi * 384:(gi + 1) * 384], in0=gt[:, :], scalar=rs[:, :],
                                                              in1=g2[:, gi * 384:(gi + 1) * 384], op0=ALU.mult, op1=ALU.mult),
                      reads=[gt, rs, g2], pwrites=[gated])
                if c < NCH - 1:
                    st = ysr.next()
                    kb.pe([lambda g: g.matmul(st[:, :, :], lhsT=btm[:, gi * 128:(gi + 1) * 128], rhs=xdtw[:, gi * 6:(gi + 1) * 6, :],
                                              start=True, stop=True)], reads=[btm, xdtw], writes=[st])
                    if c > 0:
                        kb.op("pool", lambda g: g.tensor_tensor(out=hst[gi][:, :, :], in0=hst[gi][:, :, :],
                                                                in1=bc_last(cd[:, gi * 6:(gi + 1) * 6], 64), op=ALU.mult),
                              reads=[hst[gi], cd], writes=[hst[gi]])
                        kb.op("dve", lambda g: g.tensor_tensor(out=hst[gi][:, :, :], in0=st[:, :, :], in1=hst[gi][:, :, :],
                                                               op=ALU.add), reads=[st, hst[gi]], writes=[hst[gi]])
                    else:
                        kb.op("dve", lambda g: g.tensor_copy(out=hst[gi][:, :, :], in_=st[:, :, :]), reads=[st], writes=[hst[gi]])
                    kb.op("act", lambda g: g.activation(out=hbf[gi][:, :], in_=hst[gi][:, :, :].rearrange("p h e -> p (h e)"),
                                                        func=ACT.Copy), reads=[hst[gi]], writes=[hbf[gi]])
            gfm = gfr.next()
            for q in range(6):
                tp = tpr.next()
                kb.pe([(lambda g, j=j: g.transpose(out=tp[:, j, :], in_=gated[:, (4 * q + j) * 128:(4 * q + j + 1) * 128],
                                                   identity=cx.identb[:, :])) for j in range(4)],
                      reads=[gated, cx.identb], writes=[tp])
                copy_op(kb, evac_engine(q), gfm[:, 4 * q:4 * q + 4, :], tp[:, :, :], reads=[tp], pwrites=[gfm])
            kb.dma("sp", G[0:TOKW, t0:t0 + 128].rearrange("(ct p) t -> p ct t", p=128), gfm[:, :, :], gfm, False, regs=[rG])


def attn_proj(kb, cx, T, x_in, rin, P, S):
    TB = min(1024, T)
    NT = TB // 128
    NS = TB // 512
    w_in = P["w_in"]
    rA, rQ, rZ = kb.region("ATT"), kb.region("QM"), kb.region("ZS")
    with kb.phase() as ph:
        nb = NormBufs(ph)
        gbc = load_gain(kb, ph, P["norm_g"], D, "ng")
        hT = ph.sb("hT", [128, KT, TB], BF16)
        wring = Ring([ph.sb(f"w{i}", [128, KT, 512], BF16, dma=True) for i in range(3)])
        pring = Ring([ph.ps(f"pp{i}", [128, 512]) for i in range(6)])
        ofring = Ring([ph.sb(f"ofm{i}", [128, 4, TB], BF16, dma=True) for i in range(2)])
        otring = Ring([ph.sb(f"otm{i}", [128, NT, 512], BF16, dma=True) for i in range(2)])
        for gi, (_, d) in enumerate(DIL):
            nsub = T // d
            xr = x_in.rearrange("(i r) c -> r i c", r=d)
            for tb in range(T // TB):
                tok0 = tb * TB

                def rows(i, tok0=tok0, nsub=nsub, xr=xr):
                    tp = tok0 + i * 128
                    r, i0 = tp // nsub, tp % nsub
                    return xr[r, i0:i0 + 128, :]

                norm_tiles(kb, cx, ph, nb, rows, rin, gbc, hT, NT)
                base = gi * 9216
                segs = [("q", base, 6), ("k", base + 3072, 6), ("v", base + 6144, 6)]
                if gi == 0:
                    segs += [("qm", 27648, 2), ("z", 28672, 8)]
                for kind, c0, ntile in segs:
                    for j in range(ntile):
                        wt = load_w(kb, wring, w_in, c0 + j * 512, 512)
                        if kind == "v":
                            otm = otring.next()
                            for i in range(NT):
                                psb = pring.next()
                                mm_tm(kb, psb, wt, 512, hT, i * 128)
                                copy_op(kb, evac_engine(i), otm[:, i, :], psb[:, :], reads=[psb], pwrites=[otm])
                            kb.dma("sp", S["V"][gi][tok0:tok0 + TB, j * 512:(j + 1) * 512].rearrange("(s p) c -> p s c", p=128),
                                   otm[:, :, :], otm, False, regs=[rA])
                            continue
                        ofm = ofring.next()
                        for m in range(4):
                            for n in range(NS):
                                psb = pring.next()
                                mm_fm(kb, psb, wt, m, hT, n * 512, 512)
                                dst = ofm[:, m, n * 512:(n + 1) * 512]
                                if kind == "z":
                                    kb.op("act", lambda g: g.activation(out=dst, in_=psb[:, :], func=ACT.Silu),
                                          reads=[psb], pwrites=[ofm])
                                elif kind == "q":
                                    copy_op(kb, evac_engine(n + m), dst, psb[:, :], reads=[psb], pwrites=[ofm],
                                            scale=128.0 ** -0.5)
                                elif kind == "qm":
                                    copy_op(kb, evac_engine(n + m), dst, psb[:, :], reads=[psb], pwrites=[ofm], scale=1.0 / 16.0)
                                else:
                                    copy_op(kb, evac_engine(n + m), dst, psb[:, :], reads=[psb], pwrites=[ofm])
                        if kind == "q":
                            dt_, rg = S["Q"][gi], rA
                        elif kind == "k":
                            dt_, rg = S["K"][gi], rA
                        elif kind == "qm":
                            dt_, rg = S["QM"], rQ
                        else:
                            dt_, rg = S["ZS"], rZ
                        kb.dma("sp", dt_[j * 512:(j + 1) * 512, tok0:tok0 + TB].rearrange("(m p) t -> p m t", p=128),
                               ofm[:, :, :], ofm, False, regs=[rg])


def attn_core(kb, cx, T, S, G):
    rA, rZ, rG = kb.region("ATT"), kb.region("ZS"), kb.region("G")
    NB = T // 128
    slopes = alibi_slopes()
    with kb.phase() as ph:
        qr = Ring([ph.sb(f"aq{i}", [128, T], BF16, dma=True) for i in range(2)])
        kr = Ring([ph.sb(f"ak{i}", [128, T], BF16, dma=True) for i in range(2)])
        vr = Ring([ph.sb(f"av{i}", [128, NB, 128], BF16, dma=True) for i in range(2)])
        acc = ph.sb("acc", [128, T], F32)
        dacc = ph.sb("dacc", [128, T], F32)
        ebr = Ring([ph.sb(f"eb{i}", [128, 2, 128], BF16) for i in range(3)])
        pexr = Ring([ph.sb(f"pex{i}", [128, 2, 2, 128], BF16) for i in range(3)])
        ptr_ = Ring([ph.sb(f"ptt{i}", [128, 2, 2, 128], BF16) for i in range(3)])
        spr = Ring([ph.ps(f"sp{i}", [128, 2, 2, 128]) for i in range(3)])
        opr = Ring([ph.ps(f"op{i}", [128, 4, 128]) for i in range(2)])
        dpr = Ring([ph.ps(f"dp{i}", [128, 4, 128]) for i in range(2)])
        rel = cx.cst[:, C_REL:C_REL + 256]
        for j in range(24):
            for gi, (_, d) in enumerate(DIL):
                nsub = T // d
                nbs = nsub // 128
                QB = min(4, nbs)
                qT, kT, v = qr.next(), kr.next(), vr.next()
                kb.dma("sp", qT[:, :], S["Q"][gi][j * 128:(j + 1) * 128, :], qT, True, regs=[rA])
                kb.dma("sp", kT[:, :], S["K"][gi][j * 128:(j + 1) * 128, :], kT, True, regs=[rA])
                kb.dma("sp", v[:, :, :], S["V"][gi][:, j * 128:(j + 1) * 128].rearrange("(b p) e -> p b e", p=128), v, True,
                       regs=[rA])
                eb = ebr.next()
                sc = -float(slopes[gi, j]) * d
                kb.op("act", lambda g: g.activation(out=eb[:, :, :].rearrange("p a q -> p (a q)"), in_=rel, func=ACT.Exp, scale=sc),
                      reads=[cx.cst], writes=[eb])
                for r in range(d):
                    for kb0 in range(0, nbs, QB):
                        op_, dp = opr.next(), dpr.next()
                        pts = []
                        for pair in range(0, QB, 2):
                            npair = min(2, QB - pair)
                            sp, pex, pt = spr.next(), pexr.next(), ptr_.next()
                            fns = []
                            for a in range(npair):
                                kbi = kb0 + pair + a
                                bb = r * nbs + kbi
                                if kbi > 0:
                                    fns.append(lambda g, a=a, bb=bb: g.matmul(sp[:, a, 0, :], lhsT=kT[:, (bb - 1) * 128:bb * 128],
                                                                              rhs=qT[:, bb * 128:(bb + 1) * 128], start=True, stop=True))
                                fns.append(lambda g, a=a, bb=bb: g.matmul(sp[:, a, 1, :], lhsT=kT[:, bb * 128:(bb + 1) * 128],
                                                                          rhs=qT[:, bb * 128:(bb + 1) * 128], start=True, stop=True))
                            kb.pe(fns, reads=[kT, qT], writes=[sp])
                            first = (kb0 + pair == 0)
                            if first:
                                kb.op("pool", lambda g: g.memset(pt[:, 0, 0, :], 0.0), pwrites=[pt])
                                kb.op("act", lambda g: g.activation(out=pex[:, 0, 1, :], in_=sp[:, 0, 1, :], func=ACT.Exp),
                                      reads=[sp], pwrites=[pex])
                                kb.op("dve", lambda g: g.tensor_tensor(out=pt[:, 0, 1, :], in0=pex[:, 0, 1, :], in1=eb[:, 1, :],
                                                                       op=ALU.mult), reads=[pex, eb], pwrites=[pt])
                                if npair > 1:
                                    kb.op("act", lambda g: g.activation(out=pex[:, 1, :, :], in_=sp[:, 1, :, :], func=ACT.Exp),
                                          reads=[sp], pwrites=[pex])
                                    kb.op("dve", lambda g: g.tensor_tensor(out=pt[:, 1, :, :], in0=pex[:, 1, :, :], in1=eb[:, :, :],
                                                                           op=ALU.mult), reads=[pex, eb], pwrites=[pt])
                            else:
                                kb.op("act", lambda g: g.activation(out=pex[:, :npair, :, :], in_=sp[:, :npair, :, :], func=ACT.Exp),
                                      reads=[sp], pwrites=[pex])
                                kb.op("dve", lambda g: g.tensor_tensor(
                                    out=pt[:, :npair, :, :], in0=pex[:, :npair, :, :],
                                    in1=eb[:, :, :].unsqueeze(1).broadcast_to([128, npair, 2, 128]), op=ALU.mult),
                                    reads=[pex, eb], pwrites=[pt])
                            pts.append((pt, pair, npair))
                        fo, fd = [], []
                        for pt, pair, npair in pts:
                            for a in range(npair):
                                kbi = kb0 + pair + a
                                bb = r * nbs + kbi
                                qi = pair + a
                                lo = 0 if kbi > 0 else 1
                                for part in range(lo, 2):
                                    vb = bb - 1 + part
                                    fo.append(lambda g, pt=pt, a=a, part=part, vb=vb, qi=qi, lo=lo: g.matmul(
                                        op_[:, qi, :], lhsT=v[:, vb, :], rhs=pt[:, a, part, :], start=(part == lo), stop=(part == 1)))
                                    fd.append(lambda g, pt=pt, a=a, part=part, qi=qi, lo=lo: g.matmul(
                                        dp[:, qi, :], lhsT=cx.onesb[:, :], rhs=pt[:, a, part, :], start=(part == lo), stop=(part == 1)))
                        rd = [p[0] for p in pts]
                        kb.pe(fo, reads=[v] + rd, writes=[op_])
                        kb.pe(fd, reads=[cx.onesb] + rd, writes=[dp])
                        tstart = kb0 * 128 * d + r
                        n_q = QB * 128
                        if d == 1:
                            a_out = acc[:, tstart:tstart + n_q]
                            d_out = dacc[:, tstart:tstart + n_q]
                        else:
                            a_out = acc[:, :].rearrange("p (q s) -> p q s", s=d)[:, kb0 * 128:kb0 * 128 + n_q, r]
                            d_out = dacc[:, :].rearrange("p (q s) -> p q s", s=d)[:, kb0 * 128:kb0 * 128 + n_q, r]
                        o_in = op_[:, :QB, :].rearrange("p a q -> p (a q)")
                        d_in = dp[:, :QB, :].rearrange("p a q -> p (a q)")
                        if gi == 0:
                            kb.op("act", lambda g: g.activation(out=a_out, in_=o_in, func=ACT.Copy), reads=[op_], pwrites=[acc])
                            kb.op("act", lambda g: g.activation(out=d_out, in_=d_in, func=ACT.Copy), reads=[dp], pwrites=[dacc])
                        else:
                            kb.op("dve", lambda g: g.tensor_tensor(out=a_out, in0=o_in, in1=a_out, op=ALU.add),
                                  reads=[op_, acc], pwrites=[acc])
                            kb.op("dve", lambda g: g.tensor_tensor(out=d_out, in0=d_in, in1=d_out, op=ALU.add),
                                  reads=[dp, dacc], pwrites=[dacc])
            zs = qr.next()
            kb.dma("sp", zs[:, :], S["ZS"][j * 128:(j + 1) * 128, :], zs, True, regs=[rZ])
            kb.op("dve", lambda g: g.reciprocal(out=dacc[:, :], in_=dacc[:, :]), reads=[dacc], writes=[dacc])
            kb.op("dve", lambda g: g.tensor_tensor(out=acc[:, :], in0=acc[:, :], in1=dacc[:, :], op=ALU.mult),
                  reads=[acc, dacc], writes=[acc])
            kb.op("pool", lambda g: g.tensor_tensor(out=zs[:, :], in0=acc[:, :], in1=zs[:, :], op=ALU.mult),
                  reads=[acc, zs], writes=[zs])
            kb.dma("pool", G[j * 128:(j + 1) * 128, :], zs[:, :], zs, False, regs=[rG])


SSD_KEYS = ("norm_g", "w_in", "conv_w", "conv_b", "dt_bias", "a_log", "d_skip", "ssd_norm_g", "w_mem_kv", "w_out")
ATT_KEYS = ("norm_g", "w_in", "w_mem_kv", "w_out")
SSD_SHAPES = {"norm_g": [D], "w_in": [D, SSD_IN], "conv_w": [128, 40, 4], "conv_b": [128, 40], "dt_bias": [48],
              "a_log": [48], "d_skip": [48], "ssd_norm_g": [TOKW], "w_mem_kv": [D, 2048], "w_out": [MIXW, D]}
ATT_SHAPES = {"norm_g": [D], "w_in": [D, ATT_IN], "w_mem_kv": [D, 2048], "w_out": [MIXW, D]}


def build(T, layer_ids, with_final=True):
    nc = bass.Bass("TRN2", target_bir_lowering=False)

    def din(name, shape):
        return nc.dram_tensor(name, list(shape), F32, kind="ExternalInput").ap()

    def scr(name, shape, dt=BF16):
        return nc.dram_tensor(name, list(shape), dt, kind="Internal").ap()

    x_ext = din("x", [T, D])
    mem = din("mem", [N_MEM, D])
    consts = din("consts", [128, NCONST])
    mem_g = din("mem_norm_g", [D])
    fin_g = din("final_norm_g", [D]) if with_final else None
    params = {}
    for li in layer_ids:
        shapes = SSD_SHAPES if li % 2 == 0 else ATT_SHAPES
        params[li] = {k: din(f"{k}_{li}", shp) for k, shp in shapes.items()}
    y_ext = nc.dram_tensor("y", [T, D], F32, kind="ExternalOutput").ap()
    xs_ = [scr("xa", [T, D], F32), scr("xb", [T, D], F32)]
    S = {"QM": scr("QM", [MEMW, T]), "ZS": scr("ZS", [MIXW, T])}
    G = scr("G", [MIXW, T])
    if any(li % 2 == 0 for li in layer_ids):
        S.update({"XS": scr("XS", [T, TOKW]), "BTM": scr("BTM", [T, 1024]), "BT": scr("BT", [1024, T]),
                  "CT": scr("CT", [1024, T]), "DT": scr("DT", [T, 48], F32), "ACS": scr("ACS", [T, 48], F32),
                  "ACSH": scr("ACSH", [48, T], F32), "ZT": scr("ZT", [T, TOKW])})
    if any(li % 2 == 1 for li in layer_ids):
        S.update({"Q": [scr(f"Q{g}", [TOKW, T]) for g in range(3)], "K": [scr(f"K{g}", [TOKW, T]) for g in range(3)],
                  "V": [scr(f"V{g}", [T, TOKW]) for g in range(3)]})

    kb = KB(nc)
    cx = Ctx()
    with kb.es:
        with kb.phase() as top:
            cx.cst = top.sb("cst", [128, NCONST], F32, dma=True)
            cx.cstb = top.sb("cstb", [128, NCONST], BF16, dma=True)
            kb.dma("sp", cx.cst[:, :], consts, cx.cst, True)
            kb.dma("pool", cx.cstb[:, :], consts, cx.cstb, True)
            cx.identb = Buf(cx.cstb.t, "identb")
            cx.onesb = Buf(cx.cstb.t, "onesb")
            cx.identb = _View(cx.cstb, C_ID, 128)
            cx.onesb = _View(cx.cstb, C_ONES, 128)
            cx.neghalf = top.sb("neghalf", [128, 1], F32)
            kb.op("pool", lambda g: g.memset(cx.neghalf[:, :], -0.5), writes=[cx.neghalf])
            cx.memnT = top.sb("memnT", [128, KT, N_MEM], BF16)
            mem_prep(kb, cx, top, mem, mem_g)
            cur, rcur = x_ext, kb.region("xext")
            for n_, li in enumerate(layer_ids):
                P = params[li]
                nxt = xs_[n_ % 2]
                rnxt = kb.region(f"x{n_ % 2}")
                with kb.phase() as lp:
                    mkT = lp.sb("mkT", [128, 8, N_MEM], BF16)
                    mv = lp.sb("mv", [128, 2, MEMW], BF16)
                    mem_kv(kb, cx, lp, P["w_mem_kv"], mkT, mv)
                    if li % 2 == 0:
                        ssd_proj(kb, cx, T, cur, rcur, P, S)
                        ssd_scan(kb, cx, T, P, S, G)
                    else:
                        attn_proj(kb, cx, T, cur, rcur, P, S)
                        attn_core(kb, cx, T, S, G)
                    mem_attention(kb, cx, T, S["QM"], S["ZS"][TOKW:MIXW, :], G, mkT, mv)
                    if not with_final and n_ == len(layer_ids) - 1:
                        out_proj(kb, cx, T, G, P["w_out"], cur, y_ext, rcur, kb.region("Y"))
                    else:
                        out_proj(kb, cx, T, G, P["w_out"], cur, nxt, rcur, rnxt)
                cur, rcur = nxt, rnxt
            if with_final:
                final_norm(kb, cx, T, cur, rcur, fin_g, y_ext)
            kb.finish()
    return nc


class _View:
    def __init__(self, base, c0, n):
        self.base = base
        self.c0 = c0
        self.n = n

    @property
    def w(self):
        return self.base.w

    @w.setter
    def w(self, v):
        self.base.w = v

    @property
    def r(self):
        return self.base.r

    @r.setter
    def r(self, v):
        self.base.r = v

    def __getitem__(self, k):
        assert isinstance(k, tuple) and len(k) == 2
        cs = k[1]
        assert cs == slice(None)
        return self.base.t[k[0], self.c0:self.c0 + self.n]


def _layer_inputs(inputs, li):
    d = {}
    keys = SSD_KEYS if li % 2 == 0 else ATT_KEYS
    for k in keys:
        a = np.ascontiguousarray(np.asarray(inputs[f"{k}_{li}"], dtype=np.float32))
        if k == "conv_w":
            a = np.ascontiguousarray(a.T.reshape(40, 128, 4).transpose(1, 0, 2))
        elif k == "conv_b":
            a = np.ascontiguousarray(a.reshape(40, 128).T)
        d[f"{k}_{li}"] = a
    return d


_NC_CACHE = {}


def run_layers(x, mem, inputs, layer_ids, with_final, n_cores=None):
    B, T, _ = x.shape
    key = (T, tuple(layer_ids), with_final)
    if key not in _NC_CACHE:
        _NC_CACHE[key] = build(T, list(layer_ids), with_final)
    nc = _NC_CACHE[key]
    shared = {"consts": make_consts(), "mem_norm_g": np.asarray(inputs["mem_norm_g"], np.float32)}
    if with_final:
        shared["final_norm_g"] = np.asarray(inputs["final_norm_g"], np.float32)
    for li in layer_ids:
        shared.update(_layer_inputs(inputs, li))
    in_maps = []
    for b in range(B):
        m = dict(shared)
        m["x"] = np.ascontiguousarray(x[b], dtype=np.float32)
        m["mem"] = np.ascontiguousarray(mem[b], dtype=np.float32)
        in_maps.append(m)
    res = run_bass_kernel_spmd(nc, in_maps, core_ids=list(range(B)))
    return np.stack([np.asarray(r["y"]) for r in res.results], axis=0)


def kernel(**inputs):
    x = np.asarray(inputs["x"], np.float32)
    mem = np.asarray(inputs["mem"], np.float32)
    return run_layers(x, mem, inputs, [0, 1, 2, 3], True).astype(np.float32)
```

```python
import contextlib
import math
import os
import numpy as np
import ml_dtypes
import concourse.bass as bass
import concourse.mybir as mybir
from concourse.bass_utils import run_bass_kernel_spmd

F32 = mybir.dt.float32
BF16 = mybir.dt.bfloat16
ACT = mybir.ActivationFunctionType
ALU = mybir.AluOpType

D = 2048
KT = D // 128
N_MEM = 256
MIXW = 4096
TOKW = 3072
MEMW = 1024
SSD_IN = 10288
ATT_IN = 32768
EPS = 1e-6
NEG = -30000.0
DIL = ((128, 1), (512, 4), (2048, 16))


class Sem:
    def __init__(self, h, idx):
        self.h = h
        self.idx = idx
        self.count = 0


class Eng:
    def __init__(self, name, eng, sem):
        self.name = name
        self.eng = eng
        self.sem = sem
        self.known = {}


class Buf:
    def __init__(self, t, name, dsem=None):
        self.t = t
        self.name = name
        self.w = {}
        self.r = {}
        self.dsem = dsem

    def __getitem__(self, k):
        return self.t[k]


class KB:
    def __init__(self, nc, n_dma_sems=90):
        self.nc = nc
        self.es = contextlib.ExitStack()
        self.sems = []
        self.engs = {}
        for name, eng in (("pe", nc.tensor), ("act", nc.scalar), ("dve", nc.vector),
                          ("pool", nc.gpsimd), ("sp", nc.sync)):
            s = self._new_sem("e_" + name)
            self.engs[name] = Eng(name, eng, s)
        self.dma_pool = [self._new_sem(f"d{i}") for i in range(60 if os.environ.get("KB_SIM") else n_dma_sems)]
        self.dma_free = list(self.dma_pool)
        self.regions = {}
        self.out_tokens = []
        self.uid = 0

    def _new_sem(self, name):
        h = self.es.enter_context(self.nc.semaphore(name))
        s = Sem(h, len(self.sems))
        self.sems.append(s)
        return s

    def phase(self):
        return Phase(self)

    def region(self, name):
        if name not in self.regions:
            self.regions[name] = Buf(None, name)
        return self.regions[name]

    def _need(self, E, idx, val):
        if val <= 0:
            return
        if idx == E.sem.idx and (E.name in ("pe", "sp") or not _SELFWAIT):
            return
        if E.known.get(idx, 0) >= val:
            return
        E.eng.wait_ge(self.sems[idx].h, val)
        E.known[idx] = val

    def _deps(self, E, reads, writes):
        for b in reads:
            for i, v in b.w.items():
                self._need(E, i, v)
        for b in writes:
            for i, v in b.w.items():
                self._need(E, i, v)
            for i, v in b.r.items():
                self._need(E, i, v)

    def _mark(self, idx, val, reads, writes):
        for b in reads:
            if b.r.get(idx, 0) < val:
                b.r[idx] = val
        for b in writes:
            b.w = {idx: val}
            b.r = {}

    def _pdeps(self, E, pwrites):
        for b in pwrites:
            for i, v in b.r.items():
                self._need(E, i, v)

    def _pmark(self, idx, val, pwrites):
        for b in pwrites:
            if b.w.get(idx, 0) < val:
                b.w[idx] = val

    def op(self, e, fn, reads=(), writes=(), pwrites=()):
        E = self.engs[e]
        self._deps(E, reads, writes)
        self._pdeps(E, pwrites)
        ins = fn(E.eng)
        E.sem.count += 1
        ins.then_inc(E.sem.h, 1)
        self._mark(E.sem.idx, E.sem.count, reads, writes)
        self._pmark(E.sem.idx, E.sem.count, pwrites)

    def pe(self, fns, reads=(), writes=(), pwrites=()):
        E = self.engs["pe"]
        self._deps(E, reads, writes)
        self._pdeps(E, pwrites)
        ins = None
        for f in fns:
            ins = f(E.eng)
        E.sem.count += 1
        ins.then_inc(E.sem.h, 1)
        self._mark(E.sem.idx, E.sem.count, reads, writes)
        self._pmark(E.sem.idx, E.sem.count, pwrites)

    def dma(self, q, out, in_, sb, load, regs=(), final=False):
        E = self.engs[q]
        if load:
            for b in regs:
                for i, v in b.w.items():
                    self._need(E, i, v)
            for i, v in sb.w.items():
                if i != sb.dsem.idx:
                    self._need(E, i, v)
            for i, v in sb.r.items():
                self._need(E, i, v)
        else:
            self._deps(E, [sb], ())
            for b in regs:
                for i, v in b.r.items():
                    self._need(E, i, v)
        s = sb.dsem
        E.eng.dma_start(out=out, in_=in_).then_inc(s.h, 16)
        s.count += 16
        if load:
            for b in regs:
                if b.r.get(s.idx, 0) < s.count:
                    b.r[s.idx] = s.count
            sb.w[s.idx] = s.count
        else:
            if sb.r.get(s.idx, 0) < s.count:
                sb.r[s.idx] = s.count
            for rg in regs:
                rg.w[s.idx] = s.count
            if final:
                self.out_tokens.append((s.idx, s.count))

    def barrier(self):
        names = ["pe", "act", "dve", "pool", "sp"]
        for e in names:
            E = self.engs[e]
            for x in names:
                if x != e:
                    X = self.engs[x]
                    self._need(E, X.sem.idx, X.sem.count)
            for s in self.dma_pool:
                self._need(E, s.idx, s.count)

    def finish(self):
        E = self.engs["sp"]
        for i, v in self.out_tokens:
            self._need(E, i, v)
        self.barrier()


class Phase:
    def __init__(self, kb):
        self.kb = kb
        self.es = contextlib.ExitStack()
        self.taken = []

    def __enter__(self):
        self.es.__enter__()
        return self

    def __exit__(self, *a):
        self.kb.barrier()
        self.kb.dma_free = self.taken + self.kb.dma_free
        return self.es.__exit__(*a)

    def sb(self, name, shape, dt, dma=False):
        self.kb.uid += 1
        t = self.es.enter_context(self.kb.nc.sbuf_tensor(f"{name}_{self.kb.uid}", list(shape), dt))
        ds = None
        if dma == "pool" and os.environ.get("KB_SIM"):
            ds = self.kb._new_sem(f"pd{self.kb.uid}")
        elif dma:
            ds = self.kb.dma_free.pop(0)
            self.taken.append(ds)
        return Buf(t, name, ds)

    def ps(self, name, shape, dt=F32):
        self.kb.uid += 1
        t = self.es.enter_context(self.kb.nc.psum_tensor(f"{name}_{self.kb.uid}", list(shape), dt))
        return Buf(t, name)


class Ring:
    def __init__(self, bufs):
        self.bufs = bufs
        self.i = 0

    def next(self):
        b = self.bufs[self.i % len(self.bufs)]
        self.i += 1
        return b


def bcast_rows(ap1d, n, parts=128):
    return bass.AP(ap1d.tensor, ap1d.offset, [[0, parts], [1, n]])


C_ID, C_TRIU, C_NEGM, C_ONES, C_M01, C_REL, C_SEL = 0, 128, 256, 384, 512, 640, 896
NCONST = 1024


def make_consts():
    c = np.zeros((128, NCONST), np.float32)
    s = np.arange(128)[:, None]
    l = np.arange(128)[None, :]
    c[:, C_ID:C_ID + 128] = (s == l)
    c[:, C_TRIU:C_TRIU + 128] = (s <= l)
    c[:, C_NEGM:C_NEGM + 128] = np.where(l >= s, 0.0, NEG)
    c[:, C_ONES:C_ONES + 128] = 1.0
    c[:, C_M01:C_M01 + 128] = (l >= s)
    BIG = 1.0e4
    rel_prev = 128 + l - s
    rel_diag = l - s
    c[:, C_REL:C_REL + 128] = np.where(rel_prev <= 128, rel_prev, BIG)
    c[:, C_REL + 128:C_REL + 256] = np.where(rel_diag >= 0, rel_diag, BIG)
    c[127, C_SEL:C_SEL + 128] = 1.0
    return c


def alibi_slopes():
    j = np.arange(1, 73, dtype=np.float64)
    return np.exp2(-8.0 * j / 72.0).reshape(3, 24)


class Ctx:
    pass


def evac_engine(i):
    return "act" if i % 2 == 0 else "dve"


def copy_op(kb, e, out, in_, reads, writes=(), pwrites=(), scale=None):
    if e == "act":
        if scale is None:
            kb.op("act", lambda g: g.activation(out=out, in_=in_, func=ACT.Copy), reads, writes, pwrites)
        else:
            kb.op("act", lambda g: g.activation(out=out, in_=in_, func=ACT.Copy, scale=float(scale)),
                  reads, writes, pwrites)
    else:
        if scale is None:
            kb.op(e, lambda g: g.tensor_copy(out=out, in_=in_), reads, writes, pwrites)
        else:
            kb.op(e, lambda g: g.tensor_scalar(out=out, in0=in_, scalar1=float(scale), scalar2=None,
                                                op0=ALU.mult), reads, writes, pwrites)


def norm_tiles(kb, cx, ph, nb, x_rows_fn, xreg, gbc, hT, ntiles, t_off=0):
    for i in range(ntiles):
        xt = nb.xring.next()
        kb.dma("sp", xt[:, :], x_rows_fn(i), xt, True, regs=[xreg])
        ss = nb.ssring.next()
        rs = nb.rsring.next()
        junk = nb.junk
        kb.op("act", lambda g: g.activation(out=junk[:, :], in_=xt[:, :], func=ACT.Square, accum_out=ss[:, :]),
              reads=[xt], writes=[junk, ss])
        kb.op("dve", lambda g: g.tensor_scalar(out=ss[:, :], in0=ss[:, :], scalar1=1.0 / D, scalar2=EPS,
                                               op0=ALU.mult, op1=ALU.add), reads=[ss], writes=[ss])
        kb.op("pool", lambda g: g.tensor_tensor(out=rs[:, :], in0=ss[:, :], in1=cx.neghalf[:, :], op=ALU.pow),
              reads=[ss, cx.neghalf], writes=[rs])
        hb = nb.hbring.next()
        kb.op("dve", lambda g: g.scalar_tensor_tensor(out=hb[:, :], in0=xt[:, :], scalar=rs[:, :], in1=gbc[:, :],
                                                      op0=ALU.mult, op1=ALU.mult), reads=[xt, rs, gbc], writes=[hb])
        for q in range(4):
            pt = nb.tring.next()
            kb.pe([(lambda g, k=k: g.transpose(out=pt[:, k % 4, :], in_=hb[:, k * 128:(k + 1) * 128],
                                               identity=cx.identb[:, :])) for k in range(4 * q, 4 * q + 4)],
                  reads=[hb, cx.identb], writes=[pt])
            copy_op(kb, evac_engine(i), hT[:, 4 * q:4 * q + 4, t_off + i * 128:t_off + (i + 1) * 128], pt[:, :, :],
                    reads=[pt], pwrites=[hT])


class NormBufs:
    def __init__(self, ph):
        self.xring = Ring([ph.sb(f"xt{i}", [128, D], F32, dma=True) for i in range(2)])
        self.ssring = Ring([ph.sb(f"ss{i}", [128, 1], F32) for i in range(2)])
        self.rsring = Ring([ph.sb(f"rs{i}", [128, 1], F32) for i in range(2)])
        self.hbring = Ring([ph.sb(f"hb{i}", [128, D], BF16) for i in range(2)])
        self.junk = ph.sb("junk", [128, D], BF16)
        self.tring = Ring([ph.ps(f"tp{i}", [128, 4, 128], BF16) for i in range(2)])


def load_gain(kb, ph, g_ap, n, name):
    gb = ph.sb(name, [128, n], F32, dma=True)
    kb.dma("sp", gb[:, :], bcast_rows(g_ap, n), gb, True)
    return gb


def load_w(kb, wring, w_ap, c0, cw, kt=KT):
    wt = wring.next()
    src = w_ap.rearrange("(k p) c -> p k c", p=128)[:, :, c0:c0 + cw]
    kb.dma("pool", wt[:, :kt, :cw], src, wt, True)
    return wt


def mm_fm(kb, psb, wt, m, hT, n0, nw, kt=KT):
    kb.pe([(lambda g, k=k: g.matmul(psb[:, :nw], lhsT=wt[:, k, m * 128:(m + 1) * 128], rhs=hT[:, k, n0:n0 + nw],
                                    start=(k == 0), stop=(k == kt - 1))) for k in range(kt)],
          reads=[wt, hT], writes=[psb])


def mm_tm(kb, psb, wt, cw, hT, t0, kt=KT):
    kb.pe([(lambda g, k=k: g.matmul(psb[:, :cw], lhsT=hT[:, k, t0:t0 + 128], rhs=wt[:, k, :cw],
                                    start=(k == 0), stop=(k == kt - 1))) for k in range(kt)],
          reads=[wt, hT], writes=[psb])


def mem_prep(kb, cx, ph, mem_ap, g_ap):
    with kb.phase() as p2:
        nb = NormBufs(p2)
        gbc = load_gain(kb, p2, g_ap, D, "memg")
        norm_tiles(kb, cx, p2, nb, lambda i: mem_ap[i * 128:(i + 1) * 128, :], kb.region("mem"), gbc, cx.memnT, 2)


def mem_kv(kb, cx, ph, wkv_ap, mkT, mv):
    with kb.phase() as p2:
        wring = Ring([p2.sb(f"wkv{i}", [128, KT, 512], BF16, dma="pool") for i in range(2)])
        pring = Ring([p2.ps(f"pkv{i}", [128, 512]) for i in range(2)])
        for j in range(2):
            wt = load_w(kb, wring, wkv_ap, j * 512, 512)
            for m in range(4):
                psb = pring.next()
                mm_fm(kb, psb, wt, m, cx.memnT, 0, 256)
                copy_op(kb, evac_engine(m), mkT[:, j * 4 + m, :], psb[:, :256], reads=[psb], pwrites=[mkT])
        for j in range(2):
            wt = load_w(kb, wring, wkv_ap, 1024 + j * 512, 512)
            for i in range(2):
                psb = pring.next()
                mm_tm(kb, psb, wt, 512, cx.memnT, i * 128)
                copy_op(kb, evac_engine(i), mv[:, i, j * 512:(j + 1) * 512], psb[:, :], reads=[psb], pwrites=[mv])


def mem_attention(kb, cx, T, QM, ZM, G, mkT, mv):
    rQ, rZ, rG = kb.region("QM"), kb.region("ZS"), kb.region("G")
    TBm = 512
    with kb.phase() as ph:
        qring = Ring([ph.sb(f"qm{i}", [128, 8, TBm], BF16, dma=True) for i in range(2)])
        zring = Ring([ph.sb(f"zm{i}", [128, 8, TBm], BF16, dma=True) for i in range(2)])
        gring = Ring([ph.sb(f"gm{i}", [128, 8, TBm], BF16, dma=True) for i in range(2)])
        ptring = Ring([ph.sb(f"pt{i}", [128, 2, TBm], BF16) for i in range(2)])
        rdring = Ring([ph.sb(f"rd{i}", [128, TBm], F32) for i in range(2)])
        t1ring = Ring([ph.sb(f"t1{i}", [128, TBm], F32) for i in range(2)])
        sring = Ring([ph.ps(f"sps{i}", [128, TBm]) for i in range(3)])
        dring = Ring([ph.ps(f"dps{i}", [128, TBm]) for i in range(2)])
        oring = Ring([ph.ps(f"ops{i}", [128, TBm]) for i in range(3)])
        for tb in range(T // TBm):
            t0 = tb * TBm
            qm, zm, gm = qring.next(), zring.next(), gring.next()
            kb.dma("sp", qm[:, :, :], QM.rearrange("(e p) t -> p e t", p=128)[:, :, t0:t0 + TBm], qm, True, regs=[rQ])
            kb.dma("sp", zm[:, :, :], ZM.rearrange("(e p) t -> p e t", p=128)[:, :, t0:t0 + TBm], zm, True, regs=[rZ])
            for hm in range(4):
                pt = ptring.next()
                for mb in range(2):
                    sp = sring.next()
                    kb.pe([(lambda g, et=et: g.matmul(sp[:, :], lhsT=mkT[:, hm * 2 + et, mb * 128:(mb + 1) * 128],
                                                      rhs=qm[:, hm * 2 + et, :], start=(et == 0), stop=(et == 1)))
                           for et in range(2)], reads=[mkT, qm], writes=[sp])
                    kb.op("act", lambda g: g.activation(out=pt[:, mb, :], in_=sp[:, :], func=ACT.Exp),
                          reads=[sp], pwrites=[pt])
                dp = dring.next()
                kb.pe([(lambda g, mb=mb: g.matmul(dp[:, :], lhsT=cx.onesb[:, :], rhs=pt[:, mb, :],
                                                  start=(mb == 0), stop=(mb == 1))) for mb in range(2)],
                      reads=[cx.onesb, pt], writes=[dp])
                rd = rdring.next()
                kb.op("dve", lambda g: g.reciprocal(out=rd[:, :], in_=dp[:, :]), reads=[dp], writes=[rd])
                for e2 in range(2):
                    op_ = oring.next()
                    kb.pe([(lambda g, mb=mb: g.matmul(op_[:, :], lhsT=mv[:, mb, hm * 256 + e2 * 128:hm * 256 + (e2 + 1) * 128],
                                                      rhs=pt[:, mb, :], start=(mb == 0), stop=(mb == 1)))
                           for mb in range(2)], reads=[mv, pt], writes=[op_])
                    t1 = t1ring.next()
                    kb.op("dve", lambda g: g.tensor_tensor(out=t1[:, :], in0=op_[:, :], in1=rd[:, :], op=ALU.mult),
                          reads=[op_, rd], writes=[t1])
                    kb.op("pool", lambda g: g.tensor_tensor(out=gm[:, hm * 2 + e2, :], in0=t1[:, :],
                                                            in1=zm[:, hm * 2 + e2, :], op=ALU.mult),
                          reads=[t1, zm], pwrites=[gm])
            kb.dma("sp", G[TOKW:MIXW, :].rearrange("(e p) t -> p e t", p=128)[:, :, t0:t0 + TBm], gm[:, :, :], gm, False,
                   regs=[rG])


def out_proj(kb, cx, T, G, wout_ap, x_in, x_out, rin, rout):
    rG = kb.region("G")
    TBo = 512
    KO = MIXW // 128
    with kb.phase() as ph:
        gring = Ring([ph.sb(f"gt{i}", [128, KO, TBo], BF16, dma=True) for i in range(2)])
        wring = Ring([ph.sb(f"wo{i}", [128, KO, 512], BF16, dma="pool") for i in range(2)])
        xo = [ph.sb(f"xo{i}", [128, D], F32, dma=True) for i in range(4)]
        pring = Ring([ph.ps(f"pso{i}", [128, 512]) for i in range(4)])
        for tb in range(T // TBo):
            t0 = tb * TBo
            gt = gring.next()
            kb.dma("sp", gt[:, :, :], G.rearrange("(k p) t -> p k t", p=128)[:, :, t0:t0 + TBo], gt, True, regs=[rG])
            for i in range(4):
                kb.dma("sp", xo[i][:, :], x_in[t0 + i * 128:t0 + (i + 1) * 128, :], xo[i], True, regs=[rin])
            for n in range(4):
                wt = load_w(kb, wring, wout_ap, n * 512, 512, kt=KO)
                for i in range(4):
                    psb = pring.next()
                    kb.pe([(lambda g, k=k: g.matmul(psb[:, :], lhsT=gt[:, k, i * 128:(i + 1) * 128], rhs=wt[:, k, :],
                                                    start=(k == 0), stop=(k == KO - 1))) for k in range(KO)],
                          reads=[gt, wt], writes=[psb])
                    kb.op("dve", lambda g: g.tensor_tensor(out=xo[i][:, n * 512:(n + 1) * 512], in0=psb[:, :],
                                                           in1=xo[i][:, n * 512:(n + 1) * 512], op=ALU.add),
                          reads=[psb, xo[i]], pwrites=[xo[i]])
            for i in range(4):
                kb.dma("sp", x_out[t0 + i * 128:t0 + (i + 1) * 128, :], xo[i][:, :], xo[i], False, regs=[rout])


def final_norm(kb, cx, T, x_in, rin, g_ap, y_out):
    with kb.phase() as ph:
        gbc = load_gain(kb, ph, g_ap, D, "fg")
        xring = Ring([ph.sb(f"fx{i}", [128, D], F32, dma=True) for i in range(3)])
        ssring = Ring([ph.sb(f"fs{i}", [128, 1], F32) for i in range(2)])
        rsring = Ring([ph.sb(f"fr{i}", [128, 1], F32) for i in range(2)])
        junk = ph.sb("fjunk", [128, D], BF16)
        ry = kb.region("Y")
        for i in range(T // 128):
            xt, ss, rs = xring.next(), ssring.next(), rsring.next()
            kb.dma("sp", xt[:, :], x_in[i * 128:(i + 1) * 128, :], xt, True, regs=[rin])
            kb.op("act", lambda g: g.activation(out=junk[:, :], in_=xt[:, :], func=ACT.Square, accum_out=ss[:, :]),
                  reads=[xt], writes=[junk, ss])
            kb.op("dve", lambda g: g.tensor_scalar(out=ss[:, :], in0=ss[:, :], scalar1=1.0 / D, scalar2=EPS,
                                                   op0=ALU.mult, op1=ALU.add), reads=[ss], writes=[ss])
            kb.op("pool", lambda g: g.tensor_tensor(out=rs[:, :], in0=ss[:, :], in1=cx.neghalf[:, :], op=ALU.pow),
                  reads=[ss, cx.neghalf], writes=[rs])
            kb.op("dve", lambda g: g.scalar_tensor_tensor(out=xt[:, :], in0=xt[:, :], scalar=rs[:, :], in1=gbc[:, :],
                                                          op0=ALU.mult, op1=ALU.mult), reads=[xt, rs, gbc], writes=[xt])
            kb.dma("sp", y_out[i * 128:(i + 1) * 128, :], xt[:, :], xt, False, regs=[ry], final=True)


def run_pipeline(tasks):
    if not tasks:
        return
    ns = max(len(t) for t in tasks)
    for step in range(len(tasks) + ns - 1):
        for k in range(ns):
            i = step - k
            if 0 <= i < len(tasks) and k < len(tasks[i]):
                tasks[i][k]()


def _xbc_task(kb, cx, j, m, n, shared, first, last, zero_hist, tok0, TB, env):
    st = {}
    ct = j * 4 + m
    hist, cwsb, cbsb = env["hist"], env["cwsb"], env["cbsb"]
    S, rS = env["S"], env["rS"]

    def s0():
        if first:
            shared["wt"] = load_w(kb, env["wring"], env["w_in"], j * 512, 512)
            shared["ofm"] = env["ofring"].next()
            shared["otm"] = env["otring"].next() if j < 8 else None
        psb = env["pring"].next()
        mm_fm(kb, psb, shared["wt"], m, env["hT"], n * 512, 512)
        U = env["uring"].next()
        st["U"] = U
        if zero_hist:
            kb.op("pool", lambda g: g.memset(U[:, 0:3], 0.0), pwrites=[U])
        else:
            kb.op("act", lambda g: g.activation(out=U[:, 0:3], in_=hist[:, ct, :], func=ACT.Copy),
                  reads=[hist], pwrites=[U])
        kb.op("act", lambda g: g.activation(out=U[:, 3:515], in_=psb[:, :], func=ACT.Copy), reads=[psb], pwrites=[U])
        kb.op("act", lambda g: g.activation(out=hist[:, ct, :], in_=U[:, 512:515], func=ACT.Copy),
              reads=[U], pwrites=[hist])

    def s1():
        U = st["U"]
        acc = env["aring"].next()
        kb.op("dve", lambda g: g.tensor_scalar(out=acc[:, :], in0=U[:, 0:512], scalar1=cwsb[:, ct, 0:1],
                                               scalar2=cbsb[:, ct:ct + 1], op0=ALU.mult, op1=ALU.add),
              reads=[U, cwsb, cbsb], writes=[acc])
        for k in range(1, 4):
            kb.op("dve", lambda g, k=k: g.scalar_tensor_tensor(
                out=acc[:, :], in0=U[:, k:k + 512], scalar=cwsb[:, ct, k:k + 1], in1=acc[:, :],
                op0=ALU.mult, op1=ALU.add), reads=[U, cwsb, acc], writes=[acc])
        osb = env["osring"].next()
        st["osb"] = osb
        kb.op("act", lambda g: g.activation(out=osb[:, :], in_=acc[:, :], func=ACT.Silu), reads=[acc], writes=[osb])

    def s2():
        osb = st["osb"]
        ofm, otm = shared["ofm"], shared["otm"]
        if j >= 6:
            kb.op("pool", lambda g: g.tensor_copy(out=ofm[:, m, n * 512:(n + 1) * 512], in_=osb[:, :]),
                  reads=[osb], pwrites=[ofm])
        if j < 8:
            pt = env["tring"].next()
            kb.pe([(lambda g, q=q: g.transpose(out=pt[:, q, :], in_=osb[:, q * 128:(q + 1) * 128],
                                               identity=cx.identb[:, :])) for q in range(4)],
                  reads=[osb, cx.identb], writes=[pt])
            kb.op("dve", lambda g: g.tensor_copy(out=otm[:, n * 4:(n + 1) * 4, m * 128:(m + 1) * 128], in_=pt[:, :, :]),
                  reads=[pt], pwrites=[otm])
        if last:
            if j < 8:
                dst = S["TOK"][tok0:tok0 + TB, j * 512:(j + 1) * 512]
                kb.dma("sp", dst.rearrange("(s p) c -> p s c", p=128), otm[:, :, :], otm, False, regs=[rS])
            if j >= 6:
                r0 = (j - 6) * 512
                kb.dma("sp", S["BCT"][r0:r0 + 512, tok0:tok0 + TB].rearrange("(m p) t -> p m t", p=128), ofm[:, :, :], ofm,
                       False, regs=[rS])

    return [s0, s1, s2]


def ssd_proj(kb, cx, T, x_in, rin, P, S):
    TB = min(1024, T)
    NT = TB // 128
    NS = TB // 512
    w_in = P["w_in"]
    rS = kb.region("SSD")
    rQ, rZ = kb.region("QM"), kb.region("ZS")
    with kb.phase() as ph:
        nb = NormBufs(ph)
        gbc = load_gain(kb, ph, P["norm_g"], D, "ng")
        hT = ph.sb("hT", [128, KT, TB], BF16)
        wring = Ring([ph.sb(f"w{i}", [128, KT, 512], BF16, dma="pool") for i in range(3)])
        pring = Ring([ph.ps(f"pp{i}", [128, 512]) for i in range(4)])
        tring = Ring([ph.ps(f"tq{i}", [128, 4, 128], BF16) for i in range(2)])
        cwsb = ph.sb("cw", [128, 40, 4], F32, dma=True)
        cbsb = ph.sb("cb", [128, 40], F32, dma=True)
        kb.dma("sp", cwsb[:, :, :], P["conv_w"], cwsb, True)
        kb.dma("sp", cbsb[:, :], P["conv_b"], cbsb, True)
        dtb = load_gain(kb, ph, P["dt_bias"], 48, "dtb")
        abc = load_gain(kb, ph, P["a_log"], 48, "abc")
        kb.op("act", lambda g: g.activation(out=abc[:, :], in_=abc[:, :], func=ACT.Exp), reads=[abc], writes=[abc])
        kb.op("dve", lambda g: g.tensor_scalar(out=abc[:, :], in0=abc[:, :], scalar1=-1.0, scalar2=None, op0=ALU.mult),
              reads=[abc], writes=[abc])
        hist = ph.sb("hist", [128, 40, 3], F32)
        uring = Ring([ph.sb(f"U{i}", [128, 515], F32) for i in range(3)])
        aring = Ring([ph.sb(f"acc{i}", [128, 512], F32) for i in range(2)])
        osring = Ring([ph.sb(f"os{i}", [128, 512], BF16) for i in range(3)])
        ofring = Ring([ph.sb(f"ofm{i}", [128, 4, TB], BF16, dma=True) for i in range(2)])
        otring = Ring([ph.sb(f"otm{i}", [128, NT, 512], BF16, dma=True) for i in range(2)])
        dta_all = ph.sb("dtaall", [128, NT, 96], F32, dma=True)
        dtx = ph.sb("dtx", [128, 48], F32)
        dta = ph.sb("dta", [128, 48], F32)
        for tb in range(T // TB):
            tok0 = tb * TB
            norm_tiles(kb, cx, ph, nb, lambda i: x_in[tok0 + i * 128:tok0 + (i + 1) * 128, :], rin, gbc, hT, NT)
            tasks = []
            for j in range(10):
                shared = {}
                for m in range(4):
                    for n in range(NS):
                        tasks.append(_xbc_task(kb, cx, j, m, n, shared, first=(m == 0 and n == 0),
                                               last=(m == 3 and n == NS - 1), zero_hist=(tb == 0 and n == 0),
                                               tok0=tok0, TB=TB, env=dict(wring=wring, ofring=ofring, otring=otring,
                                                                          pring=pring, uring=uring, aring=aring,
                                                                          osring=osring, tring=tring, hist=hist,
                                                                          cwsb=cwsb, cbsb=cbsb, hT=hT, w_in=w_in, S=S, rS=rS)))
            run_pipeline(tasks)
            wt = load_w(kb, wring, w_in, 5120, 48)
            for i in range(NT):
                psb = pring.next()
                mm_tm(kb, psb, wt, 48, hT, i * 128)
                kb.op("dve", lambda g: g.tensor_tensor(out=dtx[:, :], in0=psb[:, :48], in1=dtb[:, :], op=ALU.add),
                      reads=[psb, dtb], writes=[dtx])
                kb.op("act", lambda g: g.activation(out=dtx[:, :], in_=dtx[:, :], func=ACT.Exp), reads=[dtx], writes=[dtx])
                kb.op("act", lambda g: g.activation(out=dta_all[:, i, 0:48], in_=dtx[:, :], func=ACT.Ln, bias=1.0),
                      reads=[dtx], pwrites=[dta_all])
                kb.op("dve", lambda g: g.tensor_tensor(out=dta[:, :], in0=dta_all[:, i, 0:48], in1=abc[:, :], op=ALU.mult),
                      reads=[dta_all, abc], writes=[dta])
                p1 = pring.next()
                kb.pe([lambda g: g.matmul(p1[:, :48], lhsT=cx.cst[:, C_TRIU:C_TRIU + 128], rhs=dta[:, :],
                                          start=True, stop=True)], reads=[cx.cst, dta], writes=[p1])
                kb.op("dve", lambda g: g.tensor_copy(out=dta_all[:, i, 48:96], in_=p1[:, :48]), reads=[p1], pwrites=[dta_all])
            kb.dma("sp", S["DTA"][tok0:tok0 + TB, :].rearrange("(i p) h -> p i h", p=128), dta_all[:, :, :], dta_all, False, regs=[rS])
            for j in range(2):
                wt = load_w(kb, wring, w_in, 5168 + j * 512, 512)
                ofm = ofring.next()
                for m in range(4):
                    for n in range(NS):
                        psb = pring.next()
                        mm_fm(kb, psb, wt, m, hT, n * 512, 512)
                        copy_op(kb, evac_engine(n), ofm[:, m, n * 512:(n + 1) * 512], psb[:, :], reads=[psb], pwrites=[ofm],
                                scale=1.0 / 16.0)
                kb.dma("sp", S["QM"][j * 512:(j + 1) * 512, tok0:tok0 + TB].rearrange("(m p) t -> p m t", p=128),
                       ofm[:, :, :], ofm, False, regs=[rQ])
            for j in range(6):
                wt = load_w(kb, wring, w_in, 6192 + j * 512, 512)
                otm = otring.next()
                for i in range(NT):
                    psb = pring.next()
                    mm_tm(kb, psb, wt, 512, hT, i * 128)
                    kb.op("act", lambda g: g.activation(out=otm[:, i, :], in_=psb[:, :], func=ACT.Silu),
                          reads=[psb], pwrites=[otm])
                kb.dma("sp", S["TOK"][tok0:tok0 + TB, 4096 + j * 512:4096 + (j + 1) * 512].rearrange("(s p) c -> p s c", p=128),
                       otm[:, :, :], otm, False, regs=[rS])
            for j in range(2):
                wt = load_w(kb, wring, w_in, 9264 + j * 512, 512)
                ofm = ofring.next()
                for m in range(4):
                    for n in range(NS):
                        psb = pring.next()
                        mm_fm(kb, psb, wt, m, hT, n * 512, 512)
                        kb.op("act", lambda g: g.activation(out=ofm[:, m, n * 512:(n + 1) * 512], in_=psb[:, :], func=ACT.Silu),
                              reads=[psb], pwrites=[ofm])
                kb.dma("sp", S["ZS"][TOKW + j * 512:TOKW + (j + 1) * 512, tok0:tok0 + TB].rearrange("(m p) t -> p m t", p=128),
                       ofm[:, :, :], ofm, False, regs=[rZ])


def bc_last(ap2d, n):
    return ap2d.unsqueeze(2).broadcast_to([ap2d.shape[0], ap2d.shape[1], n])


def ssd_scan(kb, cx, T, P, S, G):
    rS, rG = kb.region("SSD"), kb.region("G")
    NCH = T // 128
    with kb.phase() as ph:
        dsk = load_gain(kb, ph, P["d_skip"], 48, "dsk")
        g2 = ph.sb("g2", [128, 24], F32, dma=True)
        kb.dma("sp", g2[:, :], P["ssd_norm_g"], g2, True)
        tokr = Ring([ph.sb(f"tok{i}", [128, 7168], BF16, dma=True) for i in range(2)])
        bcr = Ring([ph.sb(f"bct{i}", [128, 16, 128], BF16, dma=True) for i in range(2)])
        dtar = Ring([ph.sb(f"dta{i}", [128, 96], F32, dma=True) for i in range(3)])
        decr = Ring([ph.sb(f"dec{i}", [128, 48, 128], BF16) for i in range(2)])
        MTr = Ring([ph.sb(f"MT{i}", [128, 48, 128], BF16) for i in range(2)])
        xdtr = Ring([ph.sb(f"xdt{i}", [128, 48, 64], BF16) for i in range(2)])
        xdwr = Ring([ph.sb(f"xdtw{i}", [128, 48, 64], BF16) for i in range(2)])
        xsDr = Ring([ph.sb(f"xsD{i}", [128, 48, 64], BF16) for i in range(2)])
        einr = Ring([ph.sb(f"ein{i}", [128, 48], F32) for i in range(3)])
        cdr = Ring([ph.sb(f"cd{i}", [128, 48], F32) for i in range(3)])
        nacr = Ring([ph.sb(f"nacs{i}", [128, 48], F32) for i in range(2)])
        t48r = Ring([ph.sb(f"t48{i}", [128, 48], F32) for i in range(2)])
        hst = [ph.sb(f"hst{g}", [128, 6, 64], F32) for g in range(8)]
        hbf = [ph.sb(f"hbf{g}", [128, 384], BF16) for g in range(8)]
        dtw = ph.sb("dtw", [128, 48], F32)
        t1r = Ring([ph.sb(f"t1{i}", [128, 6, 64], F32) for i in range(2)])
        t2r = Ring([ph.sb(f"t2{i}", [128, 384], F32) for i in range(2)])
        gtr = Ring([ph.sb(f"gt{i}", [128, 384], F32) for i in range(2)])
        ssr = Ring([ph.sb(f"sq{i}", [128, 1], F32) for i in range(2)])
        rsr = Ring([ph.sb(f"rq{i}", [128, 1], F32) for i in range(2)])
        junk = ph.sb("sjunk", [128, 384], BF16)
        gated = ph.sb("gated", [128, TOKW], BF16)
        gfm = ph.sb("gfm", [128, 24, 128], BF16, dma=True)
        rpr = Ring([ph.ps(f"rp{i}", [128, 4, 128]) for i in range(2)])
        cbr = Ring([ph.ps(f"cbp{i}", [128, 4, 128]) for i in range(1)])
        ydr = Ring([ph.ps(f"yd{i}", [128, 6, 64]) for i in range(2)])
        ysr = Ring([ph.ps(f"ys{i}", [128, 6, 64]) for i in range(1)])
        tpr = Ring([ph.ps(f"tg{i}", [128, 4, 128], BF16) for i in range(1)])
        cdp = ph.ps("cdp", [128, 48])
        identf = cx.cst[:, C_ID:C_ID + 128]
        negmb = cx.cstb.t[:, C_NEGM:C_NEGM + 128]
        negm = cx.cst[:, C_NEGM:C_NEGM + 128]

        def make(c):
            st = {}
            t0 = c * 128

            def P1():
                dta, ein, cd, nacs, dec = dtar.next(), einr.next(), cdr.next(), nacr.next(), decr.next()
                st.update(ein=ein, cd=cd, dec=dec, dta=dta)
                kb.dma("sp", dta[:, :], S["DTA"][t0:t0 + 128, :], dta, True, regs=[rS])
                acs = dta[:, 48:96]
                kb.op("dve", lambda g: g.tensor_scalar(out=nacs[:, :], in0=acs, scalar1=-1.0, scalar2=None, op0=ALU.mult),
                      reads=[dta], writes=[nacs])
                kb.op("act", lambda g: g.activation(out=ein[:, :], in_=acs, func=ACT.Exp), reads=[dta], writes=[ein])
                for q4 in range(12):
                    rp = rpr.next()
                    fns = []
                    for a in range(4):
                        hh = q4 * 4 + a
                        fns.append(lambda g, a=a: g.matmul(rp[:, a, :], lhsT=identf, rhs=negm, start=True, stop=False))
                        fns.append(lambda g, a=a, hh=hh: g.matmul(rp[:, a, :], lhsT=dta[:, 48 + hh:49 + hh].to_broadcast([128, 128]),
                                                                  rhs=identf, start=False, stop=False))
                        fns.append(lambda g, a=a, hh=hh: g.matmul(rp[:, a, :], lhsT=identf,
                                                                  rhs=nacs[:, hh:hh + 1].to_broadcast([128, 128]),
                                                                  start=False, stop=True))
                    kb.pe(fns, reads=[cx.cst, cx.cstb, dta, nacs], writes=[rp])
                    kb.op("act", lambda g: g.activation(out=dec[:, q4 * 4:(q4 + 1) * 4, :], in_=rp[:, :, :], func=ACT.Exp),
                          reads=[rp], pwrites=[dec])
                kb.pe([lambda g: g.matmul(cdp[:, :], lhsT=cx.cst[:, C_SEL:C_SEL + 128], rhs=dta[:, 48:96], start=True, stop=True)],
                      reads=[cx.cst, dta], writes=[cdp])
                kb.op("act", lambda g: g.activation(out=cd[:, :], in_=cdp[:, :], func=ACT.Exp), reads=[cdp], writes=[cd])

            def P2():
                tok, bct = tokr.next(), bcr.next()
                MT, xdt, xdtw, xsD = MTr.next(), xdtr.next(), xdwr.next(), xsDr.next()
                dec, dta = st["dec"], st["dta"]
                st.update(tok=tok, bct=bct, MT=MT, xdt=xdt, xdtw=xdtw, xsD=xsD)
                kb.dma("sp", tok[:, :], S["TOK"][t0:t0 + 128, :], tok, True, regs=[rS])
                kb.dma("sp", bct[:, :, :], S["BCT"][:, t0:t0 + 128].rearrange("(g p) t -> p g t", p=128), bct, True, regs=[rS])
                xs = tok[:, 0:3072].rearrange("p (h e) -> p h e", e=64)
                dtc = dta[:, 0:48]
                kb.op("dve", lambda g: g.tensor_tensor(out=dtw[:, :], in0=dtc, in1=dec[:, :, 127], op=ALU.mult),
                      reads=[dta, dec], writes=[dtw])
                kb.op("pool", lambda g: g.tensor_tensor(out=xdt[:, :, :], in0=xs, in1=bc_last(dtc, 64),
                                                        op=ALU.mult), reads=[tok, dta], writes=[xdt])
                kb.op("pool", lambda g: g.tensor_tensor(out=xsD[:, :, :], in0=xs, in1=bc_last(dsk[:, :], 64),
                                                        op=ALU.mult), reads=[tok, dsk], writes=[xsD])
                kb.op("pool", lambda g: g.tensor_tensor(out=xdtw[:, :, :], in0=xs, in1=bc_last(dtw[:, :], 64),
                                                        op=ALU.mult), reads=[tok, dtw], writes=[xdtw])
                for hf in range(2):
                    cb = cbr.next()
                    kb.pe([(lambda g, q=q: g.matmul(cb[:, q, :], lhsT=bct[:, hf * 4 + q, :], rhs=bct[:, 8 + hf * 4 + q, :],
                                                    start=True, stop=True)) for q in range(4)], reads=[bct], writes=[cb])
                    kb.op("dve", lambda g: g.tensor_tensor(
                        out=MT[:, hf * 24:(hf + 1) * 24, :].rearrange("p (q j) l -> p q j l", j=6),
                        in0=dec[:, hf * 24:(hf + 1) * 24, :].rearrange("p (q j) l -> p q j l", j=6),
                        in1=cb[:, :, :].unsqueeze(2).broadcast_to([128, 4, 6, 128]), op=ALU.mult),
                        reads=[dec, cb], pwrites=[MT])

            def B():
                tok, bct, MT, xdt, xdtw, xsD, ein, cd = (st[k] for k in ("tok", "bct", "MT", "xdt", "xdtw", "xsD", "ein", "cd"))
                for gi in range(8):
                    yd = ydr.next()
                    fns = [lambda g: g.matmul(yd[:, :, :], lhsT=cx.identb[:, :], rhs=xsD[:, gi * 6:(gi + 1) * 6, :],
                                              start=True, stop=False)]
                    for hh in range(6):
                        h = gi * 6 + hh
                        fns.append(lambda g, h=h, hh=hh: g.matmul(yd[:, hh, :], lhsT=MT[:, h, :], rhs=xdt[:, h, :],
                                                                  start=False, stop=(hh == 5)))
                    kb.pe(fns, reads=[cx.identb, xsD, MT, xdt], writes=[yd])
                    gt = gtr.next()
                    ydf = yd[:, :, :].rearrange("p h e -> p (h e)")
                    if c > 0:
                        yo = ysr.next()
                        kb.pe([lambda g: g.matmul(yo[:, :, :], lhsT=bct[:, 8 + gi, :], rhs=hbf[gi][:, :], start=True, stop=True)],
                              reads=[bct, hbf[gi]], writes=[yo])
                        t1 = t1r.next()
                        kb.op("dve", lambda g: g.tensor_tensor(out=t1[:, :, :], in0=yo[:, :, :],
                                                               in1=bc_last(ein[:, gi * 6:(gi + 1) * 6], 64), op=ALU.mult),
                              reads=[yo, ein], writes=[t1])
                        t2 = t2r.next()
                        kb.op("dve", lambda g: g.tensor_tensor(out=t2[:, :], in0=ydf, in1=t1[:, :, :].rearrange("p h e -> p (h e)"),
                                                               op=ALU.add), reads=[yd, t1], writes=[t2])
                        kb.op("dve", lambda g: g.tensor_tensor(out=gt[:, :], in0=t2[:, :], in1=tok[:, 4096 + gi * 384:4096 + (gi + 1) * 384],
                                                               op=ALU.mult), reads=[t2, tok], writes=[gt])
                    else:
                        kb.op("dve", lambda g: g.tensor_tensor(out=gt[:, :], in0=ydf, in1=tok[:, 4096 + gi * 384:4096 + (gi + 1) * 384],
                                                               op=ALU.mult), reads=[yd, tok], writes=[gt])
                    ss, rs = ssr.next(), rsr.next()
                    kb.op("act", lambda g: g.activation(out=junk[:, :], in_=gt[:, :], func=ACT.Square, accum_out=ss[:, :]),
                          reads=[gt], writes=[junk, ss])
                    kb.op("dve", lambda g: g.tensor_scalar(out=ss[:, :], in0=ss[:, :], scalar1=1.0 / 384.0, scalar2=EPS,
                                                           op0=ALU.mult, op1=ALU.add), reads=[ss], writes=[ss])
                    kb.op("pool", lambda g: g.tensor_tensor(out=rs[:, :], in0=ss[:, :], in1=cx.neghalf[:, :], op=ALU.pow),
                          reads=[ss, cx.neghalf], writes=[rs])
                    if os.environ.get("KB_B1"):
                        kb.op("dve", lambda g: g.tensor_scalar(out=gated[:, gi * 384:(gi + 1) * 384], in0=gt[:, :], scalar1=rs[:, :],
                                                               scalar2=None, op0=ALU.mult), reads=[gt, rs], pwrites=[gated])
                    else:
                        kb.op("act", lambda g: g.activation(out=gated[:, gi * 384:(gi + 1) * 384], in_=gt[:, :], func=ACT.Copy,
                                                            scale=rs[:, :]), reads=[gt, rs], pwrites=[gated])
                    if c < NCH - 1:
                        sps = ysr.next()
                        kb.pe([lambda g: g.matmul(sps[:, :, :], lhsT=tok[:, 3072 + gi * 128:3072 + (gi + 1) * 128], rhs=xdtw[:, gi * 6:(gi + 1) * 6, :],
                                                  start=True, stop=True)], reads=[tok, xdtw], writes=[sps])
                        if c > 0:
                            kb.op("pool", lambda g: g.tensor_tensor(out=hst[gi][:, :, :], in0=hst[gi][:, :, :],
                                                                    in1=bc_last(cd[:, gi * 6:(gi + 1) * 6], 64), op=ALU.mult),
                                  reads=[hst[gi], cd], writes=[hst[gi]])
                            kb.op("dve", lambda g: g.tensor_tensor(out=hst[gi][:, :, :], in0=sps[:, :, :], in1=hst[gi][:, :, :],
                                                                   op=ALU.add), reads=[sps, hst[gi]], writes=[hst[gi]])
                        else:
                            kb.op("dve", lambda g: g.tensor_copy(out=hst[gi][:, :, :], in_=sps[:, :, :]), reads=[sps], writes=[hst[gi]])
                        kb.op("act", lambda g: g.activation(out=hbf[gi][:, :], in_=hst[gi][:, :, :].rearrange("p h e -> p (h e)"),
                                                            func=ACT.Copy), reads=[hst[gi]], writes=[hbf[gi]])
                for q in range(6):
                    tp = tpr.next()
                    kb.pe([(lambda g, j=j: g.transpose(out=tp[:, j, :], in_=gated[:, (4 * q + j) * 128:(4 * q + j + 1) * 128],
                                                       identity=cx.identb[:, :])) for j in range(4)],
                          reads=[gated, cx.identb], writes=[tp])
                    kb.op("dve", lambda g: g.tensor_tensor(out=gfm[:, 4 * q:4 * q + 4, :], in0=tp[:, :, :],
                                                           in1=bc_last(g2[:, 4 * q:4 * q + 4], 128), op=ALU.mult),
                          reads=[tp, g2], pwrites=[gfm])
                kb.dma("sp", G[0:TOKW, t0:t0 + 128].rearrange("(ct p) t -> p ct t", p=128), gfm[:, :, :], gfm, False, regs=[rG])

            skip = os.environ.get("KB_SKIP", "")
            return [f_ for f_, nm in ((P1, "1"), (P2, "2"), (B, "B")) if nm not in skip]

        if os.environ.get("KB_NOPIPE"):
            for c in range(NCH):
                for f_ in make(c):
                    f_()
        else:
            run_pipeline([make(c) for c in range(NCH)])


def attn_proj(kb, cx, T, x_in, rin, P, S):
    TB = min(1024, T)
    NT = TB // 128
    NS = TB // 512
    w_in = P["w_in"]
    rA, rQ, rZ = kb.region("ATT"), kb.region("QM"), kb.region("ZS")
    with kb.phase() as ph:
        nb = NormBufs(ph)
        gbc = load_gain(kb, ph, P["norm_g"], D, "ng")
        hT = ph.sb("hT", [128, KT, TB], BF16)
        wring = Ring([ph.sb(f"w{i}", [128, KT, 512], BF16, dma="pool") for i in range(3)])
        pring = Ring([ph.ps(f"pp{i}", [128, 512]) for i in range(6)])
        ofring = Ring([ph.sb(f"ofm{i}", [128, 4, TB], BF16, dma=True) for i in range(2)])
        otring = Ring([ph.sb(f"otm{i}", [128, NT, 512], BF16, dma=True) for i in range(2)])
        for gi, (_, d) in enumerate(DIL):
            nsub = T // d
            xr = x_in.rearrange("(i r) c -> r i c", r=d)
            for tb in range(T // TB):
                tok0 = tb * TB

                def rows(i, tok0=tok0, nsub=nsub, xr=xr):
                    tp = tok0 + i * 128
                    r, i0 = tp // nsub, tp % nsub
                    return xr[r, i0:i0 + 128, :]

                norm_tiles(kb, cx, ph, nb, rows, rin, gbc, hT, NT)
                base = gi * 9216
                segs = [("q", base, 6), ("k", base + 3072, 6), ("v", base + 6144, 6)]
                if gi == 0:
                    segs += [("qm", 27648, 2), ("z", 28672, 8)]
                for kind, c0, ntile in segs:
                    for j in range(ntile):
                        wt = load_w(kb, wring, w_in, c0 + j * 512, 512)
                        if kind == "v":
                            otm = otring.next()
                            for i in range(NT):
                                psb = pring.next()
                                mm_tm(kb, psb, wt, 512, hT, i * 128)
                                copy_op(kb, evac_engine(i), otm[:, i, :], psb[:, :], reads=[psb], pwrites=[otm])
                            kb.dma("sp", S["V"][gi][tok0:tok0 + TB, j * 512:(j + 1) * 512].rearrange("(s p) c -> p s c", p=128),
                                   otm[:, :, :], otm, False, regs=[rA])
                            continue
                        ofm = ofring.next()
                        for m in range(4):
                            for n in range(NS):
                                psb = pring.next()
                                mm_fm(kb, psb, wt, m, hT, n * 512, 512)
                                dst = ofm[:, m, n * 512:(n + 1) * 512]
                                if kind == "z":
                                    kb.op("act", lambda g: g.activation(out=dst, in_=psb[:, :], func=ACT.Silu),
                                          reads=[psb], pwrites=[ofm])
                                elif kind == "q":
                                    copy_op(kb, evac_engine(n + m), dst, psb[:, :], reads=[psb], pwrites=[ofm],
                                            scale=128.0 ** -0.5)
                                elif kind == "qm":
                                    copy_op(kb, evac_engine(n + m), dst, psb[:, :], reads=[psb], pwrites=[ofm], scale=1.0 / 16.0)
                                else:
                                    copy_op(kb, evac_engine(n + m), dst, psb[:, :], reads=[psb], pwrites=[ofm])
                        if kind == "q":
                            dt_, rg = S["Q"][gi], rA
                        elif kind == "k":
                            dt_, rg = S["K"][gi], rA
                        elif kind == "qm":
                            dt_, rg = S["QM"], rQ
                        else:
                            dt_, rg = S["ZS"], rZ
                        kb.dma("sp", dt_[j * 512:(j + 1) * 512, tok0:tok0 + TB].rearrange("(m p) t -> p m t", p=128),
                               ofm[:, :, :], ofm, False, regs=[rg])


def attn_core(kb, cx, T, S, G):
    rA, rZ, rG = kb.region("ATT"), kb.region("ZS"), kb.region("G")
    NB = T // 128
    slopes = alibi_slopes()
    with kb.phase() as ph:
        qr = Ring([ph.sb(f"aq{i}", [128, T], BF16, dma=True) for i in range(2)])
        kr = Ring([ph.sb(f"ak{i}", [128, T], BF16, dma=True) for i in range(2)])
        vr = Ring([ph.sb(f"av{i}", [128, NB, 128], BF16, dma=True) for i in range(2)])
        acc = ph.sb("acc", [128, T], F32)
        dacc = ph.sb("dacc", [128, T], F32)
        ebr = Ring([ph.sb(f"eb{i}", [128, 2, 128], BF16) for i in range(3)])
        pexr = Ring([ph.sb(f"pex{i}", [128, 2, 2, 128], BF16) for i in range(3)])
        ptr_ = Ring([ph.sb(f"ptt{i}", [128, 2, 2, 128], BF16) for i in range(3)])
        spr = Ring([ph.ps(f"sp{i}", [128, 2, 2, 128]) for i in range(3)])
        opr = Ring([ph.ps(f"op{i}", [128, 4, 128]) for i in range(2)])
        dpr = Ring([ph.ps(f"dp{i}", [128, 4, 128]) for i in range(2)])
        rel = cx.cst[:, C_REL:C_REL + 256]
        for j in range(24):
            for gi, (_, d) in enumerate(DIL):
                nsub = T // d
                nbs = nsub // 128
                QB = min(4, nbs)
                qT, kT, v = qr.next(), kr.next(), vr.next()
                kb.dma("sp", qT[:, :], S["Q"][gi][j * 128:(j + 1) * 128, :], qT, True, regs=[rA])
                kb.dma("sp", kT[:, :], S["K"][gi][j * 128:(j + 1) * 128, :], kT, True, regs=[rA])
                kb.dma("sp", v[:, :, :], S["V"][gi][:, j * 128:(j + 1) * 128].rearrange("(b p) e -> p b e", p=128), v, True,
                       regs=[rA])
                eb = ebr.next()
                sc = -float(slopes[gi, j]) * d
                kb.op("act", lambda g: g.activation(out=eb[:, :, :].rearrange("p a q -> p (a q)"), in_=rel, func=ACT.Exp, scale=sc),
                      reads=[cx.cst], writes=[eb])
                for r in range(d):
                    for kb0 in range(0, nbs, QB):
                        op_, dp = opr.next(), dpr.next()
                        pts = []
                        for pair in range(0, QB, 2):
                            npair = min(2, QB - pair)
                            sp, pex, pt = spr.next(), pexr.next(), ptr_.next()
                            fns = []
                            for a in range(npair):
                                kbi = kb0 + pair + a
                                bb = r * nbs + kbi
                                if kbi > 0:
                                    fns.append(lambda g, a=a, bb=bb: g.matmul(sp[:, a, 0, :], lhsT=kT[:, (bb - 1) * 128:bb * 128],
                                                                              rhs=qT[:, bb * 128:(bb + 1) * 128], start=True, stop=True))
                                fns.append(lambda g, a=a, bb=bb: g.matmul(sp[:, a, 1, :], lhsT=kT[:, bb * 128:(bb + 1) * 128],
                                                                          rhs=qT[:, bb * 128:(bb + 1) * 128], start=True, stop=True))
                            kb.pe(fns, reads=[kT, qT], writes=[sp])
                            first = (kb0 + pair == 0)
                            if first:
                                kb.op("pool", lambda g: g.memset(pt[:, 0, 0, :], 0.0), pwrites=[pt])
                                kb.op("act", lambda g: g.activation(out=pex[:, 0, 1, :], in_=sp[:, 0, 1, :], func=ACT.Exp),
                                      reads=[sp], pwrites=[pex])
                                kb.op("dve", lambda g: g.tensor_tensor(out=pt[:, 0, 1, :], in0=pex[:, 0, 1, :], in1=eb[:, 1, :],
                                                                       op=ALU.mult), reads=[pex, eb], pwrites=[pt])
                                if npair > 1:
                                    kb.op("act", lambda g: g.activation(out=pex[:, 1, :, :], in_=sp[:, 1, :, :], func=ACT.Exp),
                                          reads=[sp], pwrites=[pex])
                                    kb.op("dve", lambda g: g.tensor_tensor(out=pt[:, 1, :, :], in0=pex[:, 1, :, :], in1=eb[:, :, :],
                                                                           op=ALU.mult), reads=[pex, eb], pwrites=[pt])
                            else:
                                kb.op("act", lambda g: g.activation(out=pex[:, :npair, :, :], in_=sp[:, :npair, :, :], func=ACT.Exp),
                                      reads=[sp], pwrites=[pex])
                                kb.op("dve", lambda g: g.tensor_tensor(
                                    out=pt[:, :npair, :, :], in0=pex[:, :npair, :, :],
                                    in1=eb[:, :, :].unsqueeze(1).broadcast_to([128, npair, 2, 128]), op=ALU.mult),
                                    reads=[pex, eb], pwrites=[pt])
                            pts.append((pt, pair, npair))
                        fo, fd = [], []
                        for pt, pair, npair in pts:
                            for a in range(npair):
                                kbi = kb0 + pair + a
                                bb = r * nbs + kbi
                                qi = pair + a
                                lo = 0 if kbi > 0 else 1
                                for part in range(lo, 2):
                                    vb = bb - 1 + part
                                    fo.append(lambda g, pt=pt, a=a, part=part, vb=vb, qi=qi, lo=lo: g.matmul(
                                        op_[:, qi, :], lhsT=v[:, vb, :], rhs=pt[:, a, part, :], start=(part == lo), stop=(part == 1)))
                                    fd.append(lambda g, pt=pt, a=a, part=part, qi=qi, lo=lo: g.matmul(
                                        dp[:, qi, :], lhsT=cx.onesb[:, :], rhs=pt[:, a, part, :], start=(part == lo), stop=(part == 1)))
                        rd = [p[0] for p in pts]
                        kb.pe(fo, reads=[v] + rd, writes=[op_])
                        kb.pe(fd, reads=[cx.onesb] + rd, writes=[dp])
                        tstart = kb0 * 128 * d + r
                        n_q = QB * 128
                        if d == 1:
                            a_out = acc[:, tstart:tstart + n_q]
                            d_out = dacc[:, tstart:tstart + n_q]
                        else:
                            a_out = acc[:, :].rearrange("p (q s) -> p q s", s=d)[:, kb0 * 128:kb0 * 128 + n_q, r]
                            d_out = dacc[:, :].rearrange("p (q s) -> p q s", s=d)[:, kb0 * 128:kb0 * 128 + n_q, r]
                        o_in = op_[:, :QB, :].rearrange("p a q -> p (a q)")
                        d_in = dp[:, :QB, :].rearrange("p a q -> p (a q)")
                        if gi == 0:
                            kb.op("act", lambda g: g.activation(out=a_out, in_=o_in, func=ACT.Copy), reads=[op_], pwrites=[acc])
                            kb.op("act", lambda g: g.activation(out=d_out, in_=d_in, func=ACT.Copy), reads=[dp], pwrites=[dacc])
                        else:
                            kb.op("dve", lambda g: g.tensor_tensor(out=a_out, in0=o_in, in1=a_out, op=ALU.add),
                                  reads=[op_, acc], pwrites=[acc])
                            kb.op("dve", lambda g: g.tensor_tensor(out=d_out, in0=d_in, in1=d_out, op=ALU.add),
                                  reads=[dp, dacc], pwrites=[dacc])
            zs = qr.next()
            kb.dma("sp", zs[:, :], S["ZS"][j * 128:(j + 1) * 128, :], zs, True, regs=[rZ])
            kb.op("dve", lambda g: g.reciprocal(out=dacc[:, :], in_=dacc[:, :]), reads=[dacc], writes=[dacc])
            kb.op("dve", lambda g: g.tensor_tensor(out=acc[:, :], in0=acc[:, :], in1=dacc[:, :], op=ALU.mult),
                  reads=[acc, dacc], writes=[acc])
            kb.op("pool", lambda g: g.tensor_tensor(out=zs[:, :], in0=acc[:, :], in1=zs[:, :], op=ALU.mult),
                  reads=[acc, zs], writes=[zs])
            kb.dma("pool", G[j * 128:(j + 1) * 128, :], zs[:, :], zs, False, regs=[rG])


SSD_KEYS = ("norm_g", "w_in", "conv_w", "conv_b", "dt_bias", "a_log", "d_skip", "ssd_norm_g", "w_mem_kv", "w_out")
ATT_KEYS = ("norm_g", "w_in", "w_mem_kv", "w_out")
SSD_SHAPES = {"norm_g": [D], "w_in": [D, SSD_IN], "conv_w": [128, 40, 4], "conv_b": [128, 40], "dt_bias": [48],
              "a_log": [48], "d_skip": [48], "ssd_norm_g": [128, 24], "w_mem_kv": [D, 2048], "w_out": [MIXW, D]}
ATT_SHAPES = {"norm_g": [D], "w_in": [D, ATT_IN], "w_mem_kv": [D, 2048], "w_out": [MIXW, D]}


def build(T, layer_ids, with_final=True):
    nc = bass.Bass("TRN2", target_bir_lowering=False)

    def din(name, shape):
        return nc.dram_tensor(name, list(shape), F32, kind="ExternalInput").ap()

    def scr(name, shape, dt=BF16):
        return nc.dram_tensor(name, list(shape), dt, kind="Internal").ap()

    x_ext = din("x", [T, D])
    mem = din("mem", [N_MEM, D])
    consts = din("consts", [128, NCONST])
    mem_g = din("mem_norm_g", [D])
    fin_g = din("final_norm_g", [D]) if with_final else None
    params = {}
    for li in layer_ids:
        shapes = SSD_SHAPES if li % 2 == 0 else ATT_SHAPES
        params[li] = {k: din(f"{k}_{li}", shp) for k, shp in shapes.items()}
    y_ext = nc.dram_tensor("y", [T, D], F32, kind="ExternalOutput").ap()
    xs_ = [scr("xa", [T, D], F32), scr("xb", [T, D], F32)]
    S = {"QM": scr("QM", [MEMW, T]), "ZS": scr("ZS", [MIXW, T])}
    G = scr("G", [MIXW, T])
    if any(li % 2 == 0 for li in layer_ids):
        S.update({"TOK": scr("TOK", [T, 7168]), "BCT": scr("BCT", [2048, T]), "DTA": scr("DTA", [T, 96], F32)})
    if any(li % 2 == 1 for li in layer_ids):
        S.update({"Q": [scr(f"Q{g}", [TOKW, T]) for g in range(3)], "K": [scr(f"K{g}", [TOKW, T]) for g in range(3)],
                  "V": [scr(f"V{g}", [T, TOKW]) for g in range(3)]})

    kb = KB(nc)
    cx = Ctx()
    with kb.es:
        with kb.phase() as top:
            cx.cst = top.sb("cst", [128, NCONST], F32, dma=True)
            cx.cstb = top.sb("cstb", [128, NCONST], BF16, dma="pool")
            kb.dma("sp", cx.cst[:, :], consts, cx.cst, True)
            kb.dma("pool", cx.cstb[:, :], consts, cx.cstb, True)
            cx.identb = Buf(cx.cstb.t, "identb")
            cx.onesb = Buf(cx.cstb.t, "onesb")
            cx.identb = _View(cx.cstb, C_ID, 128)
            cx.onesb = _View(cx.cstb, C_ONES, 128)
            cx.neghalf = top.sb("neghalf", [128, 1], F32)
            kb.op("pool", lambda g: g.memset(cx.neghalf[:, :], -0.5), writes=[cx.neghalf])
            cx.memnT = top.sb("memnT", [128, KT, N_MEM], BF16)
            mem_prep(kb, cx, top, mem, mem_g)
            cur, rcur = x_ext, kb.region("xext")
            for n_, li in enumerate(layer_ids):
                P = params[li]
                nxt = xs_[n_ % 2]
                rnxt = kb.region(f"x{n_ % 2}")
                with kb.phase() as lp:
                    mkT = lp.sb("mkT", [128, 8, N_MEM], BF16)
                    mv = lp.sb("mv", [128, 2, MEMW], BF16)
                    mem_kv(kb, cx, lp, P["w_mem_kv"], mkT, mv)
                    if li % 2 == 0:
                        ssd_proj(kb, cx, T, cur, rcur, P, S)
                        ssd_scan(kb, cx, T, P, S, G)
                    else:
                        attn_proj(kb, cx, T, cur, rcur, P, S)
                        attn_core(kb, cx, T, S, G)
                    mem_attention(kb, cx, T, S["QM"], S["ZS"][TOKW:MIXW, :], G, mkT, mv)
                    if not with_final and n_ == len(layer_ids) - 1:
                        out_proj(kb, cx, T, G, P["w_out"], cur, y_ext, rcur, kb.region("Y"))
                    else:
                        out_proj(kb, cx, T, G, P["w_out"], cur, nxt, rcur, rnxt)
                cur, rcur = nxt, rnxt
            if with_final:
                final_norm(kb, cx, T, cur, rcur, fin_g, y_ext)
            kb.finish()
    return nc


class _View:
    def __init__(self, base, c0, n):
        self.base = base
        self.c0 = c0
        self.n = n

    @property
    def w(self):
        return self.base.w

    @w.setter
    def w(self, v):
        self.base.w = v

    @property
    def r(self):
        return self.base.r

    @r.setter
    def r(self, v):
        self.base.r = v

    def __getitem__(self, k):
        assert isinstance(k, tuple) and len(k) == 2
        cs = k[1]
        assert cs == slice(None)
        return self.base.t[k[0], self.c0:self.c0 + self.n]


def _layer_inputs(inputs, li):
    d = {}
    keys = SSD_KEYS if li % 2 == 0 else ATT_KEYS
    for k in keys:
        a = np.ascontiguousarray(np.asarray(inputs[f"{k}_{li}"], dtype=np.float32))
        if k == "conv_w":
            a = np.ascontiguousarray(a.T.reshape(40, 128, 4).transpose(1, 0, 2))
        elif k == "conv_b":
            a = np.ascontiguousarray(a.reshape(40, 128).T)
        elif k == "ssd_norm_g":
            a = np.ascontiguousarray(a.reshape(24, 128).T)
        d[f"{k}_{li}"] = a
    return d


_SELFWAIT = bool(int(os.environ.get('KB_SELFWAIT', '0')))
_NC_CACHE = {}
_RUN_KW = {}
_LAST = {}


def run_layers(x, mem, inputs, layer_ids, with_final, n_cores=None):
    B, T, _ = x.shape
    key = (T, tuple(layer_ids), with_final)
    if key not in _NC_CACHE:
        _NC_CACHE[key] = build(T, list(layer_ids), with_final)
    nc = _NC_CACHE[key]
    shared = {"consts": make_consts(), "mem_norm_g": np.asarray(inputs["mem_norm_g"], np.float32)}
    if with_final:
        shared["final_norm_g"] = np.asarray(inputs["final_norm_g"], np.float32)
    for li in layer_ids:
        shared.update(_layer_inputs(inputs, li))
    in_maps = []
    for b in range(B):
        m = dict(shared)
        m["x"] = np.ascontiguousarray(x[b], dtype=np.float32)
        m["mem"] = np.ascontiguousarray(mem[b], dtype=np.float32)
        in_maps.append(m)
    res = run_bass_kernel_spmd(nc, in_maps, core_ids=list(range(B)), **_RUN_KW)
    _LAST["exec_ns"] = getattr(res, "exec_time_ns", None)
    return np.stack([np.asarray(r["y"]) for r in res.results], axis=0)


ALL_INPUTS = (
    "x", "mem", "mem_norm_g", "final_norm_g",
    "norm_g_0", "w_in_0", "conv_w_0", "conv_b_0", "dt_bias_0", "a_log_0", "d_skip_0", "ssd_norm_g_0", "w_mem_kv_0", "w_out_0",
    "norm_g_1", "w_in_1", "w_mem_kv_1", "w_out_1",
    "norm_g_2", "w_in_2", "conv_w_2", "conv_b_2", "dt_bias_2", "a_log_2", "d_skip_2", "ssd_norm_g_2", "w_mem_kv_2", "w_out_2",
    "norm_g_3", "w_in_3", "w_mem_kv_3", "w_out_3",
)


def kernel(**inputs):
    inputs = {k: inputs[k] for k in ALL_INPUTS}
    x = np.asarray(inputs["x"], np.float32)
    mem = np.asarray(inputs["mem"], np.float32)
    return run_layers(x, mem, inputs, [0, 1, 2, 3], True).astype(np.float32)
```

```python
import contextlib
import math
import os
import numpy as np
import ml_dtypes
import concourse.bass as bass
import concourse.mybir as mybir
from concourse.bass_utils import run_bass_kernel_spmd

F32 = mybir.dt.float32
BF16 = mybir.dt.bfloat16
ACT = mybir.ActivationFunctionType
ALU = mybir.AluOpType

D = 2048
KT = D // 128
N_MEM = 256
MIXW = 4096
TOKW = 3072
MEMW = 1024
SSD_IN = 10288
ATT_IN = 32768
EPS = 1e-6
NEG = -30000.0
DIL = ((128, 1), (512, 4), (2048, 16))


class Sem:
    def __init__(self, h, idx):
        self.h = h
        self.idx = idx
        self.count = 0


class Eng:
    def __init__(self, name, eng, sem):
        self.name = name
        self.eng = eng
        self.sem = sem
        self.known = {}


class Buf:
    def __init__(self, t, name, dsem=None):
        self.t = t
        self.name = name
        self.w = {}
        self.r = {}
        self.dsem = dsem

    def __getitem__(self, k):
        return self.t[k]


class KB:
    def __init__(self, nc, n_dma_sems=90):
        self.nc = nc
        self.es = contextlib.ExitStack()
        self.sems = []
        self.engs = {}
        for name, eng in (("pe", nc.tensor), ("act", nc.scalar), ("dve", nc.vector),
                          ("pool", nc.gpsimd), ("sp", nc.sync)):
            s = self._new_sem("e_" + name)
            self.engs[name] = Eng(name, eng, s)
        self.dma_pool = [self._new_sem(f"d{i}") for i in range(60 if os.environ.get("KB_SIM") else n_dma_sems)]
        self.dma_free = list(self.dma_pool)
        self.regions = {}
        self.out_tokens = []
        self.uid = 0

    def _new_sem(self, name):
        h = self.es.enter_context(self.nc.semaphore(name))
        s = Sem(h, len(self.sems))
        self.sems.append(s)
        return s

    def phase(self):
        return Phase(self)

    def region(self, name):
        if name not in self.regions:
            self.regions[name] = Buf(None, name)
        return self.regions[name]

    def _need(self, E, idx, val):
        if val <= 0:
            return
        if idx == E.sem.idx and (E.name in ("pe", "sp") or not _SELFWAIT):
            return
        if E.known.get(idx, 0) >= val:
            return
        E.eng.wait_ge(self.sems[idx].h, val)
        E.known[idx] = val

    def _deps(self, E, reads, writes):
        for b in reads:
            for i, v in b.w.items():
                self._need(E, i, v)
        for b in writes:
            for i, v in b.w.items():
                self._need(E, i, v)
            for i, v in b.r.items():
                self._need(E, i, v)

    def _mark(self, idx, val, reads, writes):
        for b in reads:
            if b.r.get(idx, 0) < val:
                b.r[idx] = val
        for b in writes:
            b.w = {idx: val}
            b.r = {}

    def _pdeps(self, E, pwrites):
        for b in pwrites:
            for i, v in b.r.items():
                self._need(E, i, v)

    def _pmark(self, idx, val, pwrites):
        for b in pwrites:
            if b.w.get(idx, 0) < val:
                b.w[idx] = val

    def op(self, e, fn, reads=(), writes=(), pwrites=()):
        E = self.engs[e]
        self._deps(E, reads, writes)
        self._pdeps(E, pwrites)
        ins = fn(E.eng)
        E.sem.count += 1
        ins.then_inc(E.sem.h, 1)
        self._mark(E.sem.idx, E.sem.count, reads, writes)
        self._pmark(E.sem.idx, E.sem.count, pwrites)

    def pe(self, fns, reads=(), writes=(), pwrites=()):
        E = self.engs["pe"]
        self._deps(E, reads, writes)
        self._pdeps(E, pwrites)
        ins = None
        for f in fns:
            ins = f(E.eng)
        E.sem.count += 1
        ins.then_inc(E.sem.h, 1)
        self._mark(E.sem.idx, E.sem.count, reads, writes)
        self._pmark(E.sem.idx, E.sem.count, pwrites)

    def dma(self, q, out, in_, sb, load, regs=(), final=False):
        E = self.engs[q]
        if load:
            for b in regs:
                for i, v in b.w.items():
                    self._need(E, i, v)
            for i, v in sb.w.items():
                if i != sb.dsem.idx:
                    self._need(E, i, v)
            for i, v in sb.r.items():
                self._need(E, i, v)
        else:
            self._deps(E, [sb], ())
            for b in regs:
                for i, v in b.r.items():
                    self._need(E, i, v)
        s = sb.dsem
        E.eng.dma_start(out=out, in_=in_).then_inc(s.h, 16)
        s.count += 16
        if load:
            for b in regs:
                if b.r.get(s.idx, 0) < s.count:
                    b.r[s.idx] = s.count
            sb.w[s.idx] = s.count
        else:
            if sb.r.get(s.idx, 0) < s.count:
                sb.r[s.idx] = s.count
            for rg in regs:
                rg.w[s.idx] = s.count
            if final:
                self.out_tokens.append((s.idx, s.count))

    def barrier(self):
        names = ["pe", "act", "dve", "pool", "sp"]
        for e in names:
            E = self.engs[e]
            for x in names:
                if x != e:
                    X = self.engs[x]
                    self._need(E, X.sem.idx, X.sem.count)
            for s in self.dma_pool:
                self._need(E, s.idx, s.count)

    def finish(self):
        E = self.engs["sp"]
        for i, v in self.out_tokens:
            self._need(E, i, v)
        self.barrier()


class Phase:
    def __init__(self, kb):
        self.kb = kb
        self.es = contextlib.ExitStack()
        self.taken = []

    def __enter__(self):
        self.es.__enter__()
        return self

    def __exit__(self, *a):
        self.kb.barrier()
        self.kb.dma_free = self.taken + self.kb.dma_free
        return self.es.__exit__(*a)

    def sb(self, name, shape, dt, dma=False):
        self.kb.uid += 1
        t = self.es.enter_context(self.kb.nc.sbuf_tensor(f"{name}_{self.kb.uid}", list(shape), dt))
        ds = None
        if dma == "pool" and os.environ.get("KB_SIM"):
            ds = self.kb._new_sem(f"pd{self.kb.uid}")
        elif dma:
            ds = self.kb.dma_free.pop(0)
            self.taken.append(ds)
        return Buf(t, name, ds)

    def ps(self, name, shape, dt=F32):
        self.kb.uid += 1
        t = self.es.enter_context(self.kb.nc.psum_tensor(f"{name}_{self.kb.uid}", list(shape), dt))
        return Buf(t, name)


class Ring:
    def __init__(self, bufs):
        self.bufs = bufs
        self.i = 0

    def next(self):
        b = self.bufs[self.i % len(self.bufs)]
        self.i += 1
        return b


def bcast_rows(ap1d, n, parts=128):
    return bass.AP(ap1d.tensor, ap1d.offset, [[0, parts], [1, n]])


C_ID, C_TRIU, C_NEGM, C_ONES, C_M01, C_REL, C_SEL = 0, 128, 256, 384, 512, 640, 896
NCONST = 1024


def make_consts():
    c = np.zeros((128, NCONST), np.float32)
    s = np.arange(128)[:, None]
    l = np.arange(128)[None, :]
    c[:, C_ID:C_ID + 128] = (s == l)
    c[:, C_TRIU:C_TRIU + 128] = (s <= l)
    c[:, C_NEGM:C_NEGM + 128] = np.where(l >= s, 0.0, NEG)
    c[:, C_ONES:C_ONES + 128] = 1.0
    c[:, C_M01:C_M01 + 128] = (l >= s)
    BIG = 1.0e4
    rel_prev = 128 + l - s
    rel_diag = l - s
    c[:, C_REL:C_REL + 128] = np.where(rel_prev <= 128, rel_prev, BIG)
    c[:, C_REL + 128:C_REL + 256] = np.where(rel_diag >= 0, rel_diag, BIG)
    c[127, C_SEL:C_SEL + 128] = 1.0
    return c


def alibi_slopes():
    j = np.arange(1, 73, dtype=np.float64)
    return np.exp2(-8.0 * j / 72.0).reshape(3, 24)


class Ctx:
    pass


def evac_engine(i):
    return "act" if i % 2 == 0 else "dve"


def copy_op(kb, e, out, in_, reads, writes=(), pwrites=(), scale=None):
    if e == "act":
        if scale is None:
            kb.op("act", lambda g: g.activation(out=out, in_=in_, func=ACT.Copy), reads, writes, pwrites)
        else:
            kb.op("act", lambda g: g.activation(out=out, in_=in_, func=ACT.Copy, scale=float(scale)),
                  reads, writes, pwrites)
    else:
        if scale is None:
            kb.op(e, lambda g: g.tensor_copy(out=out, in_=in_), reads, writes, pwrites)
        else:
            kb.op(e, lambda g: g.tensor_scalar(out=out, in0=in_, scalar1=float(scale), scalar2=None,
                                                op0=ALU.mult), reads, writes, pwrites)


def norm_tiles(kb, cx, ph, nb, x_rows_fn, xreg, gbc, hT, ntiles, t_off=0):
    for i in range(ntiles):
        xt = nb.xring.next()
        kb.dma("sp", xt[:, :], x_rows_fn(i), xt, True, regs=[xreg])
        ss = nb.ssring.next()
        rs = nb.rsring.next()
        junk = nb.junk
        kb.op("act", lambda g: g.activation(out=junk[:, :], in_=xt[:, :], func=ACT.Square, accum_out=ss[:, :]),
              reads=[xt], writes=[junk, ss])
        kb.op("dve", lambda g: g.tensor_scalar(out=ss[:, :], in0=ss[:, :], scalar1=1.0 / D, scalar2=EPS,
                                               op0=ALU.mult, op1=ALU.add), reads=[ss], writes=[ss])
        kb.op("pool", lambda g: g.tensor_tensor(out=rs[:, :], in0=ss[:, :], in1=cx.neghalf[:, :], op=ALU.pow),
              reads=[ss, cx.neghalf], writes=[rs])
        hb = nb.hbring.next()
        kb.op("dve", lambda g: g.scalar_tensor_tensor(out=hb[:, :], in0=xt[:, :], scalar=rs[:, :], in1=gbc[:, :],
                                                      op0=ALU.mult, op1=ALU.mult), reads=[xt, rs, gbc], writes=[hb])
        for q in range(4):
            pt = nb.tring.next()
            kb.pe([(lambda g, k=k: g.transpose(out=pt[:, k % 4, :], in_=hb[:, k * 128:(k + 1) * 128],
                                               identity=cx.identb[:, :])) for k in range(4 * q, 4 * q + 4)],
                  reads=[hb, cx.identb], writes=[pt])
            copy_op(kb, evac_engine(i), hT[:, 4 * q:4 * q + 4, t_off + i * 128:t_off + (i + 1) * 128], pt[:, :, :],
                    reads=[pt], pwrites=[hT])


class NormBufs:
    def __init__(self, ph):
        self.xring = Ring([ph.sb(f"xt{i}", [128, D], F32, dma=True) for i in range(2)])
        self.ssring = Ring([ph.sb(f"ss{i}", [128, 1], F32) for i in range(2)])
        self.rsring = Ring([ph.sb(f"rs{i}", [128, 1], F32) for i in range(2)])
        self.hbring = Ring([ph.sb(f"hb{i}", [128, D], BF16) for i in range(2)])
        self.junk = ph.sb("junk", [128, D], BF16)
        self.tring = Ring([ph.ps(f"tp{i}", [128, 4, 128], BF16) for i in range(2)])


def load_gain(kb, ph, g_ap, n, name):
    gb = ph.sb(name, [128, n], F32, dma=True)
    kb.dma("sp", gb[:, :], bcast_rows(g_ap, n), gb, True)
    return gb


def load_w(kb, wring, w_ap, c0, cw, kt=KT):
    wt = wring.next()
    src = w_ap.rearrange("(k p) c -> p k c", p=128)[:, :, c0:c0 + cw]
    kb.dma("pool", wt[:, :kt, :cw], src, wt, True)
    return wt


def mm_fm(kb, psb, wt, m, hT, n0, nw, kt=KT):
    kb.pe([(lambda g, k=k: g.matmul(psb[:, :nw], lhsT=wt[:, k, m * 128:(m + 1) * 128], rhs=hT[:, k, n0:n0 + nw],
                                    start=(k == 0), stop=(k == kt - 1))) for k in range(kt)],
          reads=[wt, hT], writes=[psb])


def mm_tm(kb, psb, wt, cw, hT, t0, kt=KT):
    kb.pe([(lambda g, k=k: g.matmul(psb[:, :cw], lhsT=hT[:, k, t0:t0 + 128], rhs=wt[:, k, :cw],
                                    start=(k == 0), stop=(k == kt - 1))) for k in range(kt)],
          reads=[wt, hT], writes=[psb])


def mem_prep(kb, cx, ph, mem_ap, g_ap):
    with kb.phase() as p2:
        nb = NormBufs(p2)
        gbc = load_gain(kb, p2, g_ap, D, "memg")
        norm_tiles(kb, cx, p2, nb, lambda i: mem_ap[i * 128:(i + 1) * 128, :], kb.region("mem"), gbc, cx.memnT, 2)


def mem_kv(kb, cx, ph, wkv_ap, mkT, mv):
    with kb.phase() as p2:
        wring = Ring([p2.sb(f"wkv{i}", [128, KT, 512], BF16, dma="pool") for i in range(2)])
        pring = Ring([p2.ps(f"pkv{i}", [128, 512]) for i in range(2)])
        for j in range(2):
            wt = load_w(kb, wring, wkv_ap, j * 512, 512)
            for m in range(4):
                psb = pring.next()
                mm_fm(kb, psb, wt, m, cx.memnT, 0, 256)
                copy_op(kb, evac_engine(m), mkT[:, j * 4 + m, :], psb[:, :256], reads=[psb], pwrites=[mkT])
        for j in range(2):
            wt = load_w(kb, wring, wkv_ap, 1024 + j * 512, 512)
            for i in range(2):
                psb = pring.next()
                mm_tm(kb, psb, wt, 512, cx.memnT, i * 128)
                copy_op(kb, evac_engine(i), mv[:, i, j * 512:(j + 1) * 512], psb[:, :], reads=[psb], pwrites=[mv])


def mem_attention(kb, cx, T, QM, ZM, G, mkT, mv):
    rQ, rZ, rG = kb.region("QM"), kb.region("ZS"), kb.region("G")
    TBm = 512
    with kb.phase() as ph:
        qring = Ring([ph.sb(f"qm{i}", [128, 8, TBm], BF16, dma=True) for i in range(2)])
        zring = Ring([ph.sb(f"zm{i}", [128, 8, TBm], BF16, dma=True) for i in range(2)])
        gring = Ring([ph.sb(f"gm{i}", [128, 8, TBm], BF16, dma=True) for i in range(2)])
        ptring = Ring([ph.sb(f"pt{i}", [128, 2, TBm], BF16) for i in range(2)])
        rdring = Ring([ph.sb(f"rd{i}", [128, TBm], F32) for i in range(2)])
        t1ring = Ring([ph.sb(f"t1{i}", [128, TBm], F32) for i in range(2)])
        sring = Ring([ph.ps(f"sps{i}", [128, TBm]) for i in range(3)])
        dring = Ring([ph.ps(f"dps{i}", [128, TBm]) for i in range(2)])
        oring = Ring([ph.ps(f"ops{i}", [128, TBm]) for i in range(3)])
        for tb in range(T // TBm):
            t0 = tb * TBm
            qm, zm, gm = qring.next(), zring.next(), gring.next()
            kb.dma("sp", qm[:, :, :], QM.rearrange("(e p) t -> p e t", p=128)[:, :, t0:t0 + TBm], qm, True, regs=[rQ])
            kb.dma("sp", zm[:, :, :], ZM.rearrange("(e p) t -> p e t", p=128)[:, :, t0:t0 + TBm], zm, True, regs=[rZ])
            for hm in range(4):
                pt = ptring.next()
                for mb in range(2):
                    sp = sring.next()
                    kb.pe([(lambda g, et=et: g.matmul(sp[:, :], lhsT=mkT[:, hm * 2 + et, mb * 128:(mb + 1) * 128],
                                                      rhs=qm[:, hm * 2 + et, :], start=(et == 0), stop=(et == 1)))
                           for et in range(2)], reads=[mkT, qm], writes=[sp])
                    kb.op("act", lambda g: g.activation(out=pt[:, mb, :], in_=sp[:, :], func=ACT.Exp),
                          reads=[sp], pwrites=[pt])
                dp = dring.next()
                kb.pe([(lambda g, mb=mb: g.matmul(dp[:, :], lhsT=cx.onesb[:, :], rhs=pt[:, mb, :],
                                                  start=(mb == 0), stop=(mb == 1))) for mb in range(2)],
                      reads=[cx.onesb, pt], writes=[dp])
                rd = rdring.next()
                kb.op("dve", lambda g: g.reciprocal(out=rd[:, :], in_=dp[:, :]), reads=[dp], writes=[rd])
                for e2 in range(2):
                    op_ = oring.next()
                    kb.pe([(lambda g, mb=mb: g.matmul(op_[:, :], lhsT=mv[:, mb, hm * 256 + e2 * 128:hm * 256 + (e2 + 1) * 128],
                                                      rhs=pt[:, mb, :], start=(mb == 0), stop=(mb == 1)))
                           for mb in range(2)], reads=[mv, pt], writes=[op_])
                    t1 = t1ring.next()
                    kb.op("dve", lambda g: g.tensor_tensor(out=t1[:, :], in0=op_[:, :], in1=rd[:, :], op=ALU.mult),
                          reads=[op_, rd], writes=[t1])
                    kb.op("pool", lambda g: g.tensor_tensor(out=gm[:, hm * 2 + e2, :], in0=t1[:, :],
                                                            in1=zm[:, hm * 2 + e2, :], op=ALU.mult),
                          reads=[t1, zm], pwrites=[gm])
            kb.dma("sp", G[TOKW:MIXW, :].rearrange("(e p) t -> p e t", p=128)[:, :, t0:t0 + TBm], gm[:, :, :], gm, False,
                   regs=[rG])


def out_proj(kb, cx, T, G, wout_ap, x_in, x_out, rin, rout):
    rG = kb.region("G")
    TBo = min(1024, T)
    NTo = TBo // 128
    KO = MIXW // 128
    with kb.phase() as ph:
        gt = ph.sb("gt", [128, KO, TBo], BF16, dma=True)
        wring = Ring([ph.sb(f"wo{i}", [128, KO, 512], BF16, dma="pool") for i in range(2)])
        xring = Ring([ph.sb(f"xo{i}", [128, 512], F32, dma=True) for i in range(6)])
        pring = Ring([ph.ps(f"pso{i}", [128, 512]) for i in range(4)])
        for tb in range(T // TBo):
            t0 = tb * TBo
            kb.dma("sp", gt[:, :, :], G.rearrange("(k p) t -> p k t", p=128)[:, :, t0:t0 + TBo], gt, True, regs=[rG])
            for n in range(4):
                wt = load_w(kb, wring, wout_ap, n * 512, 512, kt=KO)
                for i in range(NTo):
                    xp = xring.next()
                    rows = slice(t0 + i * 128, t0 + (i + 1) * 128)
                    kb.dma("sp", xp[:, :], x_in[rows, n * 512:(n + 1) * 512], xp, True, regs=[rin])
                    psb = pring.next()
                    kb.pe([(lambda g, k=k: g.matmul(psb[:, :], lhsT=gt[:, k, i * 128:(i + 1) * 128], rhs=wt[:, k, :],
                                                    start=(k == 0), stop=(k == KO - 1))) for k in range(KO)],
                          reads=[gt, wt], writes=[psb])
                    kb.op("dve", lambda g: g.tensor_tensor(out=xp[:, :], in0=psb[:, :], in1=xp[:, :], op=ALU.add),
                          reads=[psb, xp], writes=[xp])
                    kb.dma("sp", x_out[rows, n * 512:(n + 1) * 512], xp[:, :], xp, False, regs=[rout])


def final_norm(kb, cx, T, x_in, rin, g_ap, y_out):
    with kb.phase() as ph:
        gbc = load_gain(kb, ph, g_ap, D, "fg")
        xring = Ring([ph.sb(f"fx{i}", [128, D], F32, dma=True) for i in range(3)])
        ssring = Ring([ph.sb(f"fs{i}", [128, 1], F32) for i in range(2)])
        rsring = Ring([ph.sb(f"fr{i}", [128, 1], F32) for i in range(2)])
        junk = ph.sb("fjunk", [128, D], BF16)
        ry = kb.region("Y")
        for i in range(T // 128):
            xt, ss, rs = xring.next(), ssring.next(), rsring.next()
            kb.dma("sp", xt[:, :], x_in[i * 128:(i + 1) * 128, :], xt, True, regs=[rin])
            kb.op("act", lambda g: g.activation(out=junk[:, :], in_=xt[:, :], func=ACT.Square, accum_out=ss[:, :]),
                  reads=[xt], writes=[junk, ss])
            kb.op("dve", lambda g: g.tensor_scalar(out=ss[:, :], in0=ss[:, :], scalar1=1.0 / D, scalar2=EPS,
                                                   op0=ALU.mult, op1=ALU.add), reads=[ss], writes=[ss])
            kb.op("pool", lambda g: g.tensor_tensor(out=rs[:, :], in0=ss[:, :], in1=cx.neghalf[:, :], op=ALU.pow),
                  reads=[ss, cx.neghalf], writes=[rs])
            kb.op("dve", lambda g: g.scalar_tensor_tensor(out=xt[:, :], in0=xt[:, :], scalar=rs[:, :], in1=gbc[:, :],
                                                          op0=ALU.mult, op1=ALU.mult), reads=[xt, rs, gbc], writes=[xt])
            kb.dma("sp", y_out[i * 128:(i + 1) * 128, :], xt[:, :], xt, False, regs=[ry], final=True)


def pipeline_steps(tasks):
    if not tasks:
        return []
    ns = max(len(t) for t in tasks)

    def mk(step):
        def f():
            for k in range(ns):
                i = step - k
                if 0 <= i < len(tasks) and k < len(tasks[i]):
                    tasks[i][k]()
        return f
    return [mk(step) for step in range(len(tasks) + ns - 1)]


def interleave(a, b):
    na, nb = len(a), len(b)
    ia = ib = 0
    while ia < na or ib < nb:
        if ib >= nb or (ia < na and ia * nb <= ib * na):
            a[ia]()
            ia += 1
        else:
            b[ib]()
            ib += 1


def run_pipeline(tasks):
    if not tasks:
        return
    ns = max(len(t) for t in tasks)
    for step in range(len(tasks) + ns - 1):
        for k in range(ns):
            i = step - k
            if 0 <= i < len(tasks) and k < len(tasks[i]):
                tasks[i][k]()


def _xbc_task(kb, cx, j, m, n, shared, first, last, zero_hist, tok0, TB, env):
    st = {}
    ct = j * 4 + m
    hist, cwsb, cbsb = env["hist"], env["cwsb"], env["cbsb"]
    S, rS = env["S"], env["rS"]

    def s0():
        if first:
            shared["wt"] = load_w(kb, env["wring"], env["w_in"], j * 512, 512)
            shared["ofm"] = env["ofring"].next()
            shared["otm"] = env["otring"].next() if j < 8 else None
        psb = env["pring"].next()
        mm_fm(kb, psb, shared["wt"], m, env["hT"], n * 512, 512)
        U = env["uring"].next()
        st["U"] = U
        if zero_hist:
            kb.op("pool", lambda g: g.memset(U[:, 0:3], 0.0), pwrites=[U])
        else:
            kb.op("act", lambda g: g.activation(out=U[:, 0:3], in_=hist[:, ct, :], func=ACT.Copy),
                  reads=[hist], pwrites=[U])
        kb.op("act", lambda g: g.activation(out=U[:, 3:515], in_=psb[:, :], func=ACT.Copy), reads=[psb], pwrites=[U])
        kb.op("act", lambda g: g.activation(out=hist[:, ct, :], in_=U[:, 512:515], func=ACT.Copy),
              reads=[U], pwrites=[hist])

    def s1():
        U = st["U"]
        acc = env["aring"].next()
        kb.op("dve", lambda g: g.tensor_scalar(out=acc[:, :], in0=U[:, 0:512], scalar1=cwsb[:, ct, 0:1],
                                               scalar2=cbsb[:, ct:ct + 1], op0=ALU.mult, op1=ALU.add),
              reads=[U, cwsb, cbsb], writes=[acc])
        for k in range(1, 4):
            kb.op("dve", lambda g, k=k: g.scalar_tensor_tensor(
                out=acc[:, :], in0=U[:, k:k + 512], scalar=cwsb[:, ct, k:k + 1], in1=acc[:, :],
                op0=ALU.mult, op1=ALU.add), reads=[U, cwsb, acc], writes=[acc])
        osb = env["osring"].next()
        st["osb"] = osb
        kb.op("act", lambda g: g.activation(out=osb[:, :], in_=acc[:, :], func=ACT.Silu), reads=[acc], writes=[osb])

    def s2():
        osb = st["osb"]
        ofm, otm = shared["ofm"], shared["otm"]
        if j >= 6:
            kb.op("pool", lambda g: g.tensor_copy(out=ofm[:, m, n * 512:(n + 1) * 512], in_=osb[:, :]),
                  reads=[osb], pwrites=[ofm])
        if j < 8:
            pt = env["tring"].next()
            kb.pe([(lambda g, q=q: g.transpose(out=pt[:, q, :], in_=osb[:, q * 128:(q + 1) * 128],
                                               identity=cx.identb[:, :])) for q in range(4)],
                  reads=[osb, cx.identb], writes=[pt])
            kb.op("dve", lambda g: g.tensor_copy(out=otm[:, n * 4:(n + 1) * 4, m * 128:(m + 1) * 128], in_=pt[:, :, :]),
                  reads=[pt], pwrites=[otm])
        if last:
            if j < 8:
                dst = S["TOK"][tok0:tok0 + TB, j * 512:(j + 1) * 512]
                kb.dma("sp", dst.rearrange("(s p) c -> p s c", p=128), otm[:, :, :], otm, False, regs=[rS])
            if j >= 6:
                r0 = (j - 6) * 512
                kb.dma("sp", S["BCT"][r0:r0 + 512, tok0:tok0 + TB].rearrange("(m p) t -> p m t", p=128), ofm[:, :, :], ofm,
                       False, regs=[rS])

    return [s0, s1, s2]


def ssd_proj(kb, cx, T, x_in, rin, P, S):
    TB = min(1024, T)
    NT = TB // 128
    NS = TB // 512
    w_in = P["w_in"]
    rS = kb.region("SSD")
    rQ, rZ = kb.region("QM"), kb.region("ZS")
    with kb.phase() as ph:
        nb = NormBufs(ph)
        gbc = load_gain(kb, ph, P["norm_g"], D, "ng")
        hT = ph.sb("hT", [128, KT, TB], BF16)
        wring = Ring([ph.sb(f"w{i}", [128, KT, 512], BF16, dma="pool") for i in range(3)])
        pring = Ring([ph.ps(f"pp{i}", [128, 512]) for i in range(4)])
        tring = Ring([ph.ps(f"tq{i}", [128, 4, 128], BF16) for i in range(2)])
        cwsb = ph.sb("cw", [128, 40, 4], F32, dma=True)
        cbsb = ph.sb("cb", [128, 40], F32, dma=True)
        kb.dma("sp", cwsb[:, :, :], P["conv_w"], cwsb, True)
        kb.dma("sp", cbsb[:, :], P["conv_b"], cbsb, True)
        dtb = load_gain(kb, ph, P["dt_bias"], 48, "dtb")
        abc = load_gain(kb, ph, P["a_log"], 48, "abc")
        kb.op("act", lambda g: g.activation(out=abc[:, :], in_=abc[:, :], func=ACT.Exp), reads=[abc], writes=[abc])
        kb.op("dve", lambda g: g.tensor_scalar(out=abc[:, :], in0=abc[:, :], scalar1=-1.0, scalar2=None, op0=ALU.mult),
              reads=[abc], writes=[abc])
        hist = ph.sb("hist", [128, 40, 3], F32)
        uring = Ring([ph.sb(f"U{i}", [128, 515], F32) for i in range(3)])
        aring = Ring([ph.sb(f"acc{i}", [128, 512], F32) for i in range(2)])
        osring = Ring([ph.sb(f"os{i}", [128, 512], BF16) for i in range(3)])
        ofring = Ring([ph.sb(f"ofm{i}", [128, 4, TB], BF16, dma=True) for i in range(2)])
        otring = Ring([ph.sb(f"otm{i}", [128, NT, 512], BF16, dma=True) for i in range(2)])
        dta_all = ph.sb("dtaall", [128, NT, 96], F32, dma=True)
        dtx = ph.sb("dtx", [128, 48], F32)
        dta = ph.sb("dta", [128, 48], F32)
        for tb in range(T // TB):
            tok0 = tb * TB
            norm_tiles(kb, cx, ph, nb, lambda i: x_in[tok0 + i * 128:tok0 + (i + 1) * 128, :], rin, gbc, hT, NT)
            tasks = []
            for j in range(10):
                shared = {}
                for m in range(4):
                    for n in range(NS):
                        tasks.append(_xbc_task(kb, cx, j, m, n, shared, first=(m == 0 and n == 0),
                                               last=(m == 3 and n == NS - 1), zero_hist=(tb == 0 and n == 0),
                                               tok0=tok0, TB=TB, env=dict(wring=wring, ofring=ofring, otring=otring,
                                                                          pring=pring, uring=uring, aring=aring,
                                                                          osring=osring, tring=tring, hist=hist,
                                                                          cwsb=cwsb, cbsb=cbsb, hT=hT, w_in=w_in, S=S, rS=rS)))
            run_pipeline(tasks)
            wt = load_w(kb, wring, w_in, 5120, 48)
            for i in range(NT):
                psb = pring.next()
                mm_tm(kb, psb, wt, 48, hT, i * 128)
                kb.op("dve", lambda g: g.tensor_tensor(out=dtx[:, :], in0=psb[:, :48], in1=dtb[:, :], op=ALU.add),
                      reads=[psb, dtb], writes=[dtx])
                kb.op("act", lambda g: g.activation(out=dtx[:, :], in_=dtx[:, :], func=ACT.Exp), reads=[dtx], writes=[dtx])
                kb.op("act", lambda g: g.activation(out=dta_all[:, i, 0:48], in_=dtx[:, :], func=ACT.Ln, bias=1.0),
                      reads=[dtx], pwrites=[dta_all])
                kb.op("dve", lambda g: g.tensor_tensor(out=dta[:, :], in0=dta_all[:, i, 0:48], in1=abc[:, :], op=ALU.mult),
                      reads=[dta_all, abc], writes=[dta])
                p1 = pring.next()
                kb.pe([lambda g: g.matmul(p1[:, :48], lhsT=cx.cst[:, C_TRIU:C_TRIU + 128], rhs=dta[:, :],
                                          start=True, stop=True)], reads=[cx.cst, dta], writes=[p1])
                kb.op("dve", lambda g: g.tensor_copy(out=dta_all[:, i, 48:96], in_=p1[:, :48]), reads=[p1], pwrites=[dta_all])
            kb.dma("sp", S["DTA"][tok0:tok0 + TB, :].rearrange("(i p) h -> p i h", p=128), dta_all[:, :, :], dta_all, False, regs=[rS])
            for j in range(2):
                wt = load_w(kb, wring, w_in, 5168 + j * 512, 512)
                ofm = ofring.next()
                for m in range(4):
                    for n in range(NS):
                        psb = pring.next()
                        mm_fm(kb, psb, wt, m, hT, n * 512, 512)
                        copy_op(kb, evac_engine(n), ofm[:, m, n * 512:(n + 1) * 512], psb[:, :], reads=[psb], pwrites=[ofm],
                                scale=1.0 / 16.0)
                kb.dma("sp", S["QM"][j * 512:(j + 1) * 512, tok0:tok0 + TB].rearrange("(m p) t -> p m t", p=128),
                       ofm[:, :, :], ofm, False, regs=[rQ])
            for j in range(6):
                wt = load_w(kb, wring, w_in, 6192 + j * 512, 512)
                otm = otring.next()
                for i in range(NT):
                    psb = pring.next()
                    mm_tm(kb, psb, wt, 512, hT, i * 128)
                    kb.op("act", lambda g: g.activation(out=otm[:, i, :], in_=psb[:, :], func=ACT.Silu),
                          reads=[psb], pwrites=[otm])
                kb.dma("sp", S["TOK"][tok0:tok0 + TB, 4096 + j * 512:4096 + (j + 1) * 512].rearrange("(s p) c -> p s c", p=128),
                       otm[:, :, :], otm, False, regs=[rS])
            for j in range(2):
                wt = load_w(kb, wring, w_in, 9264 + j * 512, 512)
                ofm = ofring.next()
                for m in range(4):
                    for n in range(NS):
                        psb = pring.next()
                        mm_fm(kb, psb, wt, m, hT, n * 512, 512)
                        kb.op("act", lambda g: g.activation(out=ofm[:, m, n * 512:(n + 1) * 512], in_=psb[:, :], func=ACT.Silu),
                              reads=[psb], pwrites=[ofm])
                kb.dma("sp", S["ZS"][TOKW + j * 512:TOKW + (j + 1) * 512, tok0:tok0 + TB].rearrange("(m p) t -> p m t", p=128),
                       ofm[:, :, :], ofm, False, regs=[rZ])


def bc_last(ap2d, n):
    return ap2d.unsqueeze(2).broadcast_to([ap2d.shape[0], ap2d.shape[1], n])


def ssd_scan(kb, cx, T, P, S, G):
    rS, rG = kb.region("SSD"), kb.region("G")
    NCH = T // 128
    with kb.phase() as ph:
        dsk = load_gain(kb, ph, P["d_skip"], 48, "dsk")
        g2 = ph.sb("g2", [128, 24], F32, dma=True)
        kb.dma("sp", g2[:, :], P["ssd_norm_g"], g2, True)
        tokr = Ring([ph.sb(f"tok{i}", [128, 7168], BF16, dma=True) for i in range(2)])
        bcr = Ring([ph.sb(f"bct{i}", [128, 16, 128], BF16, dma=True) for i in range(2)])
        dtar = Ring([ph.sb(f"dta{i}", [128, 96], F32, dma=True) for i in range(3)])
        decr = Ring([ph.sb(f"dec{i}", [128, 48, 128], BF16) for i in range(2)])
        MTr = Ring([ph.sb(f"MT{i}", [128, 48, 128], BF16) for i in range(2)])
        xdtr = Ring([ph.sb(f"xdt{i}", [128, 48, 64], BF16) for i in range(2)])
        xdwr = Ring([ph.sb(f"xdtw{i}", [128, 48, 64], BF16) for i in range(2)])
        xsDr = Ring([ph.sb(f"xsD{i}", [128, 48, 64], BF16) for i in range(2)])
        einr = Ring([ph.sb(f"ein{i}", [128, 48], F32) for i in range(3)])
        cdr = Ring([ph.sb(f"cd{i}", [128, 48], F32) for i in range(3)])
        nacr = Ring([ph.sb(f"nacs{i}", [128, 48], F32) for i in range(2)])
        t48r = Ring([ph.sb(f"t48{i}", [128, 48], F32) for i in range(2)])
        hst = [ph.sb(f"hst{g}", [128, 6, 64], F32) for g in range(8)]
        hbf = [ph.sb(f"hbf{g}", [128, 384], BF16) for g in range(8)]
        dtw = ph.sb("dtw", [128, 48], F32)
        t1r = Ring([ph.sb(f"t1{i}", [128, 6, 64], F32) for i in range(2)])
        t2r = Ring([ph.sb(f"t2{i}", [128, 384], F32) for i in range(2)])
        gtr = Ring([ph.sb(f"gt{i}", [128, 384], F32) for i in range(4)])
        ssr = Ring([ph.sb(f"sq{i}", [128, 1], F32) for i in range(4)])
        rsr = Ring([ph.sb(f"rq{i}", [128, 1], F32) for i in range(4)])
        jkr = Ring([ph.sb(f"sjunk{i}", [128, 384], BF16) for i in range(2)])
        gated = ph.sb("gated", [128, TOKW], BF16)
        gfm = ph.sb("gfm", [128, 24, 128], BF16, dma=True)
        rpr = Ring([ph.ps(f"rp{i}", [128, 4, 128]) for i in range(2)])
        cbr = Ring([ph.ps(f"cbp{i}", [128, 4, 128]) for i in range(1)])
        ydr = Ring([ph.ps(f"yd{i}", [128, 6, 64]) for i in range(2)])
        ysr = Ring([ph.ps(f"ys{i}", [128, 6, 64]) for i in range(1)])
        tpr = Ring([ph.ps(f"tg{i}", [128, 4, 128], BF16) for i in range(1)])
        cdp = ph.ps("cdp", [128, 48])
        identf = cx.cst[:, C_ID:C_ID + 128]
        negmb = cx.cstb.t[:, C_NEGM:C_NEGM + 128]
        negm = cx.cst[:, C_NEGM:C_NEGM + 128]

        def make(c):
            st = {}
            t0 = c * 128

            def P1_pieces():
                pieces = []
                loc = {}

                def pro():
                    P1_pro(loc)
                pieces.append(pro)
                for q4 in range(12):
                    pieces.append(lambda q4=q4: P1_grp(loc, q4))
                pieces.append(lambda: P1_fin(loc))
                return pieces

            def P1_pro(loc):
                dta, ein, cd, nacs, dec = dtar.next(), einr.next(), cdr.next(), nacr.next(), decr.next()
                st.update(ein=ein, cd=cd, dec=dec, dta=dta)
                loc.update(dta=dta, ein=ein, cd=cd, nacs=nacs, dec=dec)
                kb.dma("sp", dta[:, :], S["DTA"][t0:t0 + 128, :], dta, True, regs=[rS])
                acs = dta[:, 48:96]
                kb.op("dve", lambda g: g.tensor_scalar(out=nacs[:, :], in0=acs, scalar1=-1.0, scalar2=None, op0=ALU.mult),
                      reads=[dta], writes=[nacs])
                kb.op("act", lambda g: g.activation(out=ein[:, :], in_=acs, func=ACT.Exp), reads=[dta], writes=[ein])

            def P1_grp(loc, q4):
                dta, nacs, dec = loc["dta"], loc["nacs"], loc["dec"]
                if True:
                    rp = rpr.next()
                    fns = []
                    for a in range(4):
                        hh = q4 * 4 + a
                        fns.append(lambda g, a=a: g.matmul(rp[:, a, :], lhsT=identf, rhs=negm, start=True, stop=False))
                        fns.append(lambda g, a=a, hh=hh: g.matmul(rp[:, a, :], lhsT=dta[:, 48 + hh:49 + hh].to_broadcast([128, 128]),
                                                                  rhs=identf, start=False, stop=False))
                        fns.append(lambda g, a=a, hh=hh: g.matmul(rp[:, a, :], lhsT=identf,
                                                                  rhs=nacs[:, hh:hh + 1].to_broadcast([128, 128]),
                                                                  start=False, stop=True))
                    kb.pe(fns, reads=[cx.cst, cx.cstb, dta, nacs], writes=[rp])
                    kb.op("act", lambda g: g.activation(out=dec[:, q4 * 4:(q4 + 1) * 4, :], in_=rp[:, :, :], func=ACT.Exp),
                          reads=[rp], pwrites=[dec])
            def P1_fin(loc):
                dta, cd = loc["dta"], loc["cd"]
                kb.pe([lambda g: g.matmul(cdp[:, :], lhsT=cx.cst[:, C_SEL:C_SEL + 128], rhs=dta[:, 48:96], start=True, stop=True)],
                      reads=[cx.cst, dta], writes=[cdp])
                kb.op("act", lambda g: g.activation(out=cd[:, :], in_=cdp[:, :], func=ACT.Exp), reads=[cdp], writes=[cd])

            def P2():
                tok, bct = tokr.next(), bcr.next()
                MT, xdt, xdtw, xsD = MTr.next(), xdtr.next(), xdwr.next(), xsDr.next()
                dec, dta = st["dec"], st["dta"]
                st.update(tok=tok, bct=bct, MT=MT, xdt=xdt, xdtw=xdtw, xsD=xsD)
                kb.dma("sp", tok[:, :], S["TOK"][t0:t0 + 128, :], tok, True, regs=[rS])
                kb.dma("sp", bct[:, :, :], S["BCT"][:, t0:t0 + 128].rearrange("(g p) t -> p g t", p=128), bct, True, regs=[rS])
                xs = tok[:, 0:3072].rearrange("p (h e) -> p h e", e=64)
                dtc = dta[:, 0:48]
                kb.op("dve", lambda g: g.tensor_tensor(out=dtw[:, :], in0=dtc, in1=dec[:, :, 127], op=ALU.mult),
                      reads=[dta, dec], writes=[dtw])
                kb.op("pool", lambda g: g.tensor_tensor(out=xdt[:, :, :], in0=xs, in1=bc_last(dtc, 64),
                                                        op=ALU.mult), reads=[tok, dta], writes=[xdt])
                kb.op("pool", lambda g: g.tensor_tensor(out=xsD[:, :, :], in0=xs, in1=bc_last(dsk[:, :], 64),
                                                        op=ALU.mult), reads=[tok, dsk], writes=[xsD])
                kb.op("pool", lambda g: g.tensor_tensor(out=xdtw[:, :, :], in0=xs, in1=bc_last(dtw[:, :], 64),
                                                        op=ALU.mult), reads=[tok, dtw], writes=[xdtw])
                for hf in range(2):
                    cb = cbr.next()
                    kb.pe([(lambda g, q=q: g.matmul(cb[:, q, :], lhsT=bct[:, hf * 4 + q, :], rhs=bct[:, 8 + hf * 4 + q, :],
                                                    start=True, stop=True)) for q in range(4)], reads=[bct], writes=[cb])
                    kb.op("dve", lambda g: g.tensor_tensor(
                        out=MT[:, hf * 24:(hf + 1) * 24, :].rearrange("p (q j) l -> p q j l", j=6),
                        in0=dec[:, hf * 24:(hf + 1) * 24, :].rearrange("p (q j) l -> p q j l", j=6),
                        in1=cb[:, :, :].unsqueeze(2).broadcast_to([128, 4, 6, 128]), op=ALU.mult),
                        reads=[dec, cb], pwrites=[MT])

            def B():
                tok, bct, MT, xdt, xdtw, xsD, ein, cd = (st[k] for k in ("tok", "bct", "MT", "xdt", "xdtw", "xsD", "ein", "cd"))

                def group(gi):
                    gs = {}

                    def G1():
                        yd = ydr.next()
                        fns = [lambda g: g.matmul(yd[:, :, :], lhsT=cx.identb[:, :], rhs=xsD[:, gi * 6:(gi + 1) * 6, :],
                                                  start=True, stop=False)]
                        for hh in range(6):
                            h = gi * 6 + hh
                            fns.append(lambda g, h=h, hh=hh: g.matmul(yd[:, hh, :], lhsT=MT[:, h, :], rhs=xdt[:, h, :],
                                                                      start=False, stop=(hh == 5)))
                        kb.pe(fns, reads=[cx.identb, xsD, MT, xdt], writes=[yd])
                        gt = gtr.next()
                        gs["gt"] = gt
                        ydf = yd[:, :, :].rearrange("p h e -> p (h e)")
                        zsl = tok[:, 4096 + gi * 384:4096 + (gi + 1) * 384]
                        if c > 0:
                            yo = ysr.next()
                            kb.pe([lambda g: g.matmul(yo[:, :, :], lhsT=bct[:, 8 + gi, :], rhs=hbf[gi][:, :], start=True, stop=True)],
                                  reads=[bct, hbf[gi]], writes=[yo])
                            t1 = t1r.next()
                            kb.op("dve", lambda g: g.tensor_tensor(out=t1[:, :, :], in0=yo[:, :, :],
                                                                   in1=bc_last(ein[:, gi * 6:(gi + 1) * 6], 64), op=ALU.mult),
                                  reads=[yo, ein], writes=[t1])
                            t2 = t2r.next()
                            kb.op("dve", lambda g: g.tensor_tensor(out=t2[:, :], in0=ydf, in1=t1[:, :, :].rearrange("p h e -> p (h e)"),
                                                                   op=ALU.add), reads=[yd, t1], writes=[t2])
                            kb.op("dve", lambda g: g.tensor_tensor(out=gt[:, :], in0=t2[:, :], in1=zsl, op=ALU.mult),
                                  reads=[t2, tok], writes=[gt])
                        else:
                            kb.op("dve", lambda g: g.tensor_tensor(out=gt[:, :], in0=ydf, in1=zsl, op=ALU.mult),
                                  reads=[yd, tok], writes=[gt])
                        ss = ssr.next()
                        gs["ss"] = ss
                        jk = jkr.next()
                        kb.op("act", lambda g: g.activation(out=jk[:, :], in_=gt[:, :], func=ACT.Square, accum_out=ss[:, :]),
                              reads=[gt], writes=[jk, ss])

                    def G2():
                        gt, ss = gs["gt"], gs["ss"]
                        rs = rsr.next()
                        kb.op("dve", lambda g: g.tensor_scalar(out=ss[:, :], in0=ss[:, :], scalar1=1.0 / 384.0, scalar2=EPS,
                                                               op0=ALU.mult, op1=ALU.add), reads=[ss], writes=[ss])
                        kb.op("pool", lambda g: g.tensor_tensor(out=rs[:, :], in0=ss[:, :], in1=cx.neghalf[:, :], op=ALU.pow),
                              reads=[ss, cx.neghalf], writes=[rs])
                        gs["rs"] = rs
                        if c < NCH - 1:
                            sps = ysr.next()
                            kb.pe([lambda g: g.matmul(sps[:, :, :], lhsT=tok[:, 3072 + gi * 128:3072 + (gi + 1) * 128],
                                                      rhs=xdtw[:, gi * 6:(gi + 1) * 6, :], start=True, stop=True)],
                                  reads=[tok, xdtw], writes=[sps])
                            if c > 0:
                                kb.op("pool", lambda g: g.tensor_tensor(out=hst[gi][:, :, :], in0=hst[gi][:, :, :],
                                                                        in1=bc_last(cd[:, gi * 6:(gi + 1) * 6], 64), op=ALU.mult),
                                      reads=[hst[gi], cd], writes=[hst[gi]])
                                kb.op("dve", lambda g: g.tensor_tensor(out=hst[gi][:, :, :], in0=sps[:, :, :], in1=hst[gi][:, :, :],
                                                                       op=ALU.add), reads=[sps, hst[gi]], writes=[hst[gi]])
                            else:
                                kb.op("dve", lambda g: g.tensor_copy(out=hst[gi][:, :, :], in_=sps[:, :, :]), reads=[sps], writes=[hst[gi]])

                    def G3():
                        gt, rs = gs["gt"], gs["rs"]
                        kb.op("act", lambda g: g.activation(out=gated[:, gi * 384:(gi + 1) * 384], in_=gt[:, :], func=ACT.Copy,
                                                            scale=rs[:, :]), reads=[gt, rs], pwrites=[gated])
                        if c < NCH - 1:
                            kb.op("act", lambda g: g.activation(out=hbf[gi][:, :], in_=hst[gi][:, :, :].rearrange("p h e -> p (h e)"),
                                                                func=ACT.Copy), reads=[hst[gi]], writes=[hbf[gi]])

                    return [G1, G2, G3]

                steps = pipeline_steps([group(gi) for gi in range(8)])
                def tail():
                  for q in range(6):
                    tp = tpr.next()
                    kb.pe([(lambda g, j=j: g.transpose(out=tp[:, j, :], in_=gated[:, (4 * q + j) * 128:(4 * q + j + 1) * 128],
                                                       identity=cx.identb[:, :])) for j in range(4)],
                          reads=[gated, cx.identb], writes=[tp])
                    kb.op("dve", lambda g: g.tensor_tensor(out=gfm[:, 4 * q:4 * q + 4, :], in0=tp[:, :, :],
                                                           in1=bc_last(g2[:, 4 * q:4 * q + 4], 128), op=ALU.mult),
                          reads=[tp, g2], pwrites=[gfm])
                  kb.dma("sp", G[0:TOKW, t0:t0 + 128].rearrange("(ct p) t -> p ct t", p=128), gfm[:, :, :], gfm, False, regs=[rG])
                return steps + [tail]

            return P1_pieces, P2, B

        chunks = [make(c) for c in range(NCH)]
        for step in range(NCH + 2):
            a = chunks[step][0]() if step < NCH else []
            b = []
            if 0 <= step - 1 < NCH:
                b.append(chunks[step - 1][1])
            if 0 <= step - 2 < NCH:
                b += chunks[step - 2][2]()
            interleave(a, b)


def attn_proj(kb, cx, T, x_in, rin, P, S):
    TB = min(1024, T)
    NT = TB // 128
    NS = TB // 512
    w_in = P["w_in"]
    rA, rQ, rZ = kb.region("ATT"), kb.region("QM"), kb.region("ZS")
    with kb.phase() as ph:
        nb = NormBufs(ph)
        gbc = load_gain(kb, ph, P["norm_g"], D, "ng")
        hT = ph.sb("hT", [128, KT, TB], BF16)
        wring = Ring([ph.sb(f"w{i}", [128, KT, 512], BF16, dma="pool") for i in range(3)])
        pring = Ring([ph.ps(f"pp{i}", [128, 512]) for i in range(6)])
        ofring = Ring([ph.sb(f"ofm{i}", [128, 4, TB], BF16, dma=True) for i in range(2)])
        otring = Ring([ph.sb(f"otm{i}", [128, NT, 512], BF16, dma=True) for i in range(2)])
        for gi, (_, d) in enumerate(DIL):
            nsub = T // d
            xr = x_in.rearrange("(i r) c -> r i c", r=d)
            for tb in range(T // TB):
                tok0 = tb * TB

                def rows(i, tok0=tok0, nsub=nsub, xr=xr):
                    tp = tok0 + i * 128
                    r, i0 = tp // nsub, tp % nsub
                    return xr[r, i0:i0 + 128, :]

                norm_tiles(kb, cx, ph, nb, rows, rin, gbc, hT, NT)
                base = gi * 9216
                segs = [("q", base, 6), ("k", base + 3072, 6), ("v", base + 6144, 6)]
                if gi == 0:
                    segs += [("qm", 27648, 2), ("z", 28672, 8)]
                for kind, c0, ntile in segs:
                    for j in range(ntile):
                        wt = load_w(kb, wring, w_in, c0 + j * 512, 512)
                        if kind == "v":
                            otm = otring.next()
                            for i in range(NT):
                                psb = pring.next()
                                mm_tm(kb, psb, wt, 512, hT, i * 128)
                                copy_op(kb, evac_engine(i), otm[:, i, :], psb[:, :], reads=[psb], pwrites=[otm])
                            kb.dma("sp", S["V"][gi][tok0:tok0 + TB, j * 512:(j + 1) * 512].rearrange("(s p) c -> p s c", p=128),
                                   otm[:, :, :], otm, False, regs=[rA])
                            continue
                        ofm = ofring.next()
                        for m in range(4):
                            for n in range(NS):
                                psb = pring.next()
                                mm_fm(kb, psb, wt, m, hT, n * 512, 512)
                                dst = ofm[:, m, n * 512:(n + 1) * 512]
                                if kind == "z":
                                    kb.op("act", lambda g: g.activation(out=dst, in_=psb[:, :], func=ACT.Silu),
                                          reads=[psb], pwrites=[ofm])
                                elif kind == "q":
                                    copy_op(kb, evac_engine(n + m), dst, psb[:, :], reads=[psb], pwrites=[ofm],
                                            scale=128.0 ** -0.5)
                                elif kind == "qm":
                                    copy_op(kb, evac_engine(n + m), dst, psb[:, :], reads=[psb], pwrites=[ofm], scale=1.0 / 16.0)
                                else:
                                    copy_op(kb, evac_engine(n + m), dst, psb[:, :], reads=[psb], pwrites=[ofm])
                        if kind == "q":
                            dt_, rg = S["Q"][gi], rA
                        elif kind == "k":
                            dt_, rg = S["K"][gi], rA
                        elif kind == "qm":
                            dt_, rg = S["QM"], rQ
                        else:
                            dt_, rg = S["ZS"], rZ
                        kb.dma("sp", dt_[j * 512:(j + 1) * 512, tok0:tok0 + TB].rearrange("(m p) t -> p m t", p=128),
                               ofm[:, :, :], ofm, False, regs=[rg])


def attn_core(kb, cx, T, S, G):
    rA, rZ, rG = kb.region("ATT"), kb.region("ZS"), kb.region("G")
    NB = T // 128
    slopes = alibi_slopes()
    with kb.phase() as ph:
        qr = Ring([ph.sb(f"aq{i}", [128, T], BF16, dma=True) for i in range(2)])
        kr = Ring([ph.sb(f"ak{i}", [128, T], BF16, dma=True) for i in range(2)])
        vr = Ring([ph.sb(f"av{i}", [128, NB, 128], BF16, dma=True) for i in range(2)])
        acc = ph.sb("acc", [128, T], F32)
        dacc = ph.sb("dacc", [128, T], F32)
        ebr = Ring([ph.sb(f"eb{i}", [128, 2, 128], BF16) for i in range(3)])
        pexr = Ring([ph.sb(f"pex{i}", [128, 2, 2, 128], BF16) for i in range(3)])
        ptr_ = Ring([ph.sb(f"ptt{i}", [128, 2, 2, 128], BF16) for i in range(6)])
        spr = Ring([ph.ps(f"sp{i}", [128, 2, 2, 128]) for i in range(4)])
        opr = Ring([ph.ps(f"op{i}", [128, 4, 128]) for i in range(2)])
        dpr = Ring([ph.ps(f"dp{i}", [128, 4, 128]) for i in range(2)])
        rel = cx.cst[:, C_REL:C_REL + 256]
        tasks = []
        npair_ctr = [0]

        def head_setup(j, gi, d, hs):
            def f():
                qT, kT, v = qr.next(), kr.next(), vr.next()
                kb.dma("sp", qT[:, :], S["Q"][gi][j * 128:(j + 1) * 128, :], qT, True, regs=[rA])
                kb.dma("sp", kT[:, :], S["K"][gi][j * 128:(j + 1) * 128, :], kT, True, regs=[rA])
                kb.dma("sp", v[:, :, :], S["V"][gi][:, j * 128:(j + 1) * 128].rearrange("(b p) e -> p b e", p=128), v, True,
                       regs=[rA])
                eb = ebr.next()
                sc = -float(slopes[gi, j]) * d
                kb.op("act", lambda g: g.activation(out=eb[:, :, :].rearrange("p a q -> p (a q)"), in_=rel, func=ACT.Exp, scale=sc),
                      reads=[cx.cst], writes=[eb])
                hs.update(qT=qT, kT=kT, v=v, eb=eb)
            return f

        def batch(j, gi, d, r, kb0, nbs, QB, hs, setup):
            bs = {}

            def F():
                if setup is not None:
                    setup()
                qT, kT, eb = hs["qT"], hs["kT"], hs["eb"]
                pts = []
                for pair in range(0, QB, 2):
                    npair = min(2, QB - pair)
                    sp, pex, pt = spr.next(), pexr.next(), ptr_.next()
                    fns = []
                    for a in range(npair):
                        kbi = kb0 + pair + a
                        bb = r * nbs + kbi
                        if kbi > 0:
                            fns.append(lambda g, a=a, bb=bb: g.matmul(sp[:, a, 0, :], lhsT=kT[:, (bb - 1) * 128:bb * 128],
                                                                      rhs=qT[:, bb * 128:(bb + 1) * 128], start=True, stop=True))
                        fns.append(lambda g, a=a, bb=bb: g.matmul(sp[:, a, 1, :], lhsT=kT[:, bb * 128:(bb + 1) * 128],
                                                                  rhs=qT[:, bb * 128:(bb + 1) * 128], start=True, stop=True))
                    kb.pe(fns, reads=[kT, qT], writes=[sp])
                    npair_ctr[0] += 1
                    me = "dve" if npair_ctr[0] % 2 == 0 else "pool"
                    first = (kb0 + pair == 0)
                    if first:
                        kb.op("act", lambda g: g.activation(out=pex[:, 0, 1, :], in_=sp[:, 0, 1, :], func=ACT.Exp),
                              reads=[sp], pwrites=[pex])
                        kb.op(me, lambda g: g.tensor_tensor(out=pt[:, 0, 1, :], in0=pex[:, 0, 1, :], in1=eb[:, 1, :],
                                                            op=ALU.mult), reads=[pex, eb], pwrites=[pt])
                        if npair > 1:
                            kb.op("act", lambda g: g.activation(out=pex[:, 1, :, :], in_=sp[:, 1, :, :], func=ACT.Exp),
                                  reads=[sp], pwrites=[pex])
                            kb.op(me, lambda g: g.tensor_tensor(out=pt[:, 1, :, :], in0=pex[:, 1, :, :], in1=eb[:, :, :],
                                                                op=ALU.mult), reads=[pex, eb], pwrites=[pt])
                    else:
                        kb.op("act", lambda g: g.activation(out=pex[:, :npair, :, :], in_=sp[:, :npair, :, :], func=ACT.Exp),
                              reads=[sp], pwrites=[pex])
                        kb.op(me, lambda g: g.tensor_tensor(
                            out=pt[:, :npair, :, :], in0=pex[:, :npair, :, :],
                            in1=eb[:, :, :].unsqueeze(1).broadcast_to([128, npair, 2, 128]), op=ALU.mult),
                            reads=[pex, eb], pwrites=[pt])
                    pts.append((pt, pair, npair))
                bs["pts"] = pts

            def K():
                v = hs["v"]
                pts = bs["pts"]
                op_, dp = opr.next(), dpr.next()
                fo, fd = [], []
                for pt, pair, npair in pts:
                    for a in range(npair):
                        kbi = kb0 + pair + a
                        bb = r * nbs + kbi
                        qi = pair + a
                        lo = 0 if kbi > 0 else 1
                        for part in range(lo, 2):
                            vb = bb - 1 + part
                            fo.append(lambda g, pt=pt, a=a, part=part, vb=vb, qi=qi, lo=lo: g.matmul(
                                op_[:, qi, :], lhsT=v[:, vb, :], rhs=pt[:, a, part, :], start=(part == lo), stop=(part == 1)))
                            fd.append(lambda g, pt=pt, a=a, part=part, qi=qi, lo=lo: g.matmul(
                                dp[:, qi, :], lhsT=cx.onesb[:, :], rhs=pt[:, a, part, :], start=(part == lo), stop=(part == 1)))
                rd = [p[0] for p in pts]
                kb.pe(fo, reads=[v] + rd, writes=[op_])
                kb.pe(fd, reads=[cx.onesb] + rd, writes=[dp])
                n_q = QB * 128
                if d == 1:
                    a_out = acc[:, kb0 * 128:kb0 * 128 + n_q]
                    d_out = dacc[:, kb0 * 128:kb0 * 128 + n_q]
                else:
                    a_out = acc[:, :].rearrange("p (q s) -> p q s", s=d)[:, kb0 * 128:kb0 * 128 + n_q, r]
                    d_out = dacc[:, :].rearrange("p (q s) -> p q s", s=d)[:, kb0 * 128:kb0 * 128 + n_q, r]
                o_in = op_[:, :QB, :].rearrange("p a q -> p (a q)")
                d_in = dp[:, :QB, :].rearrange("p a q -> p (a q)")
                if gi == 0:
                    kb.op("act", lambda g: g.activation(out=a_out, in_=o_in, func=ACT.Copy), reads=[op_], pwrites=[acc])
                    kb.op("act", lambda g: g.activation(out=d_out, in_=d_in, func=ACT.Copy), reads=[dp], pwrites=[dacc])
                else:
                    kb.op("dve", lambda g: g.tensor_tensor(out=a_out, in0=o_in, in1=a_out, op=ALU.add),
                          reads=[op_, acc], pwrites=[acc])
                    kb.op("dve", lambda g: g.tensor_tensor(out=d_out, in0=d_in, in1=d_out, op=ALU.add),
                          reads=[dp, dacc], pwrites=[dacc])

            return [F, K]

        def head_final(j):
            def fin():
                zs = qr.next()
                kb.dma("sp", zs[:, :], S["ZS"][j * 128:(j + 1) * 128, :], zs, True, regs=[rZ])
                kb.op("dve", lambda g: g.reciprocal(out=dacc[:, :], in_=dacc[:, :]), reads=[dacc], writes=[dacc])
                kb.op("dve", lambda g: g.tensor_tensor(out=acc[:, :], in0=acc[:, :], in1=dacc[:, :], op=ALU.mult),
                      reads=[acc, dacc], writes=[acc])
                kb.op("pool", lambda g: g.tensor_tensor(out=zs[:, :], in0=acc[:, :], in1=zs[:, :], op=ALU.mult),
                      reads=[acc, zs], writes=[zs])
                kb.dma("sp" if os.environ.get("KB_SIM") else "pool", G[j * 128:(j + 1) * 128, :], zs[:, :], zs, False, regs=[rG])
            return [lambda: None, fin]

        for j in range(24):
            for gi, (_, d) in enumerate(DIL):
                nsub = T // d
                nbs = nsub // 128
                QB = min(4, nbs)
                hs = {}
                setup = head_setup(j, gi, d, hs)
                for r in range(d):
                    for kb0 in range(0, nbs, QB):
                        tasks.append(batch(j, gi, d, r, kb0, nbs, QB, hs, setup))
                        setup = None
            tasks.append(head_final(j))
        run_pipeline(tasks)


SSD_KEYS = ("norm_g", "w_in", "conv_w", "conv_b", "dt_bias", "a_log", "d_skip", "ssd_norm_g", "w_mem_kv", "w_out")
ATT_KEYS = ("norm_g", "w_in", "w_mem_kv", "w_out")
SSD_SHAPES = {"norm_g": [D], "w_in": [D, SSD_IN], "conv_w": [128, 40, 4], "conv_b": [128, 40], "dt_bias": [48],
              "a_log": [48], "d_skip": [48], "ssd_norm_g": [128, 24], "w_mem_kv": [D, 2048], "w_out": [MIXW, D]}
ATT_SHAPES = {"norm_g": [D], "w_in": [D, ATT_IN], "w_mem_kv": [D, 2048], "w_out": [MIXW, D]}


def build(T, layer_ids, with_final=True):
    nc = bass.Bass("TRN2", target_bir_lowering=False)

    def din(name, shape):
        return nc.dram_tensor(name, list(shape), F32, kind="ExternalInput").ap()

    def scr(name, shape, dt=BF16):
        return nc.dram_tensor(name, list(shape), dt, kind="Internal").ap()

    x_ext = din("x", [T, D])
    mem = din("mem", [N_MEM, D])
    consts = din("consts", [128, NCONST])
    mem_g = din("mem_norm_g", [D])
    fin_g = din("final_norm_g", [D]) if with_final else None
    params = {}
    for li in layer_ids:
        shapes = SSD_SHAPES if li % 2 == 0 else ATT_SHAPES
        params[li] = {k: din(f"{k}_{li}", shp) for k, shp in shapes.items()}
    y_ext = nc.dram_tensor("y", [T, D], F32, kind="ExternalOutput").ap()
    xs_ = [scr("xa", [T, D], F32), scr("xb", [T, D], F32)]
    S = {"QM": scr("QM", [MEMW, T]), "ZS": scr("ZS", [MIXW, T])}
    G = scr("G", [MIXW, T])
    if any(li % 2 == 0 for li in layer_ids):
        S.update({"TOK": scr("TOK", [T, 7168]), "BCT": scr("BCT", [2048, T]), "DTA": scr("DTA", [T, 96], F32)})
    if any(li % 2 == 1 for li in layer_ids):
        S.update({"Q": [scr(f"Q{g}", [TOKW, T]) for g in range(3)], "K": [scr(f"K{g}", [TOKW, T]) for g in range(3)],
                  "V": [scr(f"V{g}", [T, TOKW]) for g in range(3)]})

    kb = KB(nc)
    cx = Ctx()
    with kb.es:
        with kb.phase() as top:
            cx.cst = top.sb("cst", [128, NCONST], F32, dma=True)
            cx.cstb = top.sb("cstb", [128, NCONST], BF16, dma="pool")
            kb.dma("sp", cx.cst[:, :], consts, cx.cst, True)
            kb.dma("pool", cx.cstb[:, :], consts, cx.cstb, True)
            cx.identb = Buf(cx.cstb.t, "identb")
            cx.onesb = Buf(cx.cstb.t, "onesb")
            cx.identb = _View(cx.cstb, C_ID, 128)
            cx.onesb = _View(cx.cstb, C_ONES, 128)
            cx.neghalf = top.sb("neghalf", [128, 1], F32)
            kb.op("pool", lambda g: g.memset(cx.neghalf[:, :], -0.5), writes=[cx.neghalf])
            cx.memnT = top.sb("memnT", [128, KT, N_MEM], BF16)
            mem_prep(kb, cx, top, mem, mem_g)
            cur, rcur = x_ext, kb.region("xext")
            for n_, li in enumerate(layer_ids):
                P = params[li]
                nxt = xs_[n_ % 2]
                rnxt = kb.region(f"x{n_ % 2}")
                with kb.phase() as lp:
                    mkT = lp.sb("mkT", [128, 8, N_MEM], BF16)
                    mv = lp.sb("mv", [128, 2, MEMW], BF16)
                    mem_kv(kb, cx, lp, P["w_mem_kv"], mkT, mv)
                    if li % 2 == 0:
                        ssd_proj(kb, cx, T, cur, rcur, P, S)
                        ssd_scan(kb, cx, T, P, S, G)
                    else:
                        attn_proj(kb, cx, T, cur, rcur, P, S)
                        attn_core(kb, cx, T, S, G)
                    mem_attention(kb, cx, T, S["QM"], S["ZS"][TOKW:MIXW, :], G, mkT, mv)
                    if not with_final and n_ == len(layer_ids) - 1:
                        out_proj(kb, cx, T, G, P["w_out"], cur, y_ext, rcur, kb.region("Y"))
                    else:
                        out_proj(kb, cx, T, G, P["w_out"], cur, nxt, rcur, rnxt)
                cur, rcur = nxt, rnxt
            if with_final:
                final_norm(kb, cx, T, cur, rcur, fin_g, y_ext)
            kb.finish()
    return nc


class _View:
    def __init__(self, base, c0, n):
        self.base = base
        self.c0 = c0
        self.n = n

    @property
    def w(self):
        return self.base.w

    @w.setter
    def w(self, v):
        self.base.w = v

    @property
    def r(self):
        return self.base.r

    @r.setter
    def r(self, v):
        self.base.r = v

    def __getitem__(self, k):
        assert isinstance(k, tuple) and len(k) == 2
        cs = k[1]
        assert cs == slice(None)
        return self.base.t[k[0], self.c0:self.c0 + self.n]


def _layer_inputs(inputs, li):
    d = {}
    keys = SSD_KEYS if li % 2 == 0 else ATT_KEYS
    for k in keys:
        a = np.ascontiguousarray(np.asarray(inputs[f"{k}_{li}"], dtype=np.float32))
        if k == "conv_w":
            a = np.ascontiguousarray(a.T.reshape(40, 128, 4).transpose(1, 0, 2))
        elif k == "conv_b":
            a = np.ascontiguousarray(a.reshape(40, 128).T)
        elif k == "ssd_norm_g":
            a = np.ascontiguousarray(a.reshape(24, 128).T)
        d[f"{k}_{li}"] = a
    return d


_SELFWAIT = bool(int(os.environ.get('KB_SELFWAIT', '0')))
_NC_CACHE = {}
_RUN_KW = {}
_LAST = {}


def run_layers(x, mem, inputs, layer_ids, with_final, n_cores=None):
    B, T, _ = x.shape
    key = (T, tuple(layer_ids), with_final)
    if key not in _NC_CACHE:
        _NC_CACHE[key] = build(T, list(layer_ids), with_final)
    nc = _NC_CACHE[key]
    shared = {"consts": make_consts(), "mem_norm_g": np.asarray(inputs["mem_norm_g"], np.float32)}
    if with_final:
        shared["final_norm_g"] = np.asarray(inputs["final_norm_g"], np.float32)
    for li in layer_ids:
        shared.update(_layer_inputs(inputs, li))
    in_maps = []
    for b in range(B):
        m = dict(shared)
        m["x"] = np.ascontiguousarray(x[b], dtype=np.float32)
        m["mem"] = np.ascontiguousarray(mem[b], dtype=np.float32)
        in_maps.append(m)
    res = run_bass_kernel_spmd(nc, in_maps, core_ids=list(range(B)), **_RUN_KW)
    _LAST["exec_ns"] = getattr(res, "exec_time_ns", None)
    return np.stack([np.asarray(r["y"]) for r in res.results], axis=0)


ALL_INPUTS = (
    "x", "mem", "mem_norm_g", "final_norm_g",
    "norm_g_0", "w_in_0", "conv_w_0", "conv_b_0", "dt_bias_0", "a_log_0", "d_skip_0", "ssd_norm_g_0", "w_mem_kv_0", "w_out_0",
    "norm_g_1", "w_in_1", "w_mem_kv_1", "w_out_1",
    "norm_g_2", "w_in_2", "conv_w_2", "conv_b_2", "dt_bias_2", "a_log_2", "d_skip_2", "ssd_norm_g_2", "w_mem_kv_2", "w_out_2",
    "norm_g_3", "w_in_3", "w_mem_kv_3", "w_out_3",
)


def kernel(**inputs):
    inputs = {k: inputs[k] for k in ALL_INPUTS}
    x = np.asarray(inputs["x"], np.float32)
    mem = np.asarray(inputs["mem"], np.float32)
    return run_layers(x, mem, inputs, [0, 1, 2, 3], True).astype(np.float32)
```

```python
import contextlib
import math
import os
import numpy as np
import ml_dtypes
import concourse.bass as bass
import concourse.mybir as mybir
from concourse.bass_utils import run_bass_kernel_spmd

F32 = mybir.dt.float32
BF16 = mybir.dt.bfloat16
ACT = mybir.ActivationFunctionType
ALU = mybir.AluOpType

D = 2048
KT = D // 128
N_MEM = 256
MIXW = 4096
TOKW = 3072
MEMW = 1024
SSD_IN = 10288
ATT_IN = 32768
EPS = 1e-6
NEG = -30000.0
DIL = ((128, 1), (512, 4), (2048, 16))


class Sem:
    def __init__(self, h, idx):
        self.h = h
        self.idx = idx
        self.count = 0


class Eng:
    def __init__(self, name, eng, sem):
        self.name = name
        self.eng = eng
        self.sem = sem
        self.known = {}


class Buf:
    def __init__(self, t, name, dsem=None):
        self.t = t
        self.name = name
        self.w = {}
        self.r = {}
        self.dsem = dsem

    def __getitem__(self, k):
        return self.t[k]


class KB:
    def __init__(self, nc, n_dma_sems=90):
        self.nc = nc
        self.es = contextlib.ExitStack()
        self.sems = []
        self.engs = {}
        for name, eng in (("pe", nc.tensor), ("act", nc.scalar), ("dve", nc.vector),
                          ("pool", nc.gpsimd), ("sp", nc.sync)):
            s = self._new_sem("e_" + name)
            self.engs[name] = Eng(name, eng, s)
        self.dma_pool = [self._new_sem(f"d{i}") for i in range(60 if os.environ.get("KB_SIM") else n_dma_sems)]
        self.dma_free = list(self.dma_pool)
        self.regions = {}
        self.out_tokens = []
        self.uid = 0

    def _new_sem(self, name):
        h = self.es.enter_context(self.nc.semaphore(name))
        s = Sem(h, len(self.sems))
        self.sems.append(s)
        return s

    def phase(self):
        return Phase(self)

    def region(self, name):
        if name not in self.regions:
            self.regions[name] = Buf(None, name)
        return self.regions[name]

    def _need(self, E, idx, val):
        if val <= 0:
            return
        if idx == E.sem.idx and (E.name in ("pe", "sp") or not _SELFWAIT):
            return
        if E.known.get(idx, 0) >= val:
            return
        E.eng.wait_ge(self.sems[idx].h, val)
        E.known[idx] = val

    def _deps(self, E, reads, writes):
        for b in reads:
            for i, v in b.w.items():
                self._need(E, i, v)
        for b in writes:
            for i, v in b.w.items():
                self._need(E, i, v)
            for i, v in b.r.items():
                self._need(E, i, v)

    def _mark(self, idx, val, reads, writes):
        for b in reads:
            if b.r.get(idx, 0) < val:
                b.r[idx] = val
        for b in writes:
            b.w = {idx: val}
            b.r = {}

    def _pdeps(self, E, pwrites):
        for b in pwrites:
            for i, v in b.r.items():
                self._need(E, i, v)

    def _pmark(self, idx, val, pwrites):
        for b in pwrites:
            if b.w.get(idx, 0) < val:
                b.w[idx] = val

    def op(self, e, fn, reads=(), writes=(), pwrites=()):
        E = self.engs[e]
        self._deps(E, reads, writes)
        self._pdeps(E, pwrites)
        ins = fn(E.eng)
        E.sem.count += 1
        ins.then_inc(E.sem.h, 1)
        self._mark(E.sem.idx, E.sem.count, reads, writes)
        self._pmark(E.sem.idx, E.sem.count, pwrites)

    def pe(self, fns, reads=(), writes=(), pwrites=()):
        E = self.engs["pe"]
        self._deps(E, reads, writes)
        self._pdeps(E, pwrites)
        ins = None
        for f in fns:
            ins = f(E.eng)
        E.sem.count += 1
        ins.then_inc(E.sem.h, 1)
        self._mark(E.sem.idx, E.sem.count, reads, writes)
        self._pmark(E.sem.idx, E.sem.count, pwrites)

    def dma(self, q, out, in_, sb, load, regs=(), final=False):
        E = self.engs[q]
        if load:
            for b in regs:
                for i, v in b.w.items():
                    self._need(E, i, v)
            for i, v in sb.w.items():
                if i != sb.dsem.idx:
                    self._need(E, i, v)
            for i, v in sb.r.items():
                self._need(E, i, v)
        else:
            self._deps(E, [sb], ())
            for b in regs:
                for i, v in b.r.items():
                    self._need(E, i, v)
        s = sb.dsem
        E.eng.dma_start(out=out, in_=in_).then_inc(s.h, 16)
        s.count += 16
        if load:
            for b in regs:
                if b.r.get(s.idx, 0) < s.count:
                    b.r[s.idx] = s.count
            sb.w[s.idx] = s.count
        else:
            if sb.r.get(s.idx, 0) < s.count:
                sb.r[s.idx] = s.count
            for rg in regs:
                rg.w[s.idx] = s.count
            if final:
                self.out_tokens.append((s.idx, s.count))

    def barrier(self):
        names = ["pe", "act", "dve", "pool", "sp"]
        for e in names:
            E = self.engs[e]
            for x in names:
                if x != e:
                    X = self.engs[x]
                    self._need(E, X.sem.idx, X.sem.count)
            for s in self.dma_pool:
                self._need(E, s.idx, s.count)

    def finish(self):
        E = self.engs["sp"]
        for i, v in self.out_tokens:
            self._need(E, i, v)
        self.barrier()


class Phase:
    def __init__(self, kb):
        self.kb = kb
        self.es = contextlib.ExitStack()
        self.taken = []

    def __enter__(self):
        self.es.__enter__()
        return self

    def __exit__(self, *a):
        self.kb.barrier()
        self.kb.dma_free = self.taken + self.kb.dma_free
        return self.es.__exit__(*a)

    def sb(self, name, shape, dt, dma=False):
        self.kb.uid += 1
        t = self.es.enter_context(self.kb.nc.sbuf_tensor(f"{name}_{self.kb.uid}", list(shape), dt))
        ds = None
        if dma == "pool" and os.environ.get("KB_SIM"):
            ds = self.kb._new_sem(f"pd{self.kb.uid}")
        elif dma:
            ds = self.kb.dma_free.pop(0)
            self.taken.append(ds)
        return Buf(t, name, ds)

    def ps(self, name, shape, dt=F32):
        self.kb.uid += 1
        t = self.es.enter_context(self.kb.nc.psum_tensor(f"{name}_{self.kb.uid}", list(shape), dt))
        return Buf(t, name)


class Ring:
    def __init__(self, bufs):
        self.bufs = bufs
        self.i = 0

    def next(self):
        b = self.bufs[self.i % len(self.bufs)]
        self.i += 1
        return b


def bcast_rows(ap1d, n, parts=128):
    return bass.AP(ap1d.tensor, ap1d.offset, [[0, parts], [1, n]])


C_ID, C_TRIU, C_NEGM, C_ONES, C_M01, C_REL, C_SEL = 0, 128, 256, 384, 512, 640, 896
NCONST = 1024


def make_consts():
    c = np.zeros((128, NCONST), np.float32)
    s = np.arange(128)[:, None]
    l = np.arange(128)[None, :]
    c[:, C_ID:C_ID + 128] = (s == l)
    c[:, C_TRIU:C_TRIU + 128] = (s <= l)
    c[:, C_NEGM:C_NEGM + 128] = np.where(l >= s, 0.0, NEG)
    c[:, C_ONES:C_ONES + 128] = 1.0
    c[:, C_M01:C_M01 + 128] = (l >= s)
    BIG = 1.0e4
    rel_prev = 128 + l - s
    rel_diag = l - s
    c[:, C_REL:C_REL + 128] = np.where(rel_prev <= 128, rel_prev, BIG)
    c[:, C_REL + 128:C_REL + 256] = np.where(rel_diag >= 0, rel_diag, BIG)
    c[127, C_SEL:C_SEL + 128] = 1.0
    return c


def alibi_slopes():
    j = np.arange(1, 73, dtype=np.float64)
    return np.exp2(-8.0 * j / 72.0).reshape(3, 24)


class Ctx:
    pass


def evac_engine(i):
    return "act" if i % 2 == 0 else "dve"


def copy_op(kb, e, out, in_, reads, writes=(), pwrites=(), scale=None):
    if e == "act":
        if scale is None:
            kb.op("act", lambda g: g.activation(out=out, in_=in_, func=ACT.Copy), reads, writes, pwrites)
        else:
            kb.op("act", lambda g: g.activation(out=out, in_=in_, func=ACT.Copy, scale=float(scale)),
                  reads, writes, pwrites)
    else:
        if scale is None:
            kb.op(e, lambda g: g.tensor_copy(out=out, in_=in_), reads, writes, pwrites)
        else:
            kb.op(e, lambda g: g.tensor_scalar(out=out, in0=in_, scalar1=float(scale), scalar2=None,
                                                op0=ALU.mult), reads, writes, pwrites)


def norm_tiles(kb, cx, ph, nb, x_rows_fn, xreg, gbc, hT, ntiles, t_off=0):
    for i in range(ntiles):
        xt = nb.xring.next()
        kb.dma("sp", xt[:, :], x_rows_fn(i), xt, True, regs=[xreg])
        ss = nb.ssring.next()
        rs = nb.rsring.next()
        junk = nb.junk
        kb.op("act", lambda g: g.activation(out=junk[:, :], in_=xt[:, :], func=ACT.Square, accum_out=ss[:, :]),
              reads=[xt], writes=[junk, ss])
        kb.op("dve", lambda g: g.tensor_scalar(out=ss[:, :], in0=ss[:, :], scalar1=1.0 / D, scalar2=EPS,
                                               op0=ALU.mult, op1=ALU.add), reads=[ss], writes=[ss])
        kb.op("pool", lambda g: g.tensor_tensor(out=rs[:, :], in0=ss[:, :], in1=cx.neghalf[:, :], op=ALU.pow),
              reads=[ss, cx.neghalf], writes=[rs])
        hb = nb.hbring.next()
        kb.op("dve", lambda g: g.scalar_tensor_tensor(out=hb[:, :], in0=xt[:, :], scalar=rs[:, :], in1=gbc[:, :],
                                                      op0=ALU.mult, op1=ALU.mult), reads=[xt, rs, gbc], writes=[hb])
        for q in range(4):
            pt = nb.tring.next()
            kb.pe([(lambda g, k=k: g.transpose(out=pt[:, k % 4, :], in_=hb[:, k * 128:(k + 1) * 128],
                                               identity=cx.identb[:, :])) for k in range(4 * q, 4 * q + 4)],
                  reads=[hb, cx.identb], writes=[pt])
            copy_op(kb, evac_engine(i), hT[:, 4 * q:4 * q + 4, t_off + i * 128:t_off + (i + 1) * 128], pt[:, :, :],
                    reads=[pt], pwrites=[hT])


class NormBufs:
    def __init__(self, ph):
        self.xring = Ring([ph.sb(f"xt{i}", [128, D], F32, dma=True) for i in range(2)])
        self.ssring = Ring([ph.sb(f"ss{i}", [128, 1], F32) for i in range(2)])
        self.rsring = Ring([ph.sb(f"rs{i}", [128, 1], F32) for i in range(2)])
        self.hbring = Ring([ph.sb(f"hb{i}", [128, D], BF16) for i in range(2)])
        self.junk = ph.sb("junk", [128, D], BF16)
        self.tring = Ring([ph.ps(f"tp{i}", [128, 4, 128], BF16) for i in range(2)])


def load_gain(kb, ph, g_ap, n, name):
    gb = ph.sb(name, [128, n], F32, dma=True)
    kb.dma("sp", gb[:, :], bcast_rows(g_ap, n), gb, True)
    return gb


def load_w(kb, wring, w_ap, c0, cw, kt=KT):
    wt = wring.next()
    src = w_ap.rearrange("(k p) c -> p k c", p=128)[:, :, c0:c0 + cw]
    kb.dma("pool", wt[:, :kt, :cw], src, wt, True)
    return wt


def mm_fm(kb, psb, wt, m, hT, n0, nw, kt=KT):
    kb.pe([(lambda g, k=k: g.matmul(psb[:, :nw], lhsT=wt[:, k, m * 128:(m + 1) * 128], rhs=hT[:, k, n0:n0 + nw],
                                    start=(k == 0), stop=(k == kt - 1))) for k in range(kt)],
          reads=[wt, hT], writes=[psb])


def mm_tm(kb, psb, wt, cw, hT, t0, kt=KT):
    kb.pe([(lambda g, k=k: g.matmul(psb[:, :cw], lhsT=hT[:, k, t0:t0 + 128], rhs=wt[:, k, :cw],
                                    start=(k == 0), stop=(k == kt - 1))) for k in range(kt)],
          reads=[wt, hT], writes=[psb])


def mem_prep(kb, cx, ph, mem_ap, g_ap):
    with kb.phase() as p2:
        nb = NormBufs(p2)
        gbc = load_gain(kb, p2, g_ap, D, "memg")
        norm_tiles(kb, cx, p2, nb, lambda i: mem_ap[i * 128:(i + 1) * 128, :], kb.region("mem"), gbc, cx.memnT, 2)


def mem_kv(kb, cx, ph, wkv_ap, mkT, mv):
    with kb.phase() as p2:
        wring = Ring([p2.sb(f"wkv{i}", [128, KT, 512], BF16, dma="pool") for i in range(2)])
        pring = Ring([p2.ps(f"pkv{i}", [128, 512]) for i in range(2)])
        for j in range(2):
            wt = load_w(kb, wring, wkv_ap, j * 512, 512)
            for m in range(4):
                psb = pring.next()
                mm_fm(kb, psb, wt, m, cx.memnT, 0, 256)
                copy_op(kb, evac_engine(m), mkT[:, j * 4 + m, :], psb[:, :256], reads=[psb], pwrites=[mkT])
        for j in range(2):
            wt = load_w(kb, wring, wkv_ap, 1024 + j * 512, 512)
            for i in range(2):
                psb = pring.next()
                mm_tm(kb, psb, wt, 512, cx.memnT, i * 128)
                copy_op(kb, evac_engine(i), mv[:, i, j * 512:(j + 1) * 512], psb[:, :], reads=[psb], pwrites=[mv])


def mem_attention(kb, cx, T, QM, ZM, G, mkT, mv):
    rQ, rZ, rG = kb.region("QM"), kb.region("ZS"), kb.region("G")
    TBm = 512
    with kb.phase() as ph:
        qring = Ring([ph.sb(f"qm{i}", [128, 8, TBm], BF16, dma=True) for i in range(2)])
        zring = Ring([ph.sb(f"zm{i}", [128, 8, TBm], BF16, dma=True) for i in range(2)])
        gring = Ring([ph.sb(f"gm{i}", [128, 8, TBm], BF16, dma=True) for i in range(2)])
        ptring = Ring([ph.sb(f"pt{i}", [128, 2, TBm], BF16) for i in range(2)])
        rdring = Ring([ph.sb(f"rd{i}", [128, TBm], F32) for i in range(2)])
        t1ring = Ring([ph.sb(f"t1{i}", [128, TBm], F32) for i in range(2)])
        sring = Ring([ph.ps(f"sps{i}", [128, TBm]) for i in range(3)])
        dring = Ring([ph.ps(f"dps{i}", [128, TBm]) for i in range(2)])
        oring = Ring([ph.ps(f"ops{i}", [128, TBm]) for i in range(3)])
        for tb in range(T // TBm):
            t0 = tb * TBm
            qm, zm, gm = qring.next(), zring.next(), gring.next()
            kb.dma("sp", qm[:, :, :], QM.rearrange("(e p) t -> p e t", p=128)[:, :, t0:t0 + TBm], qm, True, regs=[rQ])
            kb.dma("sp", zm[:, :, :], ZM.rearrange("(e p) t -> p e t", p=128)[:, :, t0:t0 + TBm], zm, True, regs=[rZ])
            for hm in range(4):
                pt = ptring.next()
                for mb in range(2):
                    sp = sring.next()
                    kb.pe([(lambda g, et=et: g.matmul(sp[:, :], lhsT=mkT[:, hm * 2 + et, mb * 128:(mb + 1) * 128],
                                                      rhs=qm[:, hm * 2 + et, :], start=(et == 0), stop=(et == 1)))
                           for et in range(2)], reads=[mkT, qm], writes=[sp])
                    kb.op("act", lambda g: g.activation(out=pt[:, mb, :], in_=sp[:, :], func=ACT.Exp),
                          reads=[sp], pwrites=[pt])
                dp = dring.next()
                kb.pe([(lambda g, mb=mb: g.matmul(dp[:, :], lhsT=cx.onesb[:, :], rhs=pt[:, mb, :],
                                                  start=(mb == 0), stop=(mb == 1))) for mb in range(2)],
                      reads=[cx.onesb, pt], writes=[dp])
                rd = rdring.next()
                kb.op("dve", lambda g: g.reciprocal(out=rd[:, :], in_=dp[:, :]), reads=[dp], writes=[rd])
                for e2 in range(2):
                    op_ = oring.next()
                    kb.pe([(lambda g, mb=mb: g.matmul(op_[:, :], lhsT=mv[:, mb, hm * 256 + e2 * 128:hm * 256 + (e2 + 1) * 128],
                                                      rhs=pt[:, mb, :], start=(mb == 0), stop=(mb == 1)))
                           for mb in range(2)], reads=[mv, pt], writes=[op_])
                    t1 = t1ring.next()
                    kb.op("dve", lambda g: g.tensor_tensor(out=t1[:, :], in0=op_[:, :], in1=rd[:, :], op=ALU.mult),
                          reads=[op_, rd], writes=[t1])
                    kb.op("pool", lambda g: g.tensor_tensor(out=gm[:, hm * 2 + e2, :], in0=t1[:, :],
                                                            in1=zm[:, hm * 2 + e2, :], op=ALU.mult),
                          reads=[t1, zm], pwrites=[gm])
            kb.dma("sp", G[TOKW:MIXW, :].rearrange("(e p) t -> p e t", p=128)[:, :, t0:t0 + TBm], gm[:, :, :], gm, False,
                   regs=[rG])


def out_proj(kb, cx, T, G, wout_ap, x_in, x_out, rin, rout):
    rG = kb.region("G")
    TBo = min(1024, T)
    NTo = TBo // 128
    KO = MIXW // 128
    with kb.phase() as ph:
        gt = ph.sb("gt", [128, KO, TBo], BF16, dma=True)
        wring = Ring([ph.sb(f"wo{i}", [128, KO, 512], BF16, dma="pool") for i in range(2)])
        xring = Ring([ph.sb(f"xo{i}", [128, 512], F32, dma=True) for i in range(6)])
        pring = Ring([ph.ps(f"pso{i}", [128, 512]) for i in range(4)])
        for tb in range(T // TBo):
            t0 = tb * TBo
            kb.dma("sp", gt[:, :, :], G.rearrange("(k p) t -> p k t", p=128)[:, :, t0:t0 + TBo], gt, True, regs=[rG])
            for n in range(4):
                wt = load_w(kb, wring, wout_ap, n * 512, 512, kt=KO)
                for i in range(NTo):
                    xp = xring.next()
                    rows = slice(t0 + i * 128, t0 + (i + 1) * 128)
                    kb.dma("sp", xp[:, :], x_in[rows, n * 512:(n + 1) * 512], xp, True, regs=[rin])
                    psb = pring.next()
                    kb.pe([(lambda g, k=k: g.matmul(psb[:, :], lhsT=gt[:, k, i * 128:(i + 1) * 128], rhs=wt[:, k, :],
                                                    start=(k == 0), stop=(k == KO - 1))) for k in range(KO)],
                          reads=[gt, wt], writes=[psb])
                    kb.op("dve", lambda g: g.tensor_tensor(out=xp[:, :], in0=psb[:, :], in1=xp[:, :], op=ALU.add),
                          reads=[psb, xp], writes=[xp])
                    kb.dma("sp", x_out[rows, n * 512:(n + 1) * 512], xp[:, :], xp, False, regs=[rout])


def final_norm(kb, cx, T, x_in, rin, g_ap, y_out):
    with kb.phase() as ph:
        gbc = load_gain(kb, ph, g_ap, D, "fg")
        xring = Ring([ph.sb(f"fx{i}", [128, D], F32, dma=True) for i in range(3)])
        ssring = Ring([ph.sb(f"fs{i}", [128, 1], F32) for i in range(2)])
        rsring = Ring([ph.sb(f"fr{i}", [128, 1], F32) for i in range(2)])
        junk = ph.sb("fjunk", [128, D], BF16)
        ry = kb.region("Y")
        for i in range(T // 128):
            xt, ss, rs = xring.next(), ssring.next(), rsring.next()
            kb.dma("sp", xt[:, :], x_in[i * 128:(i + 1) * 128, :], xt, True, regs=[rin])
            kb.op("act", lambda g: g.activation(out=junk[:, :], in_=xt[:, :], func=ACT.Square, accum_out=ss[:, :]),
                  reads=[xt], writes=[junk, ss])
            kb.op("dve", lambda g: g.tensor_scalar(out=ss[:, :], in0=ss[:, :], scalar1=1.0 / D, scalar2=EPS,
                                                   op0=ALU.mult, op1=ALU.add), reads=[ss], writes=[ss])
            kb.op("pool", lambda g: g.tensor_tensor(out=rs[:, :], in0=ss[:, :], in1=cx.neghalf[:, :], op=ALU.pow),
                  reads=[ss, cx.neghalf], writes=[rs])
            kb.op("dve", lambda g: g.scalar_tensor_tensor(out=xt[:, :], in0=xt[:, :], scalar=rs[:, :], in1=gbc[:, :],
                                                          op0=ALU.mult, op1=ALU.mult), reads=[xt, rs, gbc], writes=[xt])
            kb.dma("sp", y_out[i * 128:(i + 1) * 128, :], xt[:, :], xt, False, regs=[ry], final=True)


def pipeline_steps(tasks):
    if not tasks:
        return []
    ns = max(len(t) for t in tasks)

    def mk(step):
        def f():
            for k in range(ns):
                i = step - k
                if 0 <= i < len(tasks) and k < len(tasks[i]):
                    tasks[i][k]()
        return f
    return [mk(step) for step in range(len(tasks) + ns - 1)]


def interleave(a, b):
    na, nb = len(a), len(b)
    ia = ib = 0
    while ia < na or ib < nb:
        if ib >= nb or (ia < na and ia * nb <= ib * na):
            a[ia]()
            ia += 1
        else:
            b[ib]()
            ib += 1


def run_pipeline(tasks):
    if not tasks:
        return
    ns = max(len(t) for t in tasks)
    for step in range(len(tasks) + ns - 1):
        for k in range(ns):
            i = step - k
            if 0 <= i < len(tasks) and k < len(tasks[i]):
                tasks[i][k]()


def _xbc_task(kb, cx, j, m, n, shared, first, last, zero_hist, tok0, TB, env):
    st = {}
    ct = j * 4 + m
    hist, cwsb, cbsb = env["hist"], env["cwsb"], env["cbsb"]
    S, rS = env["S"], env["rS"]

    def s0():
        if first:
            shared["wt"] = load_w(kb, env["wring"], env["w_in"], j * 512, 512)
            shared["ofm"] = env["ofring"].next()
            shared["otm"] = env["otring"].next() if j < 8 else None
        psb = env["pring"].next()
        mm_fm(kb, psb, shared["wt"], m, env["hT"], n * 512, 512)
        U = env["uring"].next()
        st["U"] = U
        if zero_hist:
            kb.op("pool", lambda g: g.memset(U[:, 0:3], 0.0), pwrites=[U])
        else:
            kb.op("act", lambda g: g.activation(out=U[:, 0:3], in_=hist[:, ct, :], func=ACT.Copy),
                  reads=[hist], pwrites=[U])
        kb.op("act", lambda g: g.activation(out=U[:, 3:515], in_=psb[:, :], func=ACT.Copy), reads=[psb], pwrites=[U])
        kb.op("act", lambda g: g.activation(out=hist[:, ct, :], in_=U[:, 512:515], func=ACT.Copy),
              reads=[U], pwrites=[hist])

    def s1():
        U = st["U"]
        acc = env["aring"].next()
        kb.op("dve", lambda g: g.tensor_scalar(out=acc[:, :], in0=U[:, 0:512], scalar1=cwsb[:, ct, 0:1],
                                               scalar2=cbsb[:, ct:ct + 1], op0=ALU.mult, op1=ALU.add),
              reads=[U, cwsb, cbsb], writes=[acc])
        for k in range(1, 4):
            kb.op("dve", lambda g, k=k: g.scalar_tensor_tensor(
                out=acc[:, :], in0=U[:, k:k + 512], scalar=cwsb[:, ct, k:k + 1], in1=acc[:, :],
                op0=ALU.mult, op1=ALU.add), reads=[U, cwsb, acc], writes=[acc])
        osb = env["osring"].next()
        st["osb"] = osb
        kb.op("act", lambda g: g.activation(out=osb[:, :], in_=acc[:, :], func=ACT.Silu), reads=[acc], writes=[osb])

    def s2():
        osb = st["osb"]
        ofm, otm = shared["ofm"], shared["otm"]
        if j >= 6:
            kb.op("pool", lambda g: g.tensor_copy(out=ofm[:, m, n * 512:(n + 1) * 512], in_=osb[:, :]),
                  reads=[osb], pwrites=[ofm])
        if j < 8:
            pt = env["tring"].next()
            kb.pe([(lambda g, q=q: g.transpose(out=pt[:, q, :], in_=osb[:, q * 128:(q + 1) * 128],
                                               identity=cx.identb[:, :])) for q in range(4)],
                  reads=[osb, cx.identb], writes=[pt])
            kb.op("dve", lambda g: g.tensor_copy(out=otm[:, n * 4:(n + 1) * 4, m * 128:(m + 1) * 128], in_=pt[:, :, :]),
                  reads=[pt], pwrites=[otm])
        if last:
            if j < 8:
                dst = S["TOK"][tok0:tok0 + TB, j * 512:(j + 1) * 512]
                kb.dma("sp", dst.rearrange("(s p) c -> p s c", p=128), otm[:, :, :], otm, False, regs=[rS])
            if j >= 6:
                r0 = (j - 6) * 512
                kb.dma("sp", S["BCT"][r0:r0 + 512, tok0:tok0 + TB].rearrange("(m p) t -> p m t", p=128), ofm[:, :, :], ofm,
                       False, regs=[rS])

    return [s0, s1, s2]


def ssd_proj(kb, cx, T, x_in, rin, P, S):
    TB = min(1024, T)
    NT = TB // 128
    NS = TB // 512
    w_in = P["w_in"]
    rS = kb.region("SSD")
    rQ, rZ = kb.region("QM"), kb.region("ZS")
    with kb.phase() as ph:
        nb = NormBufs(ph)
        gbc = load_gain(kb, ph, P["norm_g"], D, "ng")
        hT = ph.sb("hT", [128, KT, TB], BF16)
        wring = Ring([ph.sb(f"w{i}", [128, KT, 512], BF16, dma="pool") for i in range(3)])
        pring = Ring([ph.ps(f"pp{i}", [128, 512]) for i in range(4)])
        tring = Ring([ph.ps(f"tq{i}", [128, 4, 128], BF16) for i in range(2)])
        cwsb = ph.sb("cw", [128, 40, 4], F32, dma=True)
        cbsb = ph.sb("cb", [128, 40], F32, dma=True)
        kb.dma("sp", cwsb[:, :, :], P["conv_w"], cwsb, True)
        kb.dma("sp", cbsb[:, :], P["conv_b"], cbsb, True)
        dtb = load_gain(kb, ph, P["dt_bias"], 48, "dtb")
        abc = load_gain(kb, ph, P["a_log"], 48, "abc")
        kb.op("act", lambda g: g.activation(out=abc[:, :], in_=abc[:, :], func=ACT.Exp), reads=[abc], writes=[abc])
        kb.op("dve", lambda g: g.tensor_scalar(out=abc[:, :], in0=abc[:, :], scalar1=-1.0, scalar2=None, op0=ALU.mult),
              reads=[abc], writes=[abc])
        hist = ph.sb("hist", [128, 40, 3], F32)
        uring = Ring([ph.sb(f"U{i}", [128, 515], F32) for i in range(3)])
        aring = Ring([ph.sb(f"acc{i}", [128, 512], F32) for i in range(2)])
        osring = Ring([ph.sb(f"os{i}", [128, 512], BF16) for i in range(3)])
        ofring = Ring([ph.sb(f"ofm{i}", [128, 4, TB], BF16, dma=True) for i in range(2)])
        otring = Ring([ph.sb(f"otm{i}", [128, NT, 512], BF16, dma=True) for i in range(2)])
        dta_all = ph.sb("dtaall", [128, NT, 96], F32, dma=True)
        dtx = ph.sb("dtx", [128, 48], F32)
        dta = ph.sb("dta", [128, 48], F32)
        for tb in range(T // TB):
            tok0 = tb * TB
            norm_tiles(kb, cx, ph, nb, lambda i: x_in[tok0 + i * 128:tok0 + (i + 1) * 128, :], rin, gbc, hT, NT)
            tasks = []
            for j in range(10):
                shared = {}
                for m in range(4):
                    for n in range(NS):
                        tasks.append(_xbc_task(kb, cx, j, m, n, shared, first=(m == 0 and n == 0),
                                               last=(m == 3 and n == NS - 1), zero_hist=(tb == 0 and n == 0),
                                               tok0=tok0, TB=TB, env=dict(wring=wring, ofring=ofring, otring=otring,
                                                                          pring=pring, uring=uring, aring=aring,
                                                                          osring=osring, tring=tring, hist=hist,
                                                                          cwsb=cwsb, cbsb=cbsb, hT=hT, w_in=w_in, S=S, rS=rS)))
            run_pipeline(tasks)
            wt = load_w(kb, wring, w_in, 5120, 48)
            for i in range(NT):
                psb = pring.next()
                mm_tm(kb, psb, wt, 48, hT, i * 128)
                kb.op("dve", lambda g: g.tensor_tensor(out=dtx[:, :], in0=psb[:, :48], in1=dtb[:, :], op=ALU.add),
                      reads=[psb, dtb], writes=[dtx])
                kb.op("act", lambda g: g.activation(out=dtx[:, :], in_=dtx[:, :], func=ACT.Exp), reads=[dtx], writes=[dtx])
                kb.op("act", lambda g: g.activation(out=dta_all[:, i, 0:48], in_=dtx[:, :], func=ACT.Ln, bias=1.0),
                      reads=[dtx], pwrites=[dta_all])
                kb.op("dve", lambda g: g.tensor_tensor(out=dta[:, :], in0=dta_all[:, i, 0:48], in1=abc[:, :], op=ALU.mult),
                      reads=[dta_all, abc], writes=[dta])
                p1 = pring.next()
                kb.pe([lambda g: g.matmul(p1[:, :48], lhsT=cx.cst[:, C_TRIU:C_TRIU + 128], rhs=dta[:, :],
                                          start=True, stop=True)], reads=[cx.cst, dta], writes=[p1])
                kb.op("dve", lambda g: g.tensor_copy(out=dta_all[:, i, 48:96], in_=p1[:, :48]), reads=[p1], pwrites=[dta_all])
            kb.dma("sp", S["DTA"][tok0:tok0 + TB, :].rearrange("(i p) h -> p i h", p=128), dta_all[:, :, :], dta_all, False, regs=[rS])
            for j in range(2):
                wt = load_w(kb, wring, w_in, 5168 + j * 512, 512)
                ofm = ofring.next()
                for m in range(4):
                    for n in range(NS):
                        psb = pring.next()
                        mm_fm(kb, psb, wt, m, hT, n * 512, 512)
                        copy_op(kb, evac_engine(n), ofm[:, m, n * 512:(n + 1) * 512], psb[:, :], reads=[psb], pwrites=[ofm],
                                scale=1.0 / 16.0)
                kb.dma("sp", S["QM"][j * 512:(j + 1) * 512, tok0:tok0 + TB].rearrange("(m p) t -> p m t", p=128),
                       ofm[:, :, :], ofm, False, regs=[rQ])
            for j in range(6):
                wt = load_w(kb, wring, w_in, 6192 + j * 512, 512)
                otm = otring.next()
                for i in range(NT):
                    psb = pring.next()
                    mm_tm(kb, psb, wt, 512, hT, i * 128)
                    kb.op("act", lambda g: g.activation(out=otm[:, i, :], in_=psb[:, :], func=ACT.Silu),
                          reads=[psb], pwrites=[otm])
                kb.dma("sp", S["TOK"][tok0:tok0 + TB, 4096 + j * 512:4096 + (j + 1) * 512].rearrange("(s p) c -> p s c", p=128),
                       otm[:, :, :], otm, False, regs=[rS])
            for j in range(2):
                wt = load_w(kb, wring, w_in, 9264 + j * 512, 512)
                ofm = ofring.next()
                for m in range(4):
                    for n in range(NS):
                        psb = pring.next()
                        mm_fm(kb, psb, wt, m, hT, n * 512, 512)
                        kb.op("act", lambda g: g.activation(out=ofm[:, m, n * 512:(n + 1) * 512], in_=psb[:, :], func=ACT.Silu),
                              reads=[psb], pwrites=[ofm])
                kb.dma("sp", S["ZS"][TOKW + j * 512:TOKW + (j + 1) * 512, tok0:tok0 + TB].rearrange("(m p) t -> p m t", p=128),
                       ofm[:, :, :], ofm, False, regs=[rZ])


def bc_last(ap2d, n):
    return ap2d.unsqueeze(2).broadcast_to([ap2d.shape[0], ap2d.shape[1], n])


def ssd_scan(kb, cx, T, P, S, G):
    rS, rG = kb.region("SSD"), kb.region("G")
    NCH = T // 128
    with kb.phase() as ph:
        dsk = load_gain(kb, ph, P["d_skip"], 48, "dsk")
        g2 = ph.sb("g2", [128, 24], F32, dma=True)
        kb.dma("sp", g2[:, :], P["ssd_norm_g"], g2, True)
        tokr = Ring([ph.sb(f"tok{i}", [128, 7168], BF16, dma=True) for i in range(2)])
        bcr = Ring([ph.sb(f"bct{i}", [128, 16, 128], BF16, dma=True) for i in range(2)])
        dtar = Ring([ph.sb(f"dta{i}", [128, 96], F32, dma=True) for i in range(3)])
        decr = Ring([ph.sb(f"dec{i}", [128, 48, 128], BF16) for i in range(2)])
        MTr = Ring([ph.sb(f"MT{i}", [128, 48, 128], BF16) for i in range(2)])
        xdtr = Ring([ph.sb(f"xdt{i}", [128, 48, 64], BF16) for i in range(2)])
        xdwr = Ring([ph.sb(f"xdtw{i}", [128, 48, 64], BF16) for i in range(2)])
        xsDr = Ring([ph.sb(f"xsD{i}", [128, 48, 64], BF16) for i in range(2)])
        einr = Ring([ph.sb(f"ein{i}", [128, 48], F32) for i in range(3)])
        cdr = Ring([ph.sb(f"cd{i}", [128, 48], F32) for i in range(3)])
        nacr = Ring([ph.sb(f"nacs{i}", [128, 48], F32) for i in range(2)])
        t48r = Ring([ph.sb(f"t48{i}", [128, 48], F32) for i in range(2)])
        hst = [ph.sb(f"hst{g}", [128, 6, 64], F32) for g in range(8)]
        hbf = [ph.sb(f"hbf{g}", [128, 384], BF16) for g in range(8)]
        dtw = ph.sb("dtw", [128, 48], F32)
        t1r = Ring([ph.sb(f"t1{i}", [128, 6, 64], F32) for i in range(2)])
        t2r = Ring([ph.sb(f"t2{i}", [128, 384], F32) for i in range(2)])
        gtr = Ring([ph.sb(f"gt{i}", [128, 384], F32) for i in range(4)])
        ssr = Ring([ph.sb(f"sq{i}", [128, 1], F32) for i in range(4)])
        rsr = Ring([ph.sb(f"rq{i}", [128, 1], F32) for i in range(4)])
        jkr = Ring([ph.sb(f"sjunk{i}", [128, 384], BF16) for i in range(2)])
        gated = ph.sb("gated", [128, TOKW], BF16)
        gfm = ph.sb("gfm", [128, 24, 128], BF16, dma=True)
        rpr = Ring([ph.ps(f"rp{i}", [128, 4, 128]) for i in range(2)])
        cbr = Ring([ph.ps(f"cbp{i}", [128, 4, 128]) for i in range(1)])
        ydr = Ring([ph.ps(f"yd{i}", [128, 6, 64]) for i in range(2)])
        ysr = Ring([ph.ps(f"ys{i}", [128, 6, 64]) for i in range(1)])
        tpr = Ring([ph.ps(f"tg{i}", [128, 4, 128], BF16) for i in range(1)])
        cdp = ph.ps("cdp", [128, 48])
        identf = cx.cst[:, C_ID:C_ID + 128]
        negmb = cx.cstb.t[:, C_NEGM:C_NEGM + 128]
        negm = cx.cst[:, C_NEGM:C_NEGM + 128]

        def make(c):
            st = {}
            t0 = c * 128

            def P1_pieces():
                pieces = []
                loc = {}

                def pro():
                    P1_pro(loc)
                pieces.append(pro)
                for q4 in range(12):
                    pieces.append(lambda q4=q4: P1_grp(loc, q4))
                pieces.append(lambda: P1_fin(loc))
                return pieces

            def P1_pro(loc):
                dta, ein, cd, nacs, dec = dtar.next(), einr.next(), cdr.next(), nacr.next(), decr.next()
                st.update(ein=ein, cd=cd, dec=dec, dta=dta)
                loc.update(dta=dta, ein=ein, cd=cd, nacs=nacs, dec=dec)
                kb.dma("sp", dta[:, :], S["DTA"][t0:t0 + 128, :], dta, True, regs=[rS])
                acs = dta[:, 48:96]
                kb.op("dve", lambda g: g.tensor_scalar(out=nacs[:, :], in0=acs, scalar1=-1.0, scalar2=None, op0=ALU.mult),
                      reads=[dta], writes=[nacs])
                kb.op("act", lambda g: g.activation(out=ein[:, :], in_=acs, func=ACT.Exp), reads=[dta], writes=[ein])

            def P1_grp(loc, q4):
                dta, nacs, dec = loc["dta"], loc["nacs"], loc["dec"]
                if True:
                    rp = rpr.next()
                    fns = []
                    for a in range(4):
                        hh = q4 * 4 + a
                        fns.append(lambda g, a=a: g.matmul(rp[:, a, :], lhsT=cx.identb[:, :], rhs=negmb, start=True, stop=False))
                        fns.append(lambda g, a=a, hh=hh: g.matmul(rp[:, a, :], lhsT=dta[:, 48 + hh:49 + hh].to_broadcast([128, 128]),
                                                                  rhs=identf, start=False, stop=False))
                        fns.append(lambda g, a=a, hh=hh: g.matmul(rp[:, a, :], lhsT=identf,
                                                                  rhs=nacs[:, hh:hh + 1].to_broadcast([128, 128]),
                                                                  start=False, stop=True))
                    kb.pe(fns, reads=[cx.cst, cx.cstb, dta, nacs], writes=[rp])
                    kb.op("act", lambda g: g.activation(out=dec[:, q4 * 4:(q4 + 1) * 4, :], in_=rp[:, :, :], func=ACT.Exp),
                          reads=[rp], pwrites=[dec])
            def P1_fin(loc):
                dta, cd = loc["dta"], loc["cd"]
                kb.pe([lambda g: g.matmul(cdp[:, :], lhsT=cx.cst[:, C_SEL:C_SEL + 128], rhs=dta[:, 48:96], start=True, stop=True)],
                      reads=[cx.cst, dta], writes=[cdp])
                kb.op("act", lambda g: g.activation(out=cd[:, :], in_=cdp[:, :], func=ACT.Exp), reads=[cdp], writes=[cd])

            def P2():
                tok, bct = tokr.next(), bcr.next()
                MT, xdt, xdtw, xsD = MTr.next(), xdtr.next(), xdwr.next(), xsDr.next()
                dec, dta = st["dec"], st["dta"]
                st.update(tok=tok, bct=bct, MT=MT, xdt=xdt, xdtw=xdtw, xsD=xsD)
                kb.dma("sp", tok[:, :], S["TOK"][t0:t0 + 128, :], tok, True, regs=[rS])
                kb.dma("sp", bct[:, :, :], S["BCT"][:, t0:t0 + 128].rearrange("(g p) t -> p g t", p=128), bct, True, regs=[rS])
                xs = tok[:, 0:3072].rearrange("p (h e) -> p h e", e=64)
                dtc = dta[:, 0:48]
                kb.op("dve", lambda g: g.tensor_tensor(out=dtw[:, :], in0=dtc, in1=dec[:, :, 127], op=ALU.mult),
                      reads=[dta, dec], writes=[dtw])
                kb.op("pool", lambda g: g.tensor_tensor(out=xdt[:, :, :], in0=xs, in1=bc_last(dtc, 64),
                                                        op=ALU.mult), reads=[tok, dta], writes=[xdt])
                kb.op("pool", lambda g: g.tensor_tensor(out=xsD[:, :, :], in0=xs, in1=bc_last(dsk[:, :], 64),
                                                        op=ALU.mult), reads=[tok, dsk], writes=[xsD])
                kb.op("pool", lambda g: g.tensor_tensor(out=xdtw[:, :, :], in0=xs, in1=bc_last(dtw[:, :], 64),
                                                        op=ALU.mult), reads=[tok, dtw], writes=[xdtw])
                for hf in range(2):
                    cb = cbr.next()
                    kb.pe([(lambda g, q=q: g.matmul(cb[:, q, :], lhsT=bct[:, hf * 4 + q, :], rhs=bct[:, 8 + hf * 4 + q, :],
                                                    start=True, stop=True)) for q in range(4)], reads=[bct], writes=[cb])
                    kb.op("dve", lambda g: g.tensor_tensor(
                        out=MT[:, hf * 24:(hf + 1) * 24, :].rearrange("p (q j) l -> p q j l", j=6),
                        in0=dec[:, hf * 24:(hf + 1) * 24, :].rearrange("p (q j) l -> p q j l", j=6),
                        in1=cb[:, :, :].unsqueeze(2).broadcast_to([128, 4, 6, 128]), op=ALU.mult),
                        reads=[dec, cb], pwrites=[MT])

            def B():
                tok, bct, MT, xdt, xdtw, xsD, ein, cd = (st[k] for k in ("tok", "bct", "MT", "xdt", "xdtw", "xsD", "ein", "cd"))

                def group(gi):
                    gs = {}

                    def G1():
                        yd = ydr.next()
                        fns = [lambda g: g.matmul(yd[:, :, :], lhsT=cx.identb[:, :], rhs=xsD[:, gi * 6:(gi + 1) * 6, :],
                                                  start=True, stop=False)]
                        for hh in range(6):
                            h = gi * 6 + hh
                            fns.append(lambda g, h=h, hh=hh: g.matmul(yd[:, hh, :], lhsT=MT[:, h, :], rhs=xdt[:, h, :],
                                                                      start=False, stop=(hh == 5)))
                        kb.pe(fns, reads=[cx.identb, xsD, MT, xdt], writes=[yd])
                        gt = gtr.next()
                        gs["gt"] = gt
                        ydf = yd[:, :, :].rearrange("p h e -> p (h e)")
                        zsl = tok[:, 4096 + gi * 384:4096 + (gi + 1) * 384]
                        if c > 0:
                            yo = ysr.next()
                            kb.pe([lambda g: g.matmul(yo[:, :, :], lhsT=bct[:, 8 + gi, :], rhs=hbf[gi][:, :], start=True, stop=True)],
                                  reads=[bct, hbf[gi]], writes=[yo])
                            t1 = t1r.next()
                            kb.op("dve", lambda g: g.tensor_tensor(out=t1[:, :, :], in0=yo[:, :, :],
                                                                   in1=bc_last(ein[:, gi * 6:(gi + 1) * 6], 64), op=ALU.mult),
                                  reads=[yo, ein], writes=[t1])
                            t2 = t2r.next()
                            kb.op("dve", lambda g: g.tensor_tensor(out=t2[:, :], in0=ydf, in1=t1[:, :, :].rearrange("p h e -> p (h e)"),
                                                                   op=ALU.add), reads=[yd, t1], writes=[t2])
                            kb.op("dve", lambda g: g.tensor_tensor(out=gt[:, :], in0=t2[:, :], in1=zsl, op=ALU.mult),
                                  reads=[t2, tok], writes=[gt])
                        else:
                            kb.op("dve", lambda g: g.tensor_tensor(out=gt[:, :], in0=ydf, in1=zsl, op=ALU.mult),
                                  reads=[yd, tok], writes=[gt])
                        ss = ssr.next()
                        gs["ss"] = ss
                        jk = jkr.next()
                        kb.op("act", lambda g: g.activation(out=jk[:, :], in_=gt[:, :], func=ACT.Square, accum_out=ss[:, :]),
                              reads=[gt], writes=[jk, ss])

                    def G2():
                        gt, ss = gs["gt"], gs["ss"]
                        rs = rsr.next()
                        kb.op("dve", lambda g: g.tensor_scalar(out=ss[:, :], in0=ss[:, :], scalar1=1.0 / 384.0, scalar2=EPS,
                                                               op0=ALU.mult, op1=ALU.add), reads=[ss], writes=[ss])
                        kb.op("pool", lambda g: g.tensor_tensor(out=rs[:, :], in0=ss[:, :], in1=cx.neghalf[:, :], op=ALU.pow),
                              reads=[ss, cx.neghalf], writes=[rs])
                        gs["rs"] = rs
                        if c < NCH - 1:
                            sps = ysr.next()
                            kb.pe([lambda g: g.matmul(sps[:, :, :], lhsT=tok[:, 3072 + gi * 128:3072 + (gi + 1) * 128],
                                                      rhs=xdtw[:, gi * 6:(gi + 1) * 6, :], start=True, stop=True)],
                                  reads=[tok, xdtw], writes=[sps])
                            if c > 0:
                                kb.op("pool", lambda g: g.tensor_tensor(out=hst[gi][:, :, :], in0=hst[gi][:, :, :],
                                                                        in1=bc_last(cd[:, gi * 6:(gi + 1) * 6], 64), op=ALU.mult),
                                      reads=[hst[gi], cd], writes=[hst[gi]])
                                kb.op("dve", lambda g: g.tensor_tensor(out=hst[gi][:, :, :], in0=sps[:, :, :], in1=hst[gi][:, :, :],
                                                                       op=ALU.add), reads=[sps, hst[gi]], writes=[hst[gi]])
                            else:
                                kb.op("dve", lambda g: g.tensor_copy(out=hst[gi][:, :, :], in_=sps[:, :, :]), reads=[sps], writes=[hst[gi]])

                    def G3():
                        gt, rs = gs["gt"], gs["rs"]
                        kb.op("act", lambda g: g.activation(out=gated[:, gi * 384:(gi + 1) * 384], in_=gt[:, :], func=ACT.Copy,
                                                            scale=rs[:, :]), reads=[gt, rs], pwrites=[gated])
                        if c < NCH - 1:
                            kb.op("act", lambda g: g.activation(out=hbf[gi][:, :], in_=hst[gi][:, :, :].rearrange("p h e -> p (h e)"),
                                                                func=ACT.Copy), reads=[hst[gi]], writes=[hbf[gi]])

                    return [G1, G2, G3]

                steps = pipeline_steps([group(gi) for gi in range(8)])
                def tail():
                  for q in range(6):
                    tp = tpr.next()
                    kb.pe([(lambda g, j=j: g.transpose(out=tp[:, j, :], in_=gated[:, (4 * q + j) * 128:(4 * q + j + 1) * 128],
                                                       identity=cx.identb[:, :])) for j in range(4)],
                          reads=[gated, cx.identb], writes=[tp])
                    kb.op("dve", lambda g: g.tensor_tensor(out=gfm[:, 4 * q:4 * q + 4, :], in0=tp[:, :, :],
                                                           in1=bc_last(g2[:, 4 * q:4 * q + 4], 128), op=ALU.mult),
                          reads=[tp, g2], pwrites=[gfm])
                  kb.dma("sp", G[0:TOKW, t0:t0 + 128].rearrange("(ct p) t -> p ct t", p=128), gfm[:, :, :], gfm, False, regs=[rG])
                return steps + [tail]

            return P1_pieces, P2, B

        chunks = [make(c) for c in range(NCH)]
        for step in range(NCH + 2):
            a = chunks[step][0]() if step < NCH else []
            b = []
            if 0 <= step - 1 < NCH:
                b.append(chunks[step - 1][1])
            if 0 <= step - 2 < NCH:
                b += chunks[step - 2][2]()
            interleave(a, b)


def attn_proj(kb, cx, T, x_in, rin, P, S):
    TB = min(1024, T)
    NT = TB // 128
    NS = TB // 512
    w_in = P["w_in"]
    rA, rQ, rZ = kb.region("ATT"), kb.region("QM"), kb.region("ZS")
    with kb.phase() as ph:
        nb = NormBufs(ph)
        gbc = load_gain(kb, ph, P["norm_g"], D, "ng")
        hT = ph.sb("hT", [128, KT, TB], BF16)
        wring = Ring([ph.sb(f"w{i}", [128, KT, 512], BF16, dma="pool") for i in range(3)])
        pring = Ring([ph.ps(f"pp{i}", [128, 512]) for i in range(6)])
        ofring = Ring([ph.sb(f"ofm{i}", [128, 4, TB], BF16, dma=True) for i in range(2)])
        otring = Ring([ph.sb(f"otm{i}", [128, NT, 512], BF16, dma=True) for i in range(2)])
        items = [(gi, d, tb) for gi, (_, d) in enumerate(DIL) for tb in range(T // TB)]
        hTs = [hT, ph.sb("hT2", [128, KT, TB], BF16)]

        def do_norm(idx):
            gi, d, tb = items[idx]
            nsub = T // d
            xr = x_in.rearrange("(i r) c -> r i c", r=d)
            tok0 = tb * TB

            def rows(i):
                tp = tok0 + i * 128
                r, i0 = tp // nsub, tp % nsub
                return xr[r, i0:i0 + 128, :]

            norm_tiles(kb, cx, ph, nb, rows, rin, gbc, hTs[idx % 2], NT)

        do_norm(0)
        for idx, (gi, d, tb) in enumerate(items):
            hTc = hTs[idx % 2]
            tok0 = tb * TB
            base = gi * 9216
            segs = [("q", base, 6), ("k", base + 3072, 6), ("v", base + 6144, 6)]
            if gi == 0:
                segs += [("qm", 27648, 2), ("z", 28672, 8)]
            wtiles = [(kind, c0, j) for kind, c0, ntile in segs for j in range(ntile)]
            for wi, (kind, c0, j) in enumerate(wtiles):
                if wi == len(wtiles) // 2 and idx + 1 < len(items):
                    do_norm(idx + 1)
                wt = load_w(kb, wring, w_in, c0 + j * 512, 512)
                if kind == "v":
                    otm = otring.next()
                    for i in range(NT):
                        psb = pring.next()
                        mm_tm(kb, psb, wt, 512, hTc, i * 128)
                        copy_op(kb, evac_engine(i), otm[:, i, :], psb[:, :], reads=[psb], pwrites=[otm])
                    kb.dma("sp", S["V"][gi][tok0:tok0 + TB, j * 512:(j + 1) * 512].rearrange("(s p) c -> p s c", p=128),
                           otm[:, :, :], otm, False, regs=[rA])
                    continue
                ofm = ofring.next()
                for m in range(4):
                    for n in range(NS):
                        psb = pring.next()
                        mm_fm(kb, psb, wt, m, hTc, n * 512, 512)
                        dst = ofm[:, m, n * 512:(n + 1) * 512]
                        if kind == "z":
                            kb.op("act", lambda g: g.activation(out=dst, in_=psb[:, :], func=ACT.Silu),
                                  reads=[psb], pwrites=[ofm])
                        elif kind == "q":
                            copy_op(kb, evac_engine(n + m), dst, psb[:, :], reads=[psb], pwrites=[ofm], scale=128.0 ** -0.5)
                        elif kind == "qm":
                            copy_op(kb, evac_engine(n + m), dst, psb[:, :], reads=[psb], pwrites=[ofm], scale=1.0 / 16.0)
                        else:
                            copy_op(kb, evac_engine(n + m), dst, psb[:, :], reads=[psb], pwrites=[ofm])
                if kind == "q":
                    dt_, rg = S["Q"][gi], rA
                elif kind == "k":
                    dt_, rg = S["K"][gi], rA
                elif kind == "qm":
                    dt_, rg = S["QM"], rQ
                else:
                    dt_, rg = S["ZS"], rZ
                kb.dma("sp", dt_[j * 512:(j + 1) * 512, tok0:tok0 + TB].rearrange("(m p) t -> p m t", p=128),
                       ofm[:, :, :], ofm, False, regs=[rg])


def attn_core(kb, cx, T, S, G):
    rA, rZ, rG = kb.region("ATT"), kb.region("ZS"), kb.region("G")
    NB = T // 128
    slopes = alibi_slopes()
    with kb.phase() as ph:
        qr = Ring([ph.sb(f"aq{i}", [128, T], BF16, dma=True) for i in range(2)])
        kr = Ring([ph.sb(f"ak{i}", [128, T], BF16, dma=True) for i in range(2)])
        vr = Ring([ph.sb(f"av{i}", [128, NB, 128], BF16, dma=True) for i in range(2)])
        acc = ph.sb("acc", [128, T], F32)
        dacc = ph.sb("dacc", [128, T], F32)
        ebr = Ring([ph.sb(f"eb{i}", [128, 2, 128], BF16) for i in range(3)])
        pexr = Ring([ph.sb(f"pex{i}", [128, 2, 2, 128], BF16) for i in range(3)])
        ptr_ = Ring([ph.sb(f"ptt{i}", [128, 2, 2, 128], BF16) for i in range(6)])
        spr = Ring([ph.ps(f"sp{i}", [128, 2, 2, 128]) for i in range(4)])
        opr = Ring([ph.ps(f"op{i}", [128, 4, 128]) for i in range(2)])
        dpr = Ring([ph.ps(f"dp{i}", [128, 4, 128]) for i in range(2)])
        rel = cx.cst[:, C_REL:C_REL + 256]
        tasks = []
        npair_ctr = [0]

        def head_setup(j, gi, d, hs):
            def f():
                qT, kT, v = qr.next(), kr.next(), vr.next()
                kb.dma("sp", qT[:, :], S["Q"][gi][j * 128:(j + 1) * 128, :], qT, True, regs=[rA])
                kb.dma("sp", kT[:, :], S["K"][gi][j * 128:(j + 1) * 128, :], kT, True, regs=[rA])
                kb.dma("sp", v[:, :, :], S["V"][gi][:, j * 128:(j + 1) * 128].rearrange("(b p) e -> p b e", p=128), v, True,
                       regs=[rA])
                eb = ebr.next()
                sc = -float(slopes[gi, j]) * d
                kb.op("act", lambda g: g.activation(out=eb[:, :, :].rearrange("p a q -> p (a q)"), in_=rel, func=ACT.Exp, scale=sc),
                      reads=[cx.cst], writes=[eb])
                hs.update(qT=qT, kT=kT, v=v, eb=eb)
            return f

        def batch(j, gi, d, r, kb0, nbs, QB, hs, setup):
            bs = {}

            def F():
                if setup is not None:
                    setup()
                qT, kT, eb = hs["qT"], hs["kT"], hs["eb"]
                pts = []
                for pair in range(0, QB, 2):
                    npair = min(2, QB - pair)
                    sp, pex, pt = spr.next(), pexr.next(), ptr_.next()
                    fns = []
                    for a in range(npair):
                        kbi = kb0 + pair + a
                        bb = r * nbs + kbi
                        if kbi > 0:
                            fns.append(lambda g, a=a, bb=bb: g.matmul(sp[:, a, 0, :], lhsT=kT[:, (bb - 1) * 128:bb * 128],
                                                                      rhs=qT[:, bb * 128:(bb + 1) * 128], start=True, stop=True))
                        fns.append(lambda g, a=a, bb=bb: g.matmul(sp[:, a, 1, :], lhsT=kT[:, bb * 128:(bb + 1) * 128],
                                                                  rhs=qT[:, bb * 128:(bb + 1) * 128], start=True, stop=True))
                    kb.pe(fns, reads=[kT, qT], writes=[sp])
                    npair_ctr[0] += 1
                    me = "dve" if npair_ctr[0] % 4 == 0 else "pool"
                    first = (kb0 + pair == 0)
                    if first:
                        kb.op("act", lambda g: g.activation(out=pex[:, 0, 1, :], in_=sp[:, 0, 1, :], func=ACT.Exp),
                              reads=[sp], pwrites=[pex])
                        kb.op(me, lambda g: g.tensor_tensor(out=pt[:, 0, 1, :], in0=pex[:, 0, 1, :], in1=eb[:, 1, :],
                                                            op=ALU.mult), reads=[pex, eb], pwrites=[pt])
                        if npair > 1:
                            kb.op("act", lambda g: g.activation(out=pex[:, 1, :, :], in_=sp[:, 1, :, :], func=ACT.Exp),
                                  reads=[sp], pwrites=[pex])
                            kb.op(me, lambda g: g.tensor_tensor(out=pt[:, 1, :, :], in0=pex[:, 1, :, :], in1=eb[:, :, :],
                                                                op=ALU.mult), reads=[pex, eb], pwrites=[pt])
                    else:
                        kb.op("act", lambda g: g.activation(out=pex[:, :npair, :, :], in_=sp[:, :npair, :, :], func=ACT.Exp),
                              reads=[sp], pwrites=[pex])
                        kb.op(me, lambda g: g.tensor_tensor(
                            out=pt[:, :npair, :, :], in0=pex[:, :npair, :, :],
                            in1=eb[:, :, :].unsqueeze(1).broadcast_to([128, npair, 2, 128]), op=ALU.mult),
                            reads=[pex, eb], pwrites=[pt])
                    pts.append((pt, pair, npair))
                bs["pts"] = pts

            def K():
                v = hs["v"]
                pts = bs["pts"]
                op_, dp = opr.next(), dpr.next()
                fo, fd = [], []
                for pt, pair, npair in pts:
                    for a in range(npair):
                        kbi = kb0 + pair + a
                        bb = r * nbs + kbi
                        qi = pair + a
                        lo = 0 if kbi > 0 else 1
                        for part in range(lo, 2):
                            vb = bb - 1 + part
                            fo.append(lambda g, pt=pt, a=a, part=part, vb=vb, qi=qi, lo=lo: g.matmul(
                                op_[:, qi, :], lhsT=v[:, vb, :], rhs=pt[:, a, part, :], start=(part == lo), stop=(part == 1)))
                            fd.append(lambda g, pt=pt, a=a, part=part, qi=qi, lo=lo: g.matmul(
                                dp[:, qi, :], lhsT=cx.onesb[:, :], rhs=pt[:, a, part, :], start=(part == lo), stop=(part == 1)))
                rd = [p[0] for p in pts]
                kb.pe(fo, reads=[v] + rd, writes=[op_])
                kb.pe(fd, reads=[cx.onesb] + rd, writes=[dp])
                n_q = QB * 128
                if d == 1:
                    a_out = acc[:, kb0 * 128:kb0 * 128 + n_q]
                    d_out = dacc[:, kb0 * 128:kb0 * 128 + n_q]
                else:
                    a_out = acc[:, :].rearrange("p (q s) -> p q s", s=d)[:, kb0 * 128:kb0 * 128 + n_q, r]
                    d_out = dacc[:, :].rearrange("p (q s) -> p q s", s=d)[:, kb0 * 128:kb0 * 128 + n_q, r]
                o_in = op_[:, :QB, :].rearrange("p a q -> p (a q)")
                d_in = dp[:, :QB, :].rearrange("p a q -> p (a q)")
                if gi == 0:
                    kb.op("act", lambda g: g.activation(out=a_out, in_=o_in, func=ACT.Copy), reads=[op_], pwrites=[acc])
                    kb.op("act", lambda g: g.activation(out=d_out, in_=d_in, func=ACT.Copy), reads=[dp], pwrites=[dacc])
                else:
                    kb.op("dve", lambda g: g.tensor_tensor(out=a_out, in0=o_in, in1=a_out, op=ALU.add),
                          reads=[op_, acc], pwrites=[acc])
                    kb.op("dve", lambda g: g.tensor_tensor(out=d_out, in0=d_in, in1=d_out, op=ALU.add),
                          reads=[dp, dacc], pwrites=[dacc])

            return [F, K]

        def head_final(j):
            def fin():
                zs = qr.next()
                kb.dma("sp", zs[:, :], S["ZS"][j * 128:(j + 1) * 128, :], zs, True, regs=[rZ])
                kb.op("dve", lambda g: g.reciprocal(out=dacc[:, :], in_=dacc[:, :]), reads=[dacc], writes=[dacc])
                kb.op("dve", lambda g: g.tensor_tensor(out=acc[:, :], in0=acc[:, :], in1=dacc[:, :], op=ALU.mult),
                      reads=[acc, dacc], writes=[acc])
                kb.op("pool", lambda g: g.tensor_tensor(out=zs[:, :], in0=acc[:, :], in1=zs[:, :], op=ALU.mult),
                      reads=[acc, zs], writes=[zs])
                kb.dma("sp" if os.environ.get("KB_SIM") else "pool", G[j * 128:(j + 1) * 128, :], zs[:, :], zs, False, regs=[rG])
            return [lambda: None, fin]

        for j in range(24):
            for gi, (_, d) in enumerate(DIL):
                nsub = T // d
                nbs = nsub // 128
                QB = min(4, nbs)
                hs = {}
                setup = head_setup(j, gi, d, hs)
                for r in range(d):
                    for kb0 in range(0, nbs, QB):
                        tasks.append(batch(j, gi, d, r, kb0, nbs, QB, hs, setup))
                        setup = None
            tasks.append(head_final(j))
        run_pipeline(tasks)


SSD_KEYS = ("norm_g", "w_in", "conv_w", "conv_b", "dt_bias", "a_log", "d_skip", "ssd_norm_g", "w_mem_kv", "w_out")
ATT_KEYS = ("norm_g", "w_in", "w_mem_kv", "w_out")
SSD_SHAPES = {"norm_g": [D], "w_in": [D, SSD_IN], "conv_w": [128, 40, 4], "conv_b": [128, 40], "dt_bias": [48],
              "a_log": [48], "d_skip": [48], "ssd_norm_g": [128, 24], "w_mem_kv": [D, 2048], "w_out": [MIXW, D]}
ATT_SHAPES = {"norm_g": [D], "w_in": [D, ATT_IN], "w_mem_kv": [D, 2048], "w_out": [MIXW, D]}


def build(T, layer_ids, with_final=True):
    nc = bass.Bass("TRN2", target_bir_lowering=False)

    def din(name, shape):
        return nc.dram_tensor(name, list(shape), F32, kind="ExternalInput").ap()

    def scr(name, shape, dt=BF16):
        return nc.dram_tensor(name, list(shape), dt, kind="Internal").ap()

    x_ext = din("x", [T, D])
    mem = din("mem", [N_MEM, D])
    consts = din("consts", [128, NCONST])
    mem_g = din("mem_norm_g", [D])
    fin_g = din("final_norm_g", [D]) if with_final else None
    params = {}
    for li in layer_ids:
        shapes = SSD_SHAPES if li % 2 == 0 else ATT_SHAPES
        params[li] = {k: din(f"{k}_{li}", shp) for k, shp in shapes.items()}
    y_ext = nc.dram_tensor("y", [T, D], F32, kind="ExternalOutput").ap()
    xs_ = [scr("xa", [T, D], F32), scr("xb", [T, D], F32)]
    S = {"QM": scr("QM", [MEMW, T]), "ZS": scr("ZS", [MIXW, T])}
    G = scr("G", [MIXW, T])
    if any(li % 2 == 0 for li in layer_ids):
        S.update({"TOK": scr("TOK", [T, 7168]), "BCT": scr("BCT", [2048, T]), "DTA": scr("DTA", [T, 96], F32)})
    if any(li % 2 == 1 for li in layer_ids):
        S.update({"Q": [scr(f"Q{g}", [TOKW, T]) for g in range(3)], "K": [scr(f"K{g}", [TOKW, T]) for g in range(3)],
                  "V": [scr(f"V{g}", [T, TOKW]) for g in range(3)]})

    kb = KB(nc)
    cx = Ctx()
    with kb.es:
        with kb.phase() as top:
            cx.cst = top.sb("cst", [128, NCONST], F32, dma=True)
            cx.cstb = top.sb("cstb", [128, NCONST], BF16, dma="pool")
            kb.dma("sp", cx.cst[:, :], consts, cx.cst, True)
            kb.dma("pool", cx.cstb[:, :], consts, cx.cstb, True)
            cx.identb = Buf(cx.cstb.t, "identb")
            cx.onesb = Buf(cx.cstb.t, "onesb")
            cx.identb = _View(cx.cstb, C_ID, 128)
            cx.onesb = _View(cx.cstb, C_ONES, 128)
            cx.neghalf = top.sb("neghalf", [128, 1], F32)
            kb.op("pool", lambda g: g.memset(cx.neghalf[:, :], -0.5), writes=[cx.neghalf])
            cx.memnT = top.sb("memnT", [128, KT, N_MEM], BF16)
            mem_prep(kb, cx, top, mem, mem_g)
            cur, rcur = x_ext, kb.region("xext")
            for n_, li in enumerate(layer_ids):
                P = params[li]
                nxt = xs_[n_ % 2]
                rnxt = kb.region(f"x{n_ % 2}")
                with kb.phase() as lp:
                    mkT = lp.sb("mkT", [128, 8, N_MEM], BF16)
                    mv = lp.sb("mv", [128, 2, MEMW], BF16)
                    mem_kv(kb, cx, lp, P["w_mem_kv"], mkT, mv)
                    if li % 2 == 0:
                        ssd_proj(kb, cx, T, cur, rcur, P, S)
                        ssd_scan(kb, cx, T, P, S, G)
                    else:
                        attn_proj(kb, cx, T, cur, rcur, P, S)
                        attn_core(kb, cx, T, S, G)
                    mem_attention(kb, cx, T, S["QM"], S["ZS"][TOKW:MIXW, :], G, mkT, mv)
                    if not with_final and n_ == len(layer_ids) - 1:
                        out_proj(kb, cx, T, G, P["w_out"], cur, y_ext, rcur, kb.region("Y"))
                    else:
                        out_proj(kb, cx, T, G, P["w_out"], cur, nxt, rcur, rnxt)
                cur, rcur = nxt, rnxt
            if with_final:
                final_norm(kb, cx, T, cur, rcur, fin_g, y_ext)
            kb.finish()
    return nc


class _View:
    def __init__(self, base, c0, n):
        self.base = base
        self.c0 = c0
        self.n = n

    @property
    def w(self):
        return self.base.w

    @w.setter
    def w(self, v):
        self.base.w = v

    @property
    def r(self):
        return self.base.r

    @r.setter
    def r(self, v):
        self.base.r = v

    def __getitem__(self, k):
        assert isinstance(k, tuple) and len(k) == 2
        cs = k[1]
        assert cs == slice(None)
        return self.base.t[k[0], self.c0:self.c0 + self.n]


def _layer_inputs(inputs, li):
    d = {}
    keys = SSD_KEYS if li % 2 == 0 else ATT_KEYS
    for k in keys:
        a = np.ascontiguousarray(np.asarray(inputs[f"{k}_{li}"], dtype=np.float32))
        if k == "conv_w":
            a = np.ascontiguousarray(a.T.reshape(40, 128, 4).transpose(1, 0, 2))
        elif k == "conv_b":
            a = np.ascontiguousarray(a.reshape(40, 128).T)
        elif k == "ssd_norm_g":
            a = np.ascontiguousarray(a.reshape(24, 128).T)
        d[f"{k}_{li}"] = a
    return d


_SELFWAIT = bool(int(os.environ.get('KB_SELFWAIT', '0')))
_NC_CACHE = {}
_RUN_KW = {}
_LAST = {}


def run_layers(x, mem, inputs, layer_ids, with_final, n_cores=None):
    B, T, _ = x.shape
    key = (T, tuple(layer_ids), with_final)
    if key not in _NC_CACHE:
        _NC_CACHE[key] = build(T, list(layer_ids), with_final)
    nc = _NC_CACHE[key]
    shared = {"consts": make_consts(), "mem_norm_g": np.asarray(inputs["mem_norm_g"], np.float32)}
    if with_final:
        shared["final_norm_g"] = np.asarray(inputs["final_norm_g"], np.float32)
    for li in layer_ids:
        shared.update(_layer_inputs(inputs, li))
    in_maps = []
    for b in range(B):
        m = dict(shared)
        m["x"] = np.ascontiguousarray(x[b], dtype=np.float32)
        m["mem"] = np.ascontiguousarray(mem[b], dtype=np.float32)
        in_maps.append(m)
    res = run_bass_kernel_spmd(nc, in_maps, core_ids=list(range(B)), **_RUN_KW)
    _LAST["exec_ns"] = getattr(res, "exec_time_ns", None)
    return np.stack([np.asarray(r["y"]) for r in res.results], axis=0)


ALL_INPUTS = (
    "x", "mem", "mem_norm_g", "final_norm_g",
    "norm_g_0", "w_in_0", "conv_w_0", "conv_b_0", "dt_bias_0", "a_log_0", "d_skip_0", "ssd_norm_g_0", "w_mem_kv_0", "w_out_0",
    "norm_g_1", "w_in_1", "w_mem_kv_1", "w_out_1",
    "norm_g_2", "w_in_2", "conv_w_2", "conv_b_2", "dt_bias_2", "a_log_2", "d_skip_2", "ssd_norm_g_2", "w_mem_kv_2", "w_out_2",
    "norm_g_3", "w_in_3", "w_mem_kv_3", "w_out_3",
)


def kernel(**inputs):
    inputs = {k: inputs[k] for k in ALL_INPUTS}
    x = np.asarray(inputs["x"], np.float32)
    mem = np.asarray(inputs["mem"], np.float32)
    return run_layers(x, mem, inputs, [0, 1, 2, 3], True).astype(np.float32)
```
